# Optimizing a Trainium2 kernel written in Bass

```python
import jax, jax.numpy as jnp
from jax import lax
import numpy as np

D_MODEL = 2048
BATCH = 16
SEQ = 256
DEPTH = 4
DEC_BATCH = 4
DEC_SEQ = 4096
PAST_LEN = 512

GRID_W = 64
BLK = 128
ROPE_BASE = 10000.0
EPS = 1e-6
NEG_INF = -1e30
N_BRANCH = 4
BR_W = 512

MLA_H = 4
MLA_Q_LORA = 512
MLA_KV_LORA = 256
MLA_NOPE = 128
MLA_ROPE = 64
MLA_V = 128

RET_H = 4
RET_DK = 128
RET_DV = 128

NAT_H = 4
NAT_D = 128
NAT_ROWS = 8
NAT_COLS = 16

SWA_H = 8
SWA_KVH = 2
SWA_D = 64
SWA_WINDOW = 128

MIX_SPLITS = (MLA_Q_LORA, MLA_KV_LORA, MLA_ROPE,
              RET_H * RET_DK, RET_H * RET_DK, RET_H * RET_DV,
              NAT_H * NAT_D, NAT_H * NAT_D, NAT_H * NAT_D,
              SWA_H * SWA_D, SWA_KVH * SWA_D, SWA_KVH * SWA_D,
              N_BRANCH * BR_W)
MIX_COLS = sum(MIX_SPLITS)
IN_COLS = MIX_COLS + N_BRANCH * D_MODEL

kernel_name = "hybrid_dit_mla_ret_nat_swa_step"


def rmsnorm(x, g):
    xf = x.astype(jnp.float32)
    y = xf * lax.rsqrt(jnp.mean(xf * xf, axis=-1, keepdims=True) + EPS)
    return (y * g.astype(jnp.float32)).astype(x.dtype)


def softmax_sink(s, sink):
    m = jnp.max(s, axis=-1, keepdims=True)
    if sink is None:
        e = jnp.exp(s - m)
        return e / jnp.sum(e, axis=-1, keepdims=True)
    m = jnp.maximum(m, sink)
    e = jnp.exp(s - m)
    return e / (jnp.sum(e, axis=-1, keepdims=True) + jnp.exp(sink - m))


def axial_rope_tables(n_tok, rot_dim):
    pos = jnp.arange(n_tok)
    row = (pos // GRID_W).astype(jnp.float32)
    col = (pos % GRID_W).astype(jnp.float32)
    n_freq = rot_dim // 4
    inv = ROPE_BASE ** (-jnp.arange(n_freq, dtype=jnp.float32) / n_freq)
    ang = jnp.concatenate([row[:, None] * inv[None], col[:, None] * inv[None]], axis=-1)
    return jnp.cos(ang), jnp.sin(ang)


def apply_rope(x, cos, sin):
    half = x.shape[-1] // 2
    x1, x2 = x[..., :half], x[..., half:]
    c = cos[:, None, :].astype(x.dtype)
    s = sin[:, None, :].astype(x.dtype)
    return jnp.concatenate([x1 * c - x2 * s, x1 * s + x2 * c], axis=-1)


def dense_attention(q, k, v, sink=None):
    B, Sq, H, d = q.shape
    kvh = k.shape[2]
    g = H // kvh
    dv = v.shape[-1]
    scale = d ** -0.5
    nb = Sq // BLK
    qb = jnp.moveaxis(q.reshape(B, nb, BLK, kvh, g, d), 1, 0)
    sk = None if sink is None else sink.astype(jnp.float32).reshape(kvh, g, 1, 1)

    def one_block(qi):
        s = jnp.einsum('bqkgd,bskd->bkgqs', qi, k).astype(jnp.float32) * scale
        p = softmax_sink(s, sk).astype(v.dtype)
        return jnp.einsum('bkgqs,bskd->bqkgd', p, v)

    o = lax.map(one_block, qb)
    return jnp.moveaxis(o, 0, 1).reshape(B, Sq, H * dv)


def retention_scan(q, k, v, log_g, s0, strict):
    f32 = jnp.float32
    B, S, H, dk = q.shape
    dv = v.shape[-1]
    nc = S // BLK
    lg = log_g.astype(f32)

    def chunks(t):
        return jnp.transpose(t.astype(f32).reshape(B, nc, BLK, H, t.shape[-1]), (1, 0, 3, 2, 4))

    idx = jnp.arange(BLK, dtype=f32)
    diff = idx[:, None] - idx[None, :]
    mask = (diff > 0) if strict else (diff >= 0)
    dmat = jnp.where(mask[None], jnp.exp(lg[:, None, None] * jnp.maximum(diff, 0.0)[None]), 0.0)
    q_dec = jnp.exp(lg[:, None] * (idx[None] + 1.0))[..., None]
    k_dec = jnp.exp(lg[:, None] * (BLK - 1.0 - idx[None]))[..., None]
    c_dec = jnp.exp(lg * BLK)[:, None, None]

    def step(state, inp):
        qi, ki, vi = inp
        inner = jnp.einsum('bhqd,bhkd->bhqk', qi, ki) * dmat
        o = jnp.einsum('bhqk,bhkv->bhqv', inner, vi) + jnp.einsum('bhqd,bhdv->bhqv', qi * q_dec, state)
        state = state * c_dec + jnp.einsum('bhkd,bhkv->bhdv', ki * k_dec, vi)
        return state, o

    s_fin, o = lax.scan(step, s0.astype(f32), (chunks(q), chunks(k), chunks(v)))
    o = jnp.transpose(o, (1, 0, 3, 2, 4)).reshape(B, S, H, dv)
    return o, s_fin


def retention_bidir(q, k, v, log_g2, s0_fwd, s0_bwd):
    o_f, s_f = retention_scan(q, k, v, log_g2[0], s0_fwd, strict=False)
    o_b, s_b = retention_scan(q[:, ::-1], k[:, ::-1], v[:, ::-1], log_g2[1], s0_bwd, strict=True)
    return o_f + o_b[:, ::-1], s_f, s_b


def ret_mixer(rq, rk, rv, p, s0_fwd, s0_bwd):
    B, S, _ = rq.shape
    q = rq.reshape(B, S, RET_H, RET_DK)
    k = rk.reshape(B, S, RET_H, RET_DK) * (RET_DK ** -0.5)
    v = rv.reshape(B, S, RET_H, RET_DV)
    log_g2 = jax.nn.log_sigmoid(p['ret_decay'].astype(jnp.float32))
    o, s_f, s_b = retention_bidir(q, k, v, log_g2, s0_fwd, s0_bwd)
    o = rmsnorm(o, p['ret_norm'].reshape(RET_H, RET_DV))
    return o.reshape(B, S, BR_W).astype(rq.dtype), s_f, s_b


def mla_queries(q_a, p):
    B, S, _ = q_a.shape
    cq = rmsnorm(q_a, p['mla_q_norm'])
    return (cq @ p['mla_w_q_up']).reshape(B, S, MLA_H, MLA_NOPE + MLA_ROPE)


def mla_expand(ckv, k_rope, w_kv_up):
    B, S, _ = ckv.shape
    kv = (ckv @ w_kv_up).reshape(B, S, MLA_H, MLA_NOPE + MLA_V)
    kr = jnp.broadcast_to(k_rope[:, :, None, :], (B, S, MLA_H, MLA_ROPE))
    return jnp.concatenate([kv[..., :MLA_NOPE], kr], axis=-1), kv[..., MLA_NOPE:]


def nat_latent(q, k, v, k_ctx, v_ctx, rpb):
    f32 = jnp.float32
    B, S, H, d = q.shape
    rows = S // GRID_W
    kr = min(NAT_ROWS, rows)
    scale = d ** -0.5
    qg = jnp.moveaxis(q.reshape(B, rows, GRID_W, H, d), 1, 0)
    kg = k.reshape(B, rows, GRID_W, H, d)
    vg = v.reshape(B, rows, GRID_W, H, d)
    qcol = jnp.arange(GRID_W)
    cstart = jnp.clip(qcol - NAT_COLS // 2, 0, GRID_W - NAT_COLS)
    cidx = cstart[:, None] + jnp.arange(NAT_COLS)[None, :]
    col_off = cidx - qcol[:, None] + (NAT_COLS - 1)
    n_loc = kr * NAT_COLS

    def one_row(args):
        r, qr = args
        rs = jnp.clip(r - kr // 2, 0, rows - kr)
        k_win = lax.dynamic_slice_in_dim(kg, rs, kr, axis=1)[:, :, cidx]
        v_win = lax.dynamic_slice_in_dim(vg, rs, kr, axis=1)[:, :, cidx]
        row_off = rs + jnp.arange(kr) - r + (NAT_ROWS - 1)
        bias = rpb[:, row_off[None, :, None], col_off[:, None, :]]
        s_loc = jnp.einsum('bqhd,bmqnhd->bhqmn', qr, k_win).astype(f32) * scale + bias.astype(f32)[None]
        s_loc = s_loc.reshape(B, H, GRID_W, n_loc)
        s_ctx = jnp.einsum('bqhd,blhd->bhql', qr, k_ctx).astype(f32) * scale
        pr = softmax_sink(jnp.concatenate([s_loc, s_ctx], axis=-1), None).astype(v.dtype)
        p_loc = pr[..., :n_loc].reshape(B, H, GRID_W, kr, NAT_COLS)
        return (jnp.einsum('bhqmn,bmqnhd->bqhd', p_loc, v_win)
                + jnp.einsum('bhql,blhd->bqhd', pr[..., n_loc:], v_ctx))

    o = lax.map(one_row, (jnp.arange(rows), qg))
    return jnp.moveaxis(o, 0, 1).reshape(B, S, H * d)


def swa_latent(q, k, v, k_ctx, v_ctx, sink):
    f32 = jnp.float32
    B, S, H, d = q.shape
    kvh = k.shape[2]
    g = H // kvh
    nb = S // BLK
    scale = d ** -0.5
    pad = jnp.zeros((B, BLK, kvh, d), k.dtype)
    kp = jnp.concatenate([pad, k, pad], axis=1)
    vp = jnp.concatenate([pad.astype(v.dtype), v, pad.astype(v.dtype)], axis=1)
    qb = jnp.moveaxis(q.reshape(B, nb, BLK, kvh, g, d), 1, 0)
    sk = sink.astype(f32).reshape(kvh, g, 1, 1)
    rel = jnp.arange(BLK)[:, None] - (jnp.arange(3 * BLK)[None, :] - BLK)
    band = jnp.abs(rel) <= SWA_WINDOW

    def one_block(args):
        i, qi = args
        kw = lax.dynamic_slice_in_dim(kp, i * BLK, 3 * BLK, axis=1)
        vw = lax.dynamic_slice_in_dim(vp, i * BLK, 3 * BLK, axis=1)
        kpos = (i - 1) * BLK + jnp.arange(3 * BLK)
        valid = band & ((kpos >= 0) & (kpos < S))[None, :]
        s_loc = jnp.einsum('bqkgd,bskd->bkgqs', qi, kw).astype(f32) * scale
        s_loc = jnp.where(valid, s_loc, NEG_INF)
        s_ctx = jnp.einsum('bqkgd,bskd->bkgqs', qi, k_ctx).astype(f32) * scale
        pr = softmax_sink(jnp.concatenate([s_loc, s_ctx], axis=-1), sk).astype(v.dtype)
        return (jnp.einsum('bkgqs,bskd->bqkgd', pr[..., :3 * BLK], vw)
                + jnp.einsum('bkgqs,bskd->bqkgd', pr[..., 3 * BLK:], v_ctx))

    o = lax.map(one_block, (jnp.arange(nb), qb))
    return jnp.moveaxis(o, 0, 1).reshape(B, S, H * d)


def modulation(cond, p):
    m = jax.nn.silu(cond) @ p['w_mod'] + p['b_mod']
    return jnp.split(m, 3, axis=-1)


def in_projection(x, shift, scale, p):
    h = rmsnorm(x, p['norm_pre']) * (1 + scale[:, None, :]) + shift[:, None, :]
    idx = [int(i) for i in np.cumsum(MIX_SPLITS)[:-1]]
    parts = jnp.split(h @ p['w_in'][:, :MIX_COLS], idx, axis=-1)
    return h, parts


def merge_branches(x, h, outs, gpath, gate, p):
    w_gates = p['w_in'][:, MIX_COLS:]
    y = None
    for n in range(N_BRANCH):
        o_n = outs[n] * jax.nn.silu(gpath[..., n * BR_W:(n + 1) * BR_W])
        g_n = jax.nn.sigmoid(h @ w_gates[:, n * D_MODEL:(n + 1) * D_MODEL])
        t = g_n * (o_n @ p['w_branch'][n])
        y = t if y is None else y + t
    y = y @ p['w_out']
    return x + gate[:, None, :] * rmsnorm(y, p['norm_post'])


def layer_context(x, cond, p):
    B, L, _ = x.shape
    shift, scale, gate = modulation(cond, p)
    h, (q_a, kv_a, k_rope, rq, rk, rv, nq, nk, nv, sq, sk, sv, gpath) = in_projection(x, shift, scale, p)
    q = mla_queries(q_a, p)
    ckv = rmsnorm(kv_a, p['mla_kv_norm'])
    k_m, v_m = mla_expand(ckv, k_rope, p['mla_w_kv_up'])
    o_mla = dense_attention(q, k_m, v_m)
    s0 = jnp.zeros((B, RET_H, RET_DK, RET_DV), jnp.float32)
    o_ret, s_f, s_b = ret_mixer(rq, rk, rv, p, s0, s0)
    nk_h = nk.reshape(B, L, NAT_H, NAT_D)
    nv_h = nv.reshape(B, L, NAT_H, NAT_D)
    o_nat = dense_attention(nq.reshape(B, L, NAT_H, NAT_D), nk_h, nv_h)
    sk_h = sk.reshape(B, L, SWA_KVH, SWA_D)
    sv_h = sv.reshape(B, L, SWA_KVH, SWA_D)
    o_swa = dense_attention(sq.reshape(B, L, SWA_H, SWA_D), sk_h, sv_h, sink=p['swa_sink'])
    x = merge_branches(x, h, (o_mla, o_ret, o_nat, o_swa), gpath, gate, p)
    st = jnp.stack([s_f, s_b], axis=1).astype(x.dtype)
    return x, (ckv, k_rope, st, nk_h, nv_h, sk_h, sv_h)


def layer_latent(x, cond, ckv_c, kr_c, st_c, nk_c, nv_c, sk_c, sv_c, p):
    B, S, _ = x.shape
    shift, scale, gate = modulation(cond, p)
    h, (q_a, kv_a, k_rope, rq, rk, rv, nq, nk, nv, sq, sk, sv, gpath) = in_projection(x, shift, scale, p)
    cos_m, sin_m = axial_rope_tables(S, MLA_ROPE)
    cos_s, sin_s = axial_rope_tables(S, SWA_D)
    q = mla_queries(q_a, p)
    q = jnp.concatenate([q[..., :MLA_NOPE], apply_rope(q[..., MLA_NOPE:], cos_m, sin_m)], axis=-1)
    kr = apply_rope(k_rope[:, :, None, :], cos_m, sin_m)[:, :, 0]
    ckv = rmsnorm(kv_a, p['mla_kv_norm'])
    k_lat, v_lat = mla_expand(ckv, kr, p['mla_w_kv_up'])
    k_ctx, v_ctx = mla_expand(ckv_c, kr_c, p['mla_w_kv_up'])
    o_mla = dense_attention(q, jnp.concatenate([k_lat, k_ctx], axis=1), jnp.concatenate([v_lat, v_ctx], axis=1))
    o_ret, _, _ = ret_mixer(rq, rk, rv, p, st_c[:, 0], st_c[:, 1])
    o_nat = nat_latent(nq.reshape(B, S, NAT_H, NAT_D), nk.reshape(B, S, NAT_H, NAT_D),
                       nv.reshape(B, S, NAT_H, NAT_D), nk_c, nv_c, p['nat_rpb'])
    qs = apply_rope(sq.reshape(B, S, SWA_H, SWA_D), cos_s, sin_s)
    ks = apply_rope(sk.reshape(B, S, SWA_KVH, SWA_D), cos_s, sin_s)
    o_swa = swa_latent(qs, ks, sv.reshape(B, S, SWA_KVH, SWA_D), sk_c, sv_c, p['swa_sink'])
    return merge_branches(x, h, (o_mla, o_ret, o_nat, o_swa), gpath, gate, p)


def layer_params(l, w_mod, b_mod, norm_pre, norm_post, w_in, mla_q_norm, mla_kv_norm, mla_w_q_up,
                 mla_w_kv_up, ret_decay, ret_norm, nat_rpb, swa_sink, w_branch, w_out):
    return dict(w_mod=w_mod[l], b_mod=b_mod[l], norm_pre=norm_pre[l], norm_post=norm_post[l],
                w_in=w_in[l], mla_q_norm=mla_q_norm[l], mla_kv_norm=mla_kv_norm[l],
                mla_w_q_up=mla_w_q_up[l], mla_w_kv_up=mla_w_kv_up[l], ret_decay=ret_decay[l],
                ret_norm=ret_norm[l], nat_rpb=nat_rpb[l], swa_sink=swa_sink[l],
                w_branch=w_branch[l], w_out=w_out[l])


def setup_inputs(seed: int = 0) -> dict:
    key = jax.random.key(seed)
    ks = jax.random.split(key, 28)
    f32 = jnp.float32

    def nrm(k, shape, s):
        return jax.random.normal(k, shape, f32) * s

    ret_init = jnp.log(2.0 ** (5.0 + jnp.arange(RET_H, dtype=f32)) - 1.0)
    return {
        'x_prompt': nrm(ks[0], (BATCH, SEQ, D_MODEL), 1.0),
        'x_sample': nrm(ks[1], (DEC_BATCH, DEC_SEQ, D_MODEL), 1.0),
        'cache_mla_ckv': nrm(ks[2], (DEC_BATCH, DEPTH, PAST_LEN, MLA_KV_LORA), 1.0),
        'cache_mla_krope': nrm(ks[3], (DEC_BATCH, DEPTH, PAST_LEN, MLA_ROPE), 1.0),
        'state_ret': nrm(ks[4], (DEC_BATCH, DEPTH, 2, RET_H, RET_DK, RET_DV), 0.3),
        'cache_nat_k': nrm(ks[5], (DEC_BATCH, DEPTH, PAST_LEN, NAT_H, NAT_D), 1.0),
        'cache_nat_v': nrm(ks[6], (DEC_BATCH, DEPTH, PAST_LEN, NAT_H, NAT_D), 1.0),
        'cache_swa_k': nrm(ks[7], (DEC_BATCH, DEPTH, PAST_LEN, SWA_KVH, SWA_D), 1.0),
        'cache_swa_v': nrm(ks[8], (DEC_BATCH, DEPTH, PAST_LEN, SWA_KVH, SWA_D), 1.0),
        'c': nrm(ks[9], (DEC_BATCH, D_MODEL), 1.0),
        'c_ctx': nrm(ks[10], (D_MODEL,), 1.0),
        'w_mod': nrm(ks[11], (DEPTH, D_MODEL, 3 * D_MODEL), 0.5 * D_MODEL ** -0.5),
        'b_mod': nrm(ks[12], (DEPTH, 3 * D_MODEL), 0.02),
        'norm_pre': 1.0 + nrm(ks[13], (DEPTH, D_MODEL), 0.05),
        'norm_post': 1.0 + nrm(ks[14], (DEPTH, D_MODEL), 0.05),
        'w_in': nrm(ks[15], (DEPTH, D_MODEL, IN_COLS), D_MODEL ** -0.5),
        'mla_q_norm': 1.0 + nrm(ks[16], (DEPTH, MLA_Q_LORA), 0.05),
        'mla_kv_norm': 1.0 + nrm(ks[17], (DEPTH, MLA_KV_LORA), 0.05),
        'mla_w_q_up': nrm(ks[18], (DEPTH, MLA_Q_LORA, MLA_H * (MLA_NOPE + MLA_ROPE)), MLA_Q_LORA ** -0.5),
        'mla_w_kv_up': nrm(ks[19], (DEPTH, MLA_KV_LORA, MLA_H * (MLA_NOPE + MLA_V)), MLA_KV_LORA ** -0.5),
        'ret_decay': ret_init[None, None, :] + nrm(ks[20], (DEPTH, 2, RET_H), 0.1),
        'ret_norm': 1.0 + nrm(ks[21], (DEPTH, RET_H * RET_DV), 0.05),
        'nat_rpb': nrm(ks[22], (DEPTH, NAT_H, 2 * NAT_ROWS - 1, 2 * NAT_COLS - 1), 0.5),
        'swa_sink': nrm(ks[23], (DEPTH, SWA_H), 0.5),
        'w_branch': nrm(ks[24], (DEPTH, N_BRANCH, BR_W, D_MODEL), BR_W ** -0.5),
        'w_out': nrm(ks[25], (DEPTH, D_MODEL, D_MODEL), D_MODEL ** -0.5),
    }


def reference(x_prompt, x_sample, cache_mla_ckv, cache_mla_krope, state_ret, cache_nat_k, cache_nat_v,
              cache_swa_k, cache_swa_v, c, c_ctx, w_mod, b_mod, norm_pre, norm_post, w_in, mla_q_norm,
              mla_kv_norm, mla_w_q_up, mla_w_kv_up, ret_decay, ret_norm, nat_rpb, swa_sink, w_branch, w_out):
    yp = x_prompt
    cond_ctx = c_ctx[None, :]
    ckv_l, kr_l, st_l, nk_l, nv_l, sk_l, sv_l = [], [], [], [], [], [], []
    for l in range(DEPTH):
        p = layer_params(l, w_mod, b_mod, norm_pre, norm_post, w_in, mla_q_norm, mla_kv_norm, mla_w_q_up,
                         mla_w_kv_up, ret_decay, ret_norm, nat_rpb, swa_sink, w_branch, w_out)
        yp, (ckv, kr, st, nk, nv, sk, sv) = layer_context(yp, cond_ctx, p)
        ckv_l.append(ckv); kr_l.append(kr); st_l.append(st)
        nk_l.append(nk); nv_l.append(nv); sk_l.append(sk); sv_l.append(sv)
    ys = x_sample
    for l in range(DEPTH):
        p = layer_params(l, w_mod, b_mod, norm_pre, norm_post, w_in, mla_q_norm, mla_kv_norm, mla_w_q_up,
                         mla_w_kv_up, ret_decay, ret_norm, nat_rpb, swa_sink, w_branch, w_out)
        ys = layer_latent(ys, c, cache_mla_ckv[:, l], cache_mla_krope[:, l], state_ret[:, l],
                          cache_nat_k[:, l], cache_nat_v[:, l], cache_swa_k[:, l], cache_swa_v[:, l], p)
    new_mla_ckv = jnp.stack(ckv_l, axis=1)
    new_mla_krope = jnp.stack(kr_l, axis=1)
    new_state_ret = jnp.stack(st_l, axis=1)
    new_nat_k = jnp.stack(nk_l, axis=1)
    new_nat_v = jnp.stack(nv_l, axis=1)
    new_swa_k = jnp.stack(sk_l, axis=1)
    new_swa_v = jnp.stack(sv_l, axis=1)
    return (yp, ys, new_mla_ckv, new_mla_krope, new_state_ret, new_nat_k, new_nat_v, new_swa_k, new_swa_v)
```

```python
import numpy as np
from contextlib import ExitStack
import concourse.bass as bass
import concourse.mybir as mybir
from concourse.bass_utils import run_bass_kernel_spmd

F32 = mybir.dt.float32
BF16 = mybir.dt.bfloat16
AF = mybir.ActivationFunctionType
ALU = mybir.AluOpType
ENGS = ("pe", "act", "dve", "pool", "sp")
D = 2048
L = 256
PAST = 512
EPS = 1e-6
NEG = -30000.0
SKIP = set()
MAXPHASE = 10 ** 9


class Prog:
    def __init__(self, nc):
        self.nc = nc
        self.base = ExitStack()
        self.stack = None
        self.ops = {e: [] for e in ENGS}
        self.cnt = {e: 0 for e in ENGS}
        self.sem = {e: self.base.enter_context(nc.semaphore("s_" + e)) for e in ENGS}
        self.dsem = {}
        self.physp = {}
        self.nused = {}
        self.kmap = {}
        self.last_w = {}
        self.reads = {}
        self.known = {e: {} for e in ENGS}
        self.phase_keys = set()
        self.n_ops = 0

    def sbuf(self, name, shape, dtype, persist=False):
        st = self.base if persist else self.stack
        self.uid = getattr(self, "uid", 0) + 1
        return st.enter_context(self.nc.sbuf_tensor("%s_u%d" % (name, self.uid), list(shape), dtype))

    def psum(self, name, shape, dtype=F32):
        return self.base.enter_context(self.nc.psum_tensor(name, list(shape), dtype))

    def dma_sem(self, key, eng):
        if key not in self.kmap:
            used = self.nused.setdefault(eng, 0)
            self.nused[eng] += 1
            pool = self.physp.setdefault(eng, [])
            if used >= len(pool):
                idx = len(self.dsem)
                s = self.base.enter_context(self.nc.semaphore("d_%d" % idx))
                self.dsem[idx] = [s, 0]
                pool.append(idx)
            self.kmap[key] = pool[used]
        return self.kmap[key]

    def _deps(self, eng, reads, writes):
        deps = []
        for r in reads:
            t = self.last_w.get(r)
            if t is not None:
                deps.append(t)
        for w in writes:
            t = self.last_w.get(w)
            if t is not None:
                deps.append(t)
            deps.extend(self.reads.get(w, {}).items())
        waits = {}
        kn = self.known[eng]
        for (src, val) in deps:
            if src == eng and eng in ("pe", "sp"):
                continue
            if kn.get(src, 0) >= val:
                continue
            if waits.get(src, 0) < val:
                waits[src] = val
        for src, val in waits.items():
            kn[src] = val
        return list(waits.items())

    def _commit(self, tok, reads, writes):
        for r in reads:
            d = self.reads.setdefault(r, {})
            if d.get(tok[0], 0) < tok[1]:
                d[tok[0]] = tok[1]
        for w in writes:
            self.last_w[w] = tok
            self.reads[w] = {}

    def op(self, eng, fn, reads=(), writes=()):
        if self.dead:
            return None
        waits = self._deps(eng, reads, writes)
        self.cnt[eng] += 1
        tok = (eng, self.cnt[eng])
        self.ops[eng].append((waits, fn, ("eng", eng)))
        self._commit(tok, reads, writes)
        self.n_ops += 1
        return tok

    def dma(self, eng, key, fn, reads=(), writes=()):
        if self.dead:
            return None
        key = self.dma_sem(key, eng)
        ds = self.dsem[key]
        self.phase_keys.add(key)
        waits = self._deps(eng, reads, writes)
        src = ("dma", key)
        kn = self.known[eng]
        if ds[1] > 0 and kn.get(src, 0) < ds[1]:
            waits = [w for w in waits if w[0] != src] + [(src, ds[1])]
            kn[src] = ds[1]
        ds[1] += 16
        tok = (src, ds[1])
        self.ops[eng].append((waits, fn, ("dma", key)))
        self._commit(tok, reads, writes)
        self.n_ops += 1
        return tok

    def _semof(self, src):
        if isinstance(src, tuple):
            return self.dsem[src[1]][0]
        return self.sem[src]

    def begin(self):
        self.stack = ExitStack()
        self.nphase = getattr(self, "nphase", 0) + 1
        self.dead = self.nphase > MAXPHASE

    def end(self):
        waits = [(("dma", k), self.dsem[k][1]) for k in sorted(self.phase_keys)]
        self.cnt["sp"] += 1
        self.ops["sp"].append((waits, lambda e: e.nop(), ("eng", "sp")))
        final = dict(self.cnt)
        for e in ENGS:
            w = [(f, final[f]) for f in ENGS if f != e and final[f] > self.known[e].get(f, 0)]
            self.ops[e].append((w, None, None))
        nc = self.nc
        with nc.Block() as block:
            def mk(e):
                def body(eng):
                    for (waits, fn, kind) in self.ops[e]:
                        for (src, val) in waits:
                            eng.wait_ge(self._semof(src), val)
                        if fn is None:
                            continue
                        ins = fn(eng)
                        if kind[0] == "eng":
                            ins.then_inc(self.sem[e], 1)
                        else:
                            ins.then_inc(self.dsem[kind[1]][0], 16)
                return body
            block.tensor(mk("pe"))
            block.scalar(mk("act"))
            block.vector(mk("dve"))
            block.gpsimd(mk("pool"))
            block.sync(mk("sp"))
        self.ops = {e: [] for e in ENGS}
        self.last_w = {}
        self.reads = {}
        for e in ENGS:
            for f in ENGS:
                self.known[e][f] = final[f]
            for k in self.dsem:
                self.known[e][("dma", k)] = self.dsem[k][1]
        self.phase_keys = set()
        self.kmap = {}
        self.nused = {}
        self.stack.close()
        self.stack = None

    def close(self):
        self.base.close()


class Ring:
    def __init__(self, P, name, n, shape, dtype):
        self.bufs = [P.sbuf("%s%d" % (name, i), shape, dtype) for i in range(n)]
        self.keys = ["%s%d" % (name, i) for i in range(n)]
        self.n = n
        self.i = 0

    def next(self):
        j = self.i % self.n
        self.i += 1
        return self.bufs[j], self.keys[j]


class PsRing:
    def __init__(self, banks, idxs):
        self.banks = [banks[i] for i in idxs]
        self.keys = ["ps%d" % i for i in idxs]
        self.n = len(idxs)
        self.i = 0

    def next(self):
        j = self.i % self.n
        self.i += 1
        return self.banks[j], self.keys[j]


OFF = dict(qa=0, kva=512, kr=768, rq=832, rk=1344, rv=1856, nq=2368, nk=2880, nv=3392, sq=3904, sk=4416, sv=4544,
           gp=4672, gate=6720)


def _swap64(base):
    return list(range(base + 32, base + 64)) + list(range(base, base + 32))


def fm_tiles():
    t = []
    for i in range(4):
        t.append(("qa%d" % i, "copy", list(range(OFF["qa"] + 128 * i, OFF["qa"] + 128 * (i + 1)))))
    for i in range(2):
        t.append(("kva%d" % i, "copy", list(range(OFF["kva"] + 128 * i, OFF["kva"] + 128 * (i + 1)))))
    kr = list(range(OFF["kr"], OFF["kr"] + 64))
    t.append(("kr", "rope", kr + kr))
    t.append(("kr_s", "swap", _swap64(OFF["kr"]) * 2))
    for h in range(4):
        t.append(("rq%d" % h, "copy", list(range(OFF["rq"] + 128 * h, OFF["rq"] + 128 * (h + 1)))))
    for h in range(4):
        t.append(("rk%d" % h, "copys", list(range(OFF["rk"] + 128 * h, OFF["rk"] + 128 * (h + 1)))))
    for h in range(4):
        t.append(("nq%d" % h, "copy", list(range(OFF["nq"] + 128 * h, OFF["nq"] + 128 * (h + 1)))))
    for h in range(4):
        t.append(("nk%d" % h, "copy", list(range(OFF["nk"] + 128 * h, OFF["nk"] + 128 * (h + 1)))))
    for i in range(4):
        b0, b1 = OFF["sq"] + 128 * i, OFF["sq"] + 128 * i + 64
        t.append(("sq%d" % i, "rope", list(range(b0, b0 + 128))))
        t.append(("sq%d_s" % i, "swap", _swap64(b0) + _swap64(b1)))
    for g in range(2):
        b0 = OFF["sk"] + 64 * g
        t.append(("sk%d" % g, "rope", list(range(b0, b0 + 64)) * 2))
        t.append(("sk%d_s" % g, "swap", _swap64(b0) * 2))
    for i in range(16):
        t.append(("gp%d" % i, "silu", list(range(OFF["gp"] + 128 * i, OFF["gp"] + 128 * (i + 1)))))
    for n in range(4):
        for j in range(16):
            b0 = OFF["gate"] + n * D + 128 * j
            t.append(("g%d_%d" % (n, j), "sig", list(range(b0, b0 + 128))))
    return t


FM = fm_tiles()
FM_SLOT = {}
for _n, _k, _c in FM:
    if _k != "swap":
        FM_SLOT[_n] = len(FM_SLOT)
NSLOT = len(FM_SLOT)

TMG = [("rk", list(range(OFF["rk"], OFF["rk"] + 512)), "all"),
       ("rv", list(range(OFF["rv"], OFF["rv"] + 512)), "all"),
       ("nv", list(range(OFF["nv"], OFF["nv"] + 512)), "all"),
       ("sv", list(range(OFF["sv"], OFF["sv"] + 128)), "all"),
       ("o1", list(range(OFF["kva"], OFF["kva"] + 256)) + list(range(OFF["kr"], OFF["kr"] + 64))
        + list(range(OFF["sk"], OFF["sk"] + 128)), "ctx"),
       ("nk", list(range(OFF["nk"], OFF["nk"] + 512)), "ctx")]
TM_OFF = {}
_o = 0
for _n, _c, _w in TMG:
    TM_OFF[_n] = (_o, len(_c))
    _o += len(_c)
TM_COLS = _o
ZTM = dict(rk=0, rv=512, nv=1024, sv=1536)
ZTM_COLS = 1664


def build(S, DEPTH):
    assert S % 512 == 0 and S >= 1024
    T = S + 2 * L
    NLB = S // 512
    NBLK = NLB + 1
    ROWS = S // 64
    NQB = S // 512
    nc = bass.Bass("TRN2", target_bir_lowering=False)

    def din(name, shape, dt=F32):
        return nc.dram_tensor(name, list(shape), dt, kind="ExternalInput").ap()

    def dout(name, shape):
        return nc.dram_tensor(name, list(shape), F32, kind="ExternalOutput").ap()

    def dscr(name, shape, dt):
        return nc.dram_tensor(name, list(shape), dt, kind="Internal").ap()

    x_lat = din("x_lat", [S, D])
    x_ctx = din("x_ctx", [2 * L, D])
    c_ckv = din("c_ckv", [DEPTH, PAST, 256])
    c_kr = din("c_kr", [DEPTH, PAST, 64])
    c_st = din("c_st", [DEPTH, 2, 4, 128, 128])
    c_nk = din("c_nk", [DEPTH, PAST, 512])
    c_nv = din("c_nv", [DEPTH, PAST, 512])
    c_sk = din("c_sk", [DEPTH, PAST, 128])
    c_sv = din("c_sv", [DEPTH, PAST, 128])
    condT = din("condT", [128, 16, 2])
    w_mod = din("w_mod", [DEPTH, 24, 128, 16, 256])
    bmod = din("bmod", [128, DEPTH, 48])
    npre = din("npre", [128, DEPTH, 16])
    npost = din("npost", [128, DEPTH, 16])
    w_fm = din("w_fm", [DEPTH, len(FM), 128, 16, 128])
    w_tm = din("w_tm", [DEPTH, 128, 16, TM_COLS])
    qn = din("qn", [128, DEPTH, 4])
    kvn = din("kvn", [128, DEPTH, 2])
    retn = din("retn", [128, DEPTH, 4])
    kvn_row = din("kvn_row", [128, DEPTH, 256])
    decay = din("decay", [128, DEPTH, 8])
    sink = din("sink", [128, DEPTH, 8])
    w_qup = din("w_qup", [DEPTH, 128, 4, 1024])
    w_kvup = din("w_kvup", [DEPTH, 128, 2, 1024])
    w_br = din("w_br", [DEPTH, 16, 128, 4, 4, 128])
    w_o = din("w_o", [DEPTH, 16, 128, 16, 128])
    nat_bias = din("nat_bias", [DEPTH, 3, 8, 4, 128, 512])
    cosT = din("cosT", [128, T])
    sinT = din("sinT", [128, T])
    ident_d = din("ident", [128, 128])
    ret_tab = din("ret_tab", [4, 128, 128])
    ret_col = din("ret_col", [128, 2])
    swa_mask = din("swa_mask", [6, 128, 512])

    y_lat = dout("y_lat", [S, D])
    y_ctx = dout("y_ctx", [2 * L, D])
    o_ckv = dout("o_ckv", [2, DEPTH, L, 256])
    o_kr = dout("o_kr", [2, DEPTH, L, 64])
    o_st = dout("o_st", [2, DEPTH, 2, 4, 128, 128])
    o_nk = dout("o_nk", [2, DEPTH, L, 512])
    o_nv = dout("o_nv", [2, DEPTH, L, 512])
    o_sk = dout("o_sk", [2, DEPTH, L, 128])
    o_sv = dout("o_sv", [2, DEPTH, L, 128])

    xT_d = dscr("xT_d", [16, 128, T], F32)
    hT_d = dscr("hT_d", [16, 128, T], BF16)
    zfm_d = dscr("zfm_d", [NSLOT, 128, T], BF16)
    ztm_d = dscr("ztm_d", [T, ZTM_COLS], BF16)
    oT_d = dscr("oT_d", [16, 128, T], BF16)

    P = Prog(nc)
    out_toks = []
    psb = [P.psum("psb%d" % i, [128, 512]) for i in range(8)]

    ident = P.sbuf("ident", [128, 128], F32, True)
    ones_bf = P.sbuf("ones_bf", [128, 128], BF16, True)
    ones_f = P.sbuf("ones_f", [128, 128], F32, True)
    modA = P.sbuf("modA", [128, DEPTH, 2, 16], F32, True)
    modB = P.sbuf("modB", [128, DEPTH, 2, 16], F32, True)
    modG = P.sbuf("modG", [128, DEPTH, 2, 16], F32, True)
    qn_s = P.sbuf("qn_s", [128, DEPTH, 4], F32, True)
    kvn_s = P.sbuf("kvn_s", [128, DEPTH, 2], F32, True)
    retn_s = P.sbuf("retn_s", [128, DEPTH, 4], F32, True)
    lg_s = P.sbuf("lg_s", [128, DEPTH, 8], F32, True)
    esink_s = P.sbuf("esink_s", [128, DEPTH, 8], F32, True)
    eps_t = P.sbuf("eps_t", [128, 1], F32, True)

    def mm_group(out_ap, pairs, reads, writes):
        def fn(e):
            n = len(pairs)
            ins = None
            for i, (l, r) in enumerate(pairs):
                ins = e.matmul(out_ap, l, r, start=(i == 0), stop=(i == n - 1))
            return ins
        return P.op("pe", fn, reads, writes)

    def rstd_from_ssq(ps_ap, out_ap, n_feat, reads, writes, tmpkey):
        P.op("act", lambda e: e.activation(out=out_ap, in_=ps_ap, func=AF.Sqrt, bias=eps_t[0:ps_ap.shape[0], :],
                                           scale=1.0 / n_feat), reads=reads, writes=writes)
        P.op("dve", lambda e: e.reciprocal(out=out_ap, in_=out_ap), reads=writes, writes=writes)

    P.begin()
    sc_t = P.sbuf("sc_t", [128, 16, 2], F32)
    npre_s = P.sbuf("npre_s", [128, DEPTH, 16], F32)
    npost_s = P.sbuf("npost_s", [128, DEPTH, 16], F32)
    bmod_s = P.sbuf("bmod_s", [128, DEPTH, 48], F32)
    dec_s = P.sbuf("dec_s", [128, DEPTH, 8], F32)
    mod_s = P.sbuf("mod_s", [128, 48, 2], F32)
    P.op("dve", lambda e: e.memset(ones_bf[:], 1.0), writes=["ones_bf"])
    P.op("dve", lambda e: e.memset(ones_f[:], 1.0), writes=["ones_f"])
    P.op("dve", lambda e: e.memset(eps_t[:], EPS), writes=["eps"])
    for nm, dst, src in (("ident", ident, ident_d), ("sc", sc_t, condT), ("npre", npre_s, npre), ("npost", npost_s, npost),
                         ("bmod", bmod_s, bmod), ("qn", qn_s, qn), ("kvn", kvn_s, kvn), ("retn", retn_s, retn),
                         ("dec", dec_s, decay), ("esink", esink_s, sink)):
        P.dma("sp", "ld_" + nm, (lambda e, dst=dst, src=src: e.dma_start(out=dst[:], in_=src)), writes=[nm])
    P.op("act", lambda e: e.activation(out=sc_t[:], in_=sc_t[:], func=AF.Silu), reads=["sc"], writes=["sc"])
    P.op("act", lambda e: e.activation(out=esink_s[:], in_=esink_s[:], func=AF.Exp), reads=["esink"], writes=["esink"])
    P.op("act", lambda e: e.activation(out=dec_s[:], in_=dec_s[:], func=AF.Exp, scale=-1.0), reads=["dec"], writes=["dec"])
    P.op("dve", lambda e: e.tensor_scalar(out=dec_s[:], in0=dec_s[:], scalar1=1.0, scalar2=None, op0=ALU.add),
         reads=["dec"], writes=["dec"])
    P.op("act", lambda e: e.activation(out=dec_s[:], in_=dec_s[:], func=AF.Ln), reads=["dec"], writes=["dec"])
    P.op("dve", lambda e: e.tensor_scalar(out=lg_s[:], in0=dec_s[:], scalar1=-1.0, scalar2=None, op0=ALU.mult),
         reads=["dec"], writes=["lg"])
    wm_ring = Ring(P, "wm", 3, [128, 16, 256], F32)
    for l in range(DEPTH):
        for wt in range(24):
            wb, wk = wm_ring.next()
            P.dma("sp", wk, (lambda e, wb=wb, l=l, wt=wt: e.dma_start(out=wb[:], in_=w_mod[l, wt])), writes=[wk])
            for half in range(2):
                ft = wt * 2 + half
                mm_group(psb[0][:, ft * 2:ft * 2 + 2],
                         [(wb[:, k, half * 128:(half + 1) * 128], sc_t[:, k, :]) for k in range(16)],
                         reads=[wk, "sc"], writes=["ps0"])
        for c in range(2):
            P.op("dve", lambda e, c=c, l=l: e.tensor_tensor(
                out=mod_s[:, :, c], in0=psb[0][:, 0:96].rearrange("p (f c) -> p f c", c=2)[:, :, c],
                in1=bmod_s[:, l, :], op=ALU.add), reads=["ps0", "bmod"], writes=["mod%d" % c])
        for c in range(2):
            P.op("dve", lambda e, c=c, l=l: e.scalar_tensor_tensor(
                out=modA[:, l, c, :], in0=mod_s[:, 16:32, c], scalar=1.0, in1=npre_s[:, l, :],
                op0=ALU.add, op1=ALU.mult), reads=["mod%d" % c, "npre"], writes=["modA"])
            P.op("dve", lambda e, c=c, l=l: e.tensor_copy(out=modB[:, l, c, :], in_=mod_s[:, 0:16, c]),
                 reads=["mod%d" % c], writes=["modB"])
            P.op("dve", lambda e, c=c, l=l: e.tensor_tensor(
                out=modG[:, l, c, :], in0=mod_s[:, 32:48, c], in1=npost_s[:, l, :], op=ALU.mult),
                reads=["mod%d" % c, "npost"], writes=["modG"])
    P.end()

    def finish_block(xblk, xkeys, b, lnext, R, bkey):
        cond = 0 if b < NLB else 1
        t0 = b * 512
        if lnext is not None:
            pss, psk = R["ps_ss"].next()
            for j in range(16):
                sq, sqk = R["sq"].next()
                P.op("act", lambda e, sq=sq, j=j: e.activation(out=sq[:], in_=xblk[:, j, :], func=AF.Square),
                     reads=[xkeys[j]], writes=[sqk])
                P.op("pe", lambda e, sq=sq, j=j, pss=pss: e.matmul(pss[:], ones_bf[:], sq[:], start=(j == 0), stop=(j == 15)),
                     reads=[sqk, "ones_bf"], writes=[psk])
            rs, rsk = R["rstd"].next()
            rstd_from_ssq(pss[:], rs[:], D, [psk], [rsk], None)
            hst, hk = R["hst"].next()
            for j in range(16):
                tmp, tk = R["tmp"].next()
                P.op("dve", lambda e, tmp=tmp, j=j, rs=rs: e.tensor_tensor(out=tmp[:], in0=xblk[:, j, :], in1=rs[:], op=ALU.mult),
                     reads=[xkeys[j], rsk], writes=[tk])
                P.op("act", lambda e, tmp=tmp, j=j, hst=hst: e.activation(
                    out=hst[:, j, :], in_=tmp[:], func=AF.Identity, scale=modA[:, lnext, cond, j:j + 1],
                    bias=modB[:, lnext, cond, j:j + 1]), reads=[tk, "modA", "modB"], writes=[hk + "_%d" % j])
            for q4 in range(4):
                P.dma("sp", "st_%s_%d" % (hk, q4), lambda e, hst=hst, q4=q4: e.dma_start(
                    out=hT_d[q4 * 4:q4 * 4 + 4, :, t0:t0 + 512].rearrange("j p t -> p j t"), in_=hst[:, q4 * 4:q4 * 4 + 4, :]),
                    reads=[hk + "_%d" % j for j in range(q4 * 4, q4 * 4 + 4)], writes=["hT_d%d_%d" % (b, q4)])
                P.dma("sp", "st_%s_%d" % (bkey, q4), lambda e, q4=q4: e.dma_start(
                    out=xT_d[q4 * 4:q4 * 4 + 4, :, t0:t0 + 512].rearrange("j p t -> p j t"), in_=xblk[:, q4 * 4:q4 * 4 + 4, :]),
                    reads=list(xkeys[q4 * 4:q4 * 4 + 4]), writes=["xT_d%d_%d" % (b, q4)])
        else:
            for tt in range(4):
                yo, yk = R["yo"].next()
                for q4 in range(4):
                    pst, ptk = R["ps_t"].next()
                    def tr(e, pst=pst, q4=q4, tt=tt):
                        ins = None
                        for jj in range(4):
                            j = q4 * 4 + jj
                            ins = e.transpose(out=pst[:, jj * 128:(jj + 1) * 128], in_=xblk[:, j, tt * 128:(tt + 1) * 128],
                                              identity=ident[:])
                        return ins
                    P.op("pe", tr, reads=list(xkeys[q4 * 4:q4 * 4 + 4]) + ["ident"], writes=[ptk])
                    eng = "act" if q4 % 2 == 0 else "dve"
                    if eng == "act":
                        P.op("act", lambda e, pst=pst, q4=q4, yo=yo: e.activation(out=yo[:, q4 * 512:(q4 + 1) * 512], in_=pst[:], func=AF.Copy),
                             reads=[ptk], writes=[yk + "_%d" % q4])
                    else:
                        P.op("dve", lambda e, pst=pst, q4=q4, yo=yo: e.tensor_copy(out=yo[:, q4 * 512:(q4 + 1) * 512], in_=pst[:]),
                             reads=[ptk], writes=[yk + "_%d" % q4])
                if b < NLB:
                    dst = y_lat[t0 + tt * 128:t0 + (tt + 1) * 128, :]
                else:
                    dst = y_ctx[tt * 128:(tt + 1) * 128, :]
                out_toks.append(P.dma("sp", "st_" + yk, lambda e, yo=yo, dst=dst: e.dma_start(out=dst, in_=yo[:]),
                                      reads=[yk + "_%d" % q for q in range(4)], writes=["y%d_%d" % (b, tt)]))

    def finish_rings(last):
        R = {}
        if not last:
            R["ps_ss"] = PsRing(psb, [6])
            R["sq"] = Ring(P, "fsq", 3, [128, 512], BF16)
            R["rstd"] = Ring(P, "frs", 2, [128, 512], F32)
            R["hst"] = Ring(P, "fhst", 1, [128, 16, 512], BF16)
            R["tmp"] = Ring(P, "ftmp", 3, [128, 512], F32)
        else:
            R["yo"] = Ring(P, "fyo", 2, [128, D], F32)
            R["ps_t"] = PsRing(psb, [6, 7])
        return R

    P.begin()
    R0 = finish_rings(False)
    xin_ring = Ring(P, "xin", 2, [128, 4, D], F32)
    xb_ring = Ring(P, "xblk", 1, [128, 16, 512], F32)
    ps_t0 = PsRing(psb, [0, 1, 2, 3])
    for b in range(NBLK):
        xin, xik = xin_ring.next()
        src = x_lat[b * 512:(b + 1) * 512, :] if b < NLB else x_ctx
        P.dma("sp", xik, lambda e, xin=xin, src=src: e.dma_start(out=xin[:], in_=src.rearrange("(t p) f -> p t f", p=128)),
              writes=[xik])
        xblk, xk = xb_ring.next()
        for j in range(16):
            pst, ptk = ps_t0.next()
            def tr(e, pst=pst, j=j, xin=xin):
                ins = None
                for tt in range(4):
                    ins = e.transpose(out=pst[:, tt * 128:(tt + 1) * 128], in_=xin[:, tt, j * 128:(j + 1) * 128], identity=ident[:])
                return ins
            P.op("pe", tr, reads=[xik, "ident"], writes=[ptk])
            if j % 2 == 0:
                P.op("act", lambda e, pst=pst, j=j, xblk=xblk: e.activation(out=xblk[:, j, :], in_=pst[:], func=AF.Copy),
                     reads=[ptk], writes=[xk + "_%d" % j])
            else:
                P.op("dve", lambda e, pst=pst, j=j, xblk=xblk: e.tensor_copy(out=xblk[:, j, :], in_=pst[:]),
                     reads=[ptk], writes=[xk + "_%d" % j])
        finish_block(xblk, [xk + "_%d" % j for j in range(16)], b, 0, R0, xk)
    P.end()

    GROUPS = []
    g0 = (NBLK + 1) // 2
    GROUPS.append(list(range(0, g0)))
    GROUPS.append(list(range(g0, NBLK)))

    for l in range(DEPTH):
        for grp in GROUPS:
            P.begin()
            G = len(grp) * 512
            tg0 = grp[0] * 512
            hT = P.sbuf("hT", [128, 16, G], BF16)
            cs_t = P.sbuf("cs_t", [128, G], F32)
            sn_t = P.sbuf("sn_t", [128, G], F32)
            for q4 in range(4):
                P.dma("sp", "ld_h%d" % q4, lambda e, q4=q4: e.dma_start(
                    out=hT[:, q4 * 4:(q4 + 1) * 4, :], in_=hT_d[q4 * 4:(q4 + 1) * 4, :, tg0:tg0 + G].rearrange("j p t -> p j t")),
                    writes=["hT"])
            P.dma("sp", "ld_cs", lambda e: e.dma_start(out=cs_t[:], in_=cosT[:, tg0:tg0 + G]), writes=["cs"])
            P.dma("sp", "ld_sn", lambda e: e.dma_start(out=sn_t[:], in_=sinT[:, tg0:tg0 + G]), writes=["sn"])
            wring = Ring(P, "wfm", 4, [128, 16, 128], BF16)
            stg = Ring(P, "stg", 3, [128, G], BF16)
            rt1 = Ring(P, "rt1", 2, [128, 512], F32)
            rt2 = Ring(P, "rt2", 2, [128, 512], F32)
            psr = PsRing(psb, [0, 1, 2, 3, 4, 5])
            ei = 0
            wi = 0
            while wi < len(FM):
                name, kind, _ = FM[wi]
                wb, wk = wring.next()
                P.dma("pool", wk, lambda e, wb=wb, wi=wi: e.dma_start(out=wb[:], in_=w_fm[l, wi]), writes=[wk])
                if kind == "rope":
                    wb2, wk2 = wring.next()
                    P.dma("pool", wk2, lambda e, wb2=wb2, wi=wi: e.dma_start(out=wb2[:], in_=w_fm[l, wi + 1]), writes=[wk2])
                st, sk_ = stg.next()
                for bi, b in enumerate(grp):
                    c0 = bi * 512
                    ps, pk = psr.next()
                    mm_group(ps[:], [(wb[:, k, :], hT[:, k, c0:c0 + 512]) for k in range(16)], [wk, "hT"], [pk])
                    wkey = sk_ + "_%d" % bi
                    if kind == "rope":
                        ps2, pk2 = psr.next()
                        mm_group(ps2[:], [(wb2[:, k, :], hT[:, k, c0:c0 + 512]) for k in range(16)], [wk2, "hT"], [pk2])
                        t1, t1k = rt1.next()
                        t2, t2k = rt2.next()
                        P.op("dve", lambda e, t1=t1, ps=ps, c0=c0: e.tensor_tensor(out=t1[:], in0=ps[:], in1=cs_t[:, c0:c0 + 512], op=ALU.mult),
                             reads=[pk, "cs"], writes=[t1k])
                        P.op("dve", lambda e, t2=t2, ps2=ps2, c0=c0: e.tensor_tensor(out=t2[:], in0=ps2[:], in1=sn_t[:, c0:c0 + 512], op=ALU.mult),
                             reads=[pk2, "sn"], writes=[t2k])
                        P.op("pool", lambda e, t1=t1, t2=t2, st=st, c0=c0: e.tensor_tensor(out=st[:, c0:c0 + 512], in0=t1[:], in1=t2[:], op=ALU.add),
                             reads=[t1k, t2k], writes=[wkey])
                    elif kind in ("silu", "sig"):
                        fn_ = AF.Silu if kind == "silu" else AF.Sigmoid
                        P.op("act", lambda e, ps=ps, st=st, c0=c0, fn_=fn_: e.activation(out=st[:, c0:c0 + 512], in_=ps[:], func=fn_),
                             reads=[pk], writes=[wkey])
                    else:
                        sc = (128.0 ** -0.5) if kind == "copys" else 1.0
                        if ei % 2 == 0:
                            P.op("dve", lambda e, ps=ps, st=st, c0=c0, sc=sc: e.tensor_scalar(
                                out=st[:, c0:c0 + 512], in0=ps[:], scalar1=sc, scalar2=None, op0=ALU.mult), reads=[pk], writes=[wkey])
                        else:
                            P.op("act", lambda e, ps=ps, st=st, c0=c0, sc=sc: e.activation(out=st[:, c0:c0 + 512], in_=ps[:], func=AF.Copy, scale=sc),
                                 reads=[pk], writes=[wkey])
                        ei += 1
                slot = FM_SLOT[name]
                P.dma("sp", "st_" + sk_, lambda e, st=st, slot=slot: e.dma_start(out=zfm_d[slot, :, tg0:tg0 + G], in_=st[:]),
                      reads=[sk_ + "_%d" % bi for bi in range(len(grp))], writes=["zfm%d" % slot])
                wi += 2 if kind == "rope" else 1
            wtm_ring = Ring(P, "wtm", 2, [128, 16, 512], BF16)
            tms = Ring(P, "tms", 2, [128, 4, 512], BF16)
            of_ring = Ring(P, "ofr", 3, [128, 512], F32)
            sm_ring = Ring(P, "smr", 4, [128, 2], F32)
            has_ctx = NLB in grp
            kvrow = None
            if has_ctx:
                kvrow = P.sbuf("kvrow", [128, 256], F32)
                P.dma("sp", "ld_kvrow", lambda e: e.dma_start(out=kvrow[:], in_=kvn_row[:, l, :]), writes=["kvrow"])
            for (gname, gcols, gwho) in TMG:
                if gwho == "ctx" and not has_ctx:
                    continue
                if gname in SKIP:
                    continue
                co, ncol = TM_OFF[gname]
                wb, wk = wtm_ring.next()
                P.dma("pool", wk, lambda e, wb=wb, co=co, ncol=ncol: e.dma_start(out=wb[:, :, 0:ncol], in_=w_tm[l, :, :, co:co + ncol]),
                      writes=[wk])
                blocks = grp if gwho == "all" else [NLB]
                for b in blocks:
                    bi = grp.index(b)
                    is_ctx = (b == NLB)
                    ts_, tsk = tms.next()
                    for tt in range(4):
                        c0 = bi * 512 + tt * 128
                        ps, pk = psr.next()
                        mm_group(ps[:, 0:ncol], [(hT[:, k, c0:c0 + 128], wb[:, k, 0:ncol]) for k in range(16)], [wk, "hT"], [pk])
                        ctx_out = is_ctx and gname != "rk" and gname != "rv" and "ctxout" not in SKIP and (gname + "_out") not in SKIP
                        if gwho == "all" and not ctx_out:
                            sc = (128.0 ** -0.5) if gname == "rk" else 1.0
                            if tt % 2 == 0:
                                P.op("dve", lambda e, ps=ps, ts_=ts_, tt=tt, sc=sc, ncol=ncol: e.tensor_scalar(
                                    out=ts_[:, tt, 0:ncol], in0=ps[:, 0:ncol], scalar1=sc, scalar2=None, op0=ALU.mult),
                                    reads=[pk], writes=[tsk + "_%d" % tt])
                            else:
                                P.op("act", lambda e, ps=ps, ts_=ts_, tt=tt, sc=sc, ncol=ncol: e.activation(
                                    out=ts_[:, tt, 0:ncol], in_=ps[:, 0:ncol], func=AF.Copy, scale=sc), reads=[pk], writes=[tsk + "_%d" % tt])
                        if ctx_out:
                            cb, r0 = tt // 2, (tt % 2) * 128
                            if gname == "o1":
                                of, ofk = of_ring.next()
                                sm, smk = sm_ring.next()
                                P.op("act", lambda e, ps=ps, of=of, sm=sm: e.activation(out=of[:, 0:256], in_=ps[:, 0:256], func=AF.Square,
                                                                                        accum_out=sm[:, 0:1]), reads=[pk], writes=[ofk, smk])
                                P.op("act", lambda e, sm=sm: e.activation(out=sm[:, 1:2], in_=sm[:, 0:1], func=AF.Sqrt, bias=eps_t[:, :], scale=1.0 / 256),
                                     reads=[smk], writes=[smk])
                                P.op("dve", lambda e, sm=sm: e.reciprocal(out=sm[:, 1:2], in_=sm[:, 1:2]), reads=[smk], writes=[smk])
                                P.op("dve", lambda e, ps=ps, of=of, sm=sm: e.scalar_tensor_tensor(
                                    out=of[:, 0:256], in0=ps[:, 0:256], scalar=sm[:, 1:2], in1=kvrow[:], op0=ALU.mult, op1=ALU.mult),
                                    reads=[pk, smk, "kvrow", ofk], writes=[ofk])
                                P.op("dve", lambda e, ps=ps, of=of: e.tensor_copy(out=of[:, 256:448], in_=ps[:, 256:448]),
                                     reads=[pk, ofk], writes=[ofk])
                                for (dst, a, w_) in ((o_ckv, 0, 256), (o_kr, 256, 64), (o_sk, 320, 128)):
                                    out_toks.append(P.dma("sp", "st_" + ofk, lambda e, of=of, dst=dst, a=a, w_=w_, cb=cb, r0=r0: e.dma_start(
                                        out=dst[cb, l, r0:r0 + 128, :], in_=of[:, a:a + w_]), reads=[ofk], writes=["o_%d_%d" % (a, tt)]))
                            else:
                                dst = dict(nk=o_nk, nv=o_nv, sv=o_sv)[gname]
                                of, ofk = of_ring.next()
                                P.op("act" if tt % 2 == 0 else "dve",
                                     (lambda e, ps=ps, of=of, ncol=ncol: e.activation(out=of[:, 0:ncol], in_=ps[:, 0:ncol], func=AF.Copy)) if tt % 2 == 0 else
                                     (lambda e, ps=ps, of=of, ncol=ncol: e.tensor_copy(out=of[:, 0:ncol], in_=ps[:, 0:ncol])),
                                     reads=[pk], writes=[ofk])
                                if gwho == "all":
                                    P.op("pool", lambda e, of=of, ts_=ts_, tt=tt, ncol=ncol: e.tensor_copy(out=ts_[:, tt, 0:ncol], in_=of[:, 0:ncol]),
                                         reads=[ofk], writes=[tsk + "_%d" % tt])
                                out_toks.append(P.dma("sp", "st_" + ofk, lambda e, of=of, dst=dst, ncol=ncol, cb=cb, r0=r0: e.dma_start(
                                    out=dst[cb, l, r0:r0 + 128, :], in_=of[:, 0:ncol]), reads=[ofk], writes=["o_%s_%d" % (gname, tt)]))
                    if gwho == "all":
                        zo = ZTM[gname]
                        P.dma("sp", "st_" + tsk, lambda e, ts_=ts_, b=b, zo=zo, ncol=ncol: e.dma_start(
                            out=ztm_d[b * 512:(b + 1) * 512, zo:zo + ncol].rearrange("(t p) c -> p t c", p=128), in_=ts_[:, :, 0:ncol]),
                            reads=[tsk + "_%d" % tt for tt in range(4)], writes=["ztm_%s_%d" % (gname, b)])
            P.end()

        SEQS = [("lat", 0, S)] + [("ctx%d" % i, S + i * L, L) for i in range(2)]

        def attention(tag, nq, q_parts, key_tiles, m_out, scale, out_ps, sum_ps, R, po=0):
            (ops, opk), (sps, spk) = out_ps, sum_ps
            nk = len(key_tiles)
            for ki, (kparts, vl, bias, rk_) in enumerate(key_tiles):
                ps, pk = R["ps_s"].next()
                mm_group(ps[:, 0:nq], [(kp, qp) for kp, (qp, _) in zip(kparts, q_parts)],
                         list(rk_) + [qk for _, qk in q_parts], [pk])
                ex, exk = R["ex"].next()
                if bias is None:
                    P.op("act", lambda e, ps=ps, ex=ex: e.activation(out=ex[:, 0:nq], in_=ps[:, 0:nq], func=AF.Exp, scale=scale),
                         reads=[pk], writes=[exk])
                else:
                    bt, btk = bias
                    tb, tbk = R["tb"].next()
                    P.op("dve", lambda e, ps=ps, tb=tb, bt=bt: e.scalar_tensor_tensor(
                        out=tb[:, 0:nq], in0=ps[:, 0:nq], scalar=scale, in1=bt, op0=ALU.mult, op1=ALU.add),
                        reads=[pk, btk], writes=[tbk])
                    P.op("act", lambda e, tb=tb, ex=ex: e.activation(out=ex[:, 0:nq], in_=tb[:, 0:nq], func=AF.Exp),
                         reads=[tbk], writes=[exk])
                def pv(e, ex=ex, vl=vl, ki=ki):
                    e.matmul(ops[po:po + m_out, 0:nq], vl, ex[:, 0:nq], start=(ki == 0), stop=(ki == nk - 1))
                    return e.matmul(sps[po:po + m_out, 0:nq], ones_bf[:, 0:m_out], ex[:, 0:nq], start=(ki == 0), stop=(ki == nk - 1))
                P.op("pe", pv, reads=[exk, "ones_bf"] + list(rk_), writes=[opk, spk])

        def finish_head(tag, nq, out_ps, sum_ps, gp_ap, gpk, dst_ap, R, esink_ap=None, rows=128, stkey=None):
            (ops, opk), (sps, spk) = out_ps, sum_ps
            rc, rck = R["rc"].next()
            if esink_ap is not None:
                for (p0, ea) in esink_ap:
                    P.op("dve", lambda e, rc=rc, p0=p0, ea=ea: e.tensor_scalar(out=rc[p0:p0 + 64, 0:nq], in0=sps[p0:p0 + 64, 0:nq],
                                                                             scalar1=ea, scalar2=None, op0=ALU.add),
                         reads=[spk, "esink"], writes=[rck])
                P.op("dve", lambda e, rc=rc: e.reciprocal(out=rc[0:rows, 0:nq], in_=rc[0:rows, 0:nq]), reads=[rck], writes=[rck])
            else:
                P.op("dve", lambda e, rc=rc: e.reciprocal(out=rc[0:rows, 0:nq], in_=sps[0:rows, 0:nq]), reads=[spk], writes=[rck])
            P.op("dve", lambda e, rc=rc: e.tensor_tensor(out=rc[0:rows, 0:nq], in0=ops[0:rows, 0:nq], in1=rc[0:rows, 0:nq], op=ALU.mult),
                 reads=[opk, rck], writes=[rck])
            ob, obk = R["ob"].next()
            P.op("pool", lambda e, rc=rc, ob=ob: e.tensor_tensor(out=ob[0:rows, 0:nq], in0=rc[0:rows, 0:nq], in1=gp_ap, op=ALU.mult),
                 reads=[rck, gpk], writes=[obk])
            P.dma("pool", "st_" + obk, lambda e, ob=ob: e.dma_start(out=dst_ap, in_=ob[0:rows, 0:nq]), reads=[obk], writes=[stkey])

        def load_gp(R, slot, t0, nq):
            gp, gpk = R["gp"].next()
            P.dma("sp", gpk, lambda e, gp=gp: e.dma_start(out=gp[:, 0:nq], in_=zfm_d[slot, :, t0:t0 + nq]), writes=[gpk])
            return gp, gpk

        def att_rings(P, with_tb):
            R = dict(ps_s=PsRing(psb, [0, 1, 2]), ex=Ring(P, "ex", 3, [128, 512], BF16), rc=Ring(P, "rc", 2, [128, 512], F32),
                     ob=Ring(P, "ob", 2, [128, 512], BF16), gp=Ring(P, "gp", 2, [128, 512], BF16),
                     ps_o=PsRing(psb, [3, 4]), ps_d=PsRing(psb, [5, 6]))
            if with_tb:
                R["tb"] = Ring(P, "tb", 2, [128, 512], F32)
            return R

        for (sname, s0, slen) in ([] if "mla" in SKIP else SEQS):
            is_lat = sname == "lat"
            nkey = slen + (PAST if is_lat else 0)
            nkt = nkey // 128
            P.begin()
            R = att_rings(P, False)
            wq = P.sbuf("wq", [128, 4, 1024], BF16)
            wkv = P.sbuf("wkv", [128, 2, 1024], BF16)
            ckvT = P.sbuf("ckvT", [128, 2, nkey], BF16)
            krT = P.sbuf("krT", [64, nkey], BF16)
            knT = P.sbuf("knT", [128, nkey], BF16)
            vall = P.sbuf("vall", [128, nkt, 512], BF16)
            P.dma("pool", "ld_wq", lambda e: e.dma_start(out=wq[:], in_=w_qup[l]), writes=["wq"])
            P.dma("pool", "ld_wkv", lambda e: e.dma_start(out=wkv[:], in_=w_kvup[l]), writes=["wkv"])
            P.dma("sp", "ld_kr", lambda e: e.dma_start(out=krT[:, 0:slen], in_=zfm_d[FM_SLOT["kr"], 0:64, s0:s0 + slen]), writes=["krT"])
            kva_ring = Ring(P, "kva", 2, [128, 2, 512], BF16)
            sq_ring = Ring(P, "msq", 2, [128, 4, 512], BF16)
            rs_ring = Ring(P, "mrs", 2, [128, 512], F32)
            psm = PsRing(psb, [7])
            nch = (slen + 511) // 512
            for ch in range(nch):
                n = min(512, slen - ch * 512)
                t0 = s0 + ch * 512
                kv, kvk = kva_ring.next()
                P.dma("sp", kvk, lambda e, kv=kv, t0=t0, n=n: e.dma_start(
                    out=kv[:, :, 0:n], in_=zfm_d[FM_SLOT["kva0"]:FM_SLOT["kva0"] + 2, :, t0:t0 + n].rearrange("j p t -> p j t")), writes=[kvk])
                sq, sqk = sq_ring.next()
                P.op("act", lambda e, kv=kv, sq=sq, n=n: e.activation(out=sq[:, 0:2, 0:n], in_=kv[:, :, 0:n], func=AF.Square), reads=[kvk], writes=[sqk])
                ps, pk = psm.next()
                mm_group(ps[:, 0:n], [(ones_bf[:], sq[:, j, 0:n]) for j in range(2)], [sqk, "ones_bf"], [pk])
                rs, rsk = rs_ring.next()
                rstd_from_ssq(ps[:, 0:n], rs[:, 0:n], 256, [pk], [rsk], None)
                for j in range(2):
                    P.op("dve", lambda e, kv=kv, rs=rs, j=j, n=n, ch=ch: e.scalar_tensor_tensor(
                        out=ckvT[:, j, ch * 512:ch * 512 + n], in0=kv[:, j, 0:n], scalar=kvn_s[:, l, j:j + 1], in1=rs[:, 0:n],
                        op0=ALU.mult, op1=ALU.mult), reads=[kvk, rsk, "kvn"], writes=["ckvT_%d" % ch])
            ckv_keys = ["ckvT_%d" % ch for ch in range(nch)]
            if is_lat:
                cc = P.sbuf("cc", [128, 4, 256], F32)
                ck = P.sbuf("ck", [128, 4, 64], F32)
                P.dma("sp", "ld_cc", lambda e: e.dma_start(out=cc[:], in_=c_ckv[l].rearrange("(t p) f -> p t f", p=128)), writes=["cc"])
                P.dma("sp", "ld_ck", lambda e: e.dma_start(out=ck[:], in_=c_kr[l].rearrange("(t p) f -> p t f", p=128)), writes=["ck"])
                for j in range(2):
                    ps, pk = psm.next()
                    def tr(e, ps=ps, j=j):
                        ins = None
                        for tt in range(4):
                            ins = e.transpose(out=ps[:, tt * 128:(tt + 1) * 128], in_=cc[:, tt, j * 128:(j + 1) * 128], identity=ident[:])
                        return ins
                    P.op("pe", tr, reads=["cc", "ident"], writes=[pk])
                    P.op("dve", lambda e, ps=ps, j=j: e.tensor_copy(out=ckvT[:, j, slen:slen + 512], in_=ps[:]), reads=[pk], writes=["ckvT_c%d" % j])
                ckv_keys += ["ckvT_c0", "ckvT_c1"]
                ps, pk = psm.next()
                def tr2(e, ps=ps):
                    ins = None
                    for tt in range(4):
                        ins = e.transpose(out=ps[0:64, tt * 128:(tt + 1) * 128], in_=ck[:, tt, :], identity=ident[:])
                    return ins
                P.op("pe", tr2, reads=["ck", "ident"], writes=[pk])
                P.op("dve", lambda e, ps=ps: e.tensor_copy(out=krT[:, slen:slen + 512], in_=ps[0:64, :]), reads=[pk, "krT"], writes=["krT"])
            for kt in range(nkt):
                ps, pk = psm.next()
                mm_group(ps[:], [(ckvT[:, j, kt * 128:(kt + 1) * 128], wkv[:, j, 512:1024]) for j in range(2)], ckv_keys + ["wkv"], [pk])
                if kt % 2 == 0:
                    P.op("act", lambda e, ps=ps, kt=kt: e.activation(out=vall[:, kt, :], in_=ps[:], func=AF.Copy), reads=[pk], writes=["vall_%d" % kt])
                else:
                    P.op("dve", lambda e, ps=ps, kt=kt: e.tensor_copy(out=vall[:, kt, :], in_=ps[:]), reads=[pk], writes=["vall_%d" % kt])
            vkeys = ["vall_%d" % kt for kt in range(nkt)]
            qa_ring = Ring(P, "qa", 2, [128, 4, 512], BF16)
            cq_ring = Ring(P, "cq", 2, [128, 4, 512], BF16)
            qn_ring = Ring(P, "qnp", 2, [128, 512], BF16)
            qr_ring = Ring(P, "qrp", 2, [64, 512], BF16)
            qt_ring = Ring(P, "qtp", 2, [64, 512], F32)
            cs2 = Ring(P, "cs2", 2, [64, 512], F32)
            sn2 = Ring(P, "sn2", 2, [64, 512], F32)
            nqb = (slen + 511) // 512
            for h in range(4):
                for ch in range((nkey + 511) // 512):
                    n = min(512, nkey - ch * 512)
                    ps, pk = psm.next()
                    mm_group(ps[:, 0:n], [(wkv[:, j, h * 128:(h + 1) * 128], ckvT[:, j, ch * 512:ch * 512 + n]) for j in range(2)],
                             ckv_keys + ["wkv"], [pk])
                    P.op("dve", lambda e, ps=ps, ch=ch, n=n: e.tensor_copy(out=knT[:, ch * 512:ch * 512 + n], in_=ps[:, 0:n]),
                         reads=[pk], writes=["knT"])
                for qb in range(nqb):
                    nq = min(512, slen - qb * 512)
                    t0 = s0 + qb * 512
                    qa, qak = qa_ring.next()
                    P.dma("sp", qak, lambda e, qa=qa, t0=t0, nq=nq: e.dma_start(
                        out=qa[:, :, 0:nq], in_=zfm_d[0:4, :, t0:t0 + nq].rearrange("j p t -> p j t")), writes=[qak])
                    sq, sqk = sq_ring.next()
                    P.op("act", lambda e, qa=qa, sq=sq, nq=nq: e.activation(out=sq[:, :, 0:nq], in_=qa[:, :, 0:nq], func=AF.Square), reads=[qak], writes=[sqk])
                    ps, pk = psm.next()
                    mm_group(ps[:, 0:nq], [(ones_bf[:], sq[:, j, 0:nq]) for j in range(4)], [sqk, "ones_bf"], [pk])
                    rs, rsk = rs_ring.next()
                    rstd_from_ssq(ps[:, 0:nq], rs[:, 0:nq], 512, [pk], [rsk], None)
                    cq, cqk = cq_ring.next()
                    for j in range(4):
                        P.op("dve", lambda e, qa=qa, cq=cq, rs=rs, j=j, nq=nq: e.scalar_tensor_tensor(
                            out=cq[:, j, 0:nq], in0=qa[:, j, 0:nq], scalar=qn_s[:, l, j:j + 1], in1=rs[:, 0:nq], op0=ALU.mult, op1=ALU.mult),
                            reads=[qak, rsk, "qn"], writes=[cqk])
                    c0 = h * 256
                    ps, pk = psm.next()
                    mm_group(ps[:, 0:nq], [(wq[:, j, c0:c0 + 128], cq[:, j, 0:nq]) for j in range(4)], [cqk, "wq"], [pk])
                    qnp, qnk = qn_ring.next()
                    P.op("act", lambda e, ps=ps, qnp=qnp, nq=nq: e.activation(out=qnp[:, 0:nq], in_=ps[:, 0:nq], func=AF.Copy), reads=[pk], writes=[qnk])
                    ps, pk = psm.next()
                    mm_group(ps[0:64, 0:nq], [(wq[:, j, c0 + 128:c0 + 192], cq[:, j, 0:nq]) for j in range(4)], [cqk, "wq"], [pk])
                    qrp, qrk = qr_ring.next()
                    if is_lat:
                        cs, csk = cs2.next()
                        sn, snk = sn2.next()
                        P.dma("sp", csk, lambda e, cs=cs, t0=t0, nq=nq: e.dma_start(out=cs[:, 0:nq], in_=cosT[0:64, t0:t0 + nq]), writes=[csk])
                        P.dma("sp", snk, lambda e, sn=sn, t0=t0, nq=nq: e.dma_start(out=sn[:, 0:nq], in_=sinT[0:64, t0:t0 + nq]), writes=[snk])
                        qt, qtk = qt_ring.next()
                        P.op("dve", lambda e, ps=ps, qt=qt, cs=cs, nq=nq: e.tensor_tensor(out=qt[:, 0:nq], in0=ps[0:64, 0:nq], in1=cs[:, 0:nq], op=ALU.mult),
                             reads=[pk, csk], writes=[qtk])
                        ps, pk = psm.next()
                        mm_group(ps[0:64, 0:nq], [(wq[:, j, c0 + 192:c0 + 256], cq[:, j, 0:nq]) for j in range(4)], [cqk, "wq"], [pk])
                        qt2, qt2k = qt_ring.next()
                        P.op("dve", lambda e, ps=ps, qt2=qt2, sn=sn, nq=nq: e.tensor_tensor(out=qt2[:, 0:nq], in0=ps[0:64, 0:nq], in1=sn[:, 0:nq], op=ALU.mult),
                             reads=[pk, snk], writes=[qt2k])
                        P.op("pool", lambda e, qt=qt, qt2=qt2, qrp=qrp, nq=nq: e.tensor_tensor(out=qrp[:, 0:nq], in0=qt[:, 0:nq], in1=qt2[:, 0:nq], op=ALU.add),
                             reads=[qtk, qt2k], writes=[qrk])
                    else:
                        P.op("dve", lambda e, ps=ps, qrp=qrp, nq=nq: e.tensor_copy(out=qrp[:, 0:nq], in_=ps[0:64, 0:nq]), reads=[pk], writes=[qrk])
                    ops_ = R["ps_o"].next()
                    sps_ = R["ps_d"].next()
                    kts = []
                    for kt in range(nkt):
                        kts.append(([knT[:, kt * 128:(kt + 1) * 128], krT[:, kt * 128:(kt + 1) * 128]],
                                    vall[:, kt, h * 128:(h + 1) * 128], None, ["knT", "krT", "vall_%d" % kt]))
                    attention("mla", nq, [(qnp[:, 0:nq], qnk), (qrp[:, 0:nq], qrk)], kts, 128, 192.0 ** -0.5, ops_, sps_, R)
                    gp, gpk = load_gp(R, FM_SLOT["gp%d" % h], t0, nq)
                    finish_head("mla", nq, ops_, sps_, gp[:, 0:nq], gpk, oT_d[h, :, t0:t0 + nq], R, stkey="oT_%d_%d" % (h, t0))
            P.end()

        for (sname, s0, slen) in ([] if "nat" in SKIP else SEQS):
            is_lat = sname == "lat"
            nkey = slen + (PAST if is_lat else 0)
            nkt = nkey // 128
            P.begin()
            R = att_rings(P, is_lat)
            kT = P.sbuf("nkT", [128, nkey], BF16)
            vt = P.sbuf("nvt", [128, nkt, 128], BF16)
            q_ring = Ring(P, "nq", 2, [128, 512], BF16)
            psm = PsRing(psb, [7])
            if is_lat:
                cK = P.sbuf("cK", [128, 4, 512], F32)
                P.dma("sp", "ld_cK", lambda e: e.dma_start(out=cK[:], in_=c_nk[l].rearrange("(t p) f -> p t f", p=128)), writes=["cK"])
                b_ring = Ring(P, "nb", 8, [128, 512], F32)
            nqb = (slen + 511) // 512
            for h in range(4):
                P.dma("sp", "ld_nkT", lambda e, h=h: e.dma_start(out=kT[:, 0:slen], in_=zfm_d[FM_SLOT["nk%d" % h], :, s0:s0 + slen]), writes=["nkT"])
                for c4 in range(0, slen // 128, 4):
                    n4 = min(4, slen // 128 - c4)
                    P.dma("sp", "ld_nvt%d" % ((c4 // 4) % 2), lambda e, h=h, c4=c4, n4=n4: e.dma_start(
                        out=vt[:, c4:c4 + n4, :],
                        in_=ztm_d[s0 + c4 * 128:s0 + (c4 + n4) * 128, ZTM["nv"] + h * 128:ZTM["nv"] + (h + 1) * 128].rearrange("(t p) c -> p t c", p=128)),
                        reads=["nvt"], writes=["nvt"])
                if is_lat:
                    P.dma("pool", "ld_nvc", lambda e, h=h: e.dma_start(
                        out=vt[:, slen // 128:nkt, :], in_=c_nv[l, :, h * 128:(h + 1) * 128].rearrange("(t p) c -> p t c", p=128)),
                        reads=["nvt"], writes=["nvt"])
                    ps, pk = psm.next()
                    def tr(e, ps=ps, h=h):
                        ins = None
                        for tt in range(4):
                            ins = e.transpose(out=ps[:, tt * 128:(tt + 1) * 128], in_=cK[:, tt, h * 128:(h + 1) * 128], identity=ident[:])
                        return ins
                    P.op("pe", tr, reads=["cK", "ident"], writes=[pk])
                    P.op("dve", lambda e, ps=ps: e.tensor_copy(out=kT[:, slen:slen + 512], in_=ps[:]), reads=[pk, "nkT"], writes=["nkT"])
                for qb in range(nqb):
                    nq = min(512, slen - qb * 512)
                    t0 = s0 + qb * 512
                    q, qk = q_ring.next()
                    P.dma("sp", qk, lambda e, q=q, h=h, t0=t0, nq=nq: e.dma_start(out=q[:, 0:nq], in_=zfm_d[FM_SLOT["nq%d" % h], :, t0:t0 + nq]), writes=[qk])
                    kts = []
                    if is_lat:
                        lo = max(0, 8 * qb - 4)
                        hi = min(ROWS, 8 * qb + 12)
                        cls = 0 if qb == 0 else (2 if qb == NQB - 1 else 1)
                        for ti in range((hi - lo) // 2):
                            k0 = (lo + 2 * ti) * 64
                            bt, btk = b_ring.next()
                            P.dma("sp", btk, lambda e, bt=bt, cls=cls, ti=ti, h=h: e.dma_start(out=bt[:], in_=nat_bias[l, cls, ti, h]), writes=[btk])
                            kts.append(([kT[:, k0:k0 + 128]], vt[:, k0 // 128, :], (bt[:], btk), ["nkT", "nvt"]))
                        for ti in range(4):
                            k0 = slen + ti * 128
                            kts.append(([kT[:, k0:k0 + 128]], vt[:, k0 // 128, :], None, ["nkT", "nvt"]))
                    else:
                        for kt in range(nkt):
                            kts.append(([kT[:, kt * 128:(kt + 1) * 128]], vt[:, kt, :], None, ["nkT", "nvt"]))
                    ops_ = R["ps_o"].next()
                    sps_ = R["ps_d"].next()
                    attention("nat", nq, [(q[:, 0:nq], qk)], kts, 128, 128.0 ** -0.5, ops_, sps_, R)
                    gp, gpk = load_gp(R, FM_SLOT["gp%d" % (8 + h)], t0, nq)
                    finish_head("nat", nq, ops_, sps_, gp[:, 0:nq], gpk, oT_d[8 + h, :, t0:t0 + nq], R, stkey="oT_%d_%d" % (8 + h, t0))
            P.end()

        for (sname, s0, slen) in ([] if "swa" in SKIP else SEQS):
            is_lat = sname == "lat"
            nkey = slen + (PAST if is_lat else 0)
            nkt = nkey // 128
            P.begin()
            R = att_rings(P, is_lat)
            kT = P.sbuf("skT", [128, nkey], BF16)
            vt = P.sbuf("svt", [128, nkt, 64], BF16)
            q_ring = Ring(P, "sq", 2, [128, 512], BF16)
            psm = PsRing(psb, [7])
            if is_lat:
                cK = P.sbuf("scK", [128, 4, 2, 2, 64], F32)
                for dd in range(2):
                    for g_ in range(2):
                        P.dma("sp", "ld_scK%d" % dd, lambda e, dd=dd, g_=g_: e.dma_start(
                            out=cK[:, :, g_, dd, :], in_=c_sk[l, :, g_ * 64:(g_ + 1) * 64].rearrange("(t p) c -> p t c", p=128)),
                            reads=["scK%d" % dd], writes=["scK%d" % dd])
                mk = P.sbuf("smask", [128, 6, 512], F32)
                P.dma("sp", "ld_smask", lambda e: e.dma_start(out=mk[:], in_=swa_mask.rearrange("o p q -> p o q")), writes=["smask"])
            nqb = (slen + 511) // 512
            for g in range(2):
                P.dma("sp", "ld_skT", lambda e, g=g: e.dma_start(out=kT[:, 0:slen], in_=zfm_d[FM_SLOT["sk%d" % g], :, s0:s0 + slen]), writes=["skT"])
                for c4 in range(0, slen // 128, 4):
                    n4 = min(4, slen // 128 - c4)
                    P.dma("sp", "ld_svt%d" % ((c4 // 4) % 2), lambda e, g=g, c4=c4, n4=n4: e.dma_start(
                        out=vt[:, c4:c4 + n4, :],
                        in_=ztm_d[s0 + c4 * 128:s0 + (c4 + n4) * 128, ZTM["sv"] + g * 64:ZTM["sv"] + (g + 1) * 64].rearrange("(t p) c -> p t c", p=128)),
                        reads=["svt"], writes=["svt"])
                if is_lat:
                    P.dma("pool", "ld_svc", lambda e, g=g: e.dma_start(
                        out=vt[:, slen // 128:nkt, :], in_=c_sv[l, :, g * 64:(g + 1) * 64].rearrange("(t p) c -> p t c", p=128)),
                        reads=["svt"], writes=["svt"])
                    ps, pk = psm.next()
                    def tr(e, ps=ps, g=g):
                        ins = None
                        for tt in range(4):
                            ins = e.transpose(out=ps[:, tt * 128:(tt + 1) * 128], in_=cK[:, tt, g].rearrange("p d c -> p (d c)"), identity=ident[:])
                        return ins
                    P.op("pe", tr, reads=["scK0", "scK1", "ident"], writes=[pk])
                    P.op("dve", lambda e, ps=ps: e.tensor_copy(out=kT[:, slen:slen + 512], in_=ps[:]), reads=[pk, "skT"], writes=["skT"])
                for ti2 in range(2):
                    tq = g * 2 + ti2
                    for qb in range(nqb):
                        nq = min(512, slen - qb * 512)
                        t0 = s0 + qb * 512
                        q, qk = q_ring.next()
                        P.dma("sp", qk, lambda e, q=q, tq=tq, t0=t0, nq=nq: e.dma_start(out=q[:, 0:nq], in_=zfm_d[FM_SLOT["sq%d" % tq], :, t0:t0 + nq]), writes=[qk])
                        ops_ = R["ps_o"].next()
                        sps_ = R["ps_d"].next()
                        for hh in range(2):
                            p0 = hh * 64
                            kts = []
                            if is_lat:
                                for o in range(-1, 5):
                                    kb = 4 * qb + o
                                    if kb < 0 or kb >= slen // 128:
                                        continue
                                    kts.append(([kT[p0:p0 + 64, kb * 128:(kb + 1) * 128]], vt[:, kb, :], (mk[:, o + 1, :], "smask"), ["skT", "svt"]))
                                for ti in range(4):
                                    kb = slen // 128 + ti
                                    kts.append(([kT[p0:p0 + 64, kb * 128:(kb + 1) * 128]], vt[:, kb, :], None, ["skT", "svt"]))
                            else:
                                for kt in range(nkt):
                                    kts.append(([kT[p0:p0 + 64, kt * 128:(kt + 1) * 128]], vt[:, kt, :], None, ["skT", "svt"]))
                            attention("swa", nq, [(q[p0:p0 + 64, 0:nq], qk)], kts, 64, 64.0 ** -0.5, ops_, sps_, R, po=p0)
                        gp, gpk = load_gp(R, FM_SLOT["gp%d" % (12 + tq)], t0, nq)
                        es = [(0, esink_s[0:64, l, 2 * tq:2 * tq + 1]), (64, esink_s[64:128, l, 2 * tq + 1:2 * tq + 2])]
                        finish_head("swa", nq, ops_, sps_, gp[:, 0:nq], gpk, oT_d[12 + tq, :, t0:t0 + nq], R, esink_ap=es,
                                    stkey="oT_%d_%d" % (12 + tq, t0))
            P.end()

        P.begin()
        rtab = P.sbuf("rtab", [128, 4, 128], F32)
        rcol = P.sbuf("rcol", [128, 2], F32)
        P.dma("sp", "ld_rtab", lambda e: e.dma_start(out=rtab[:], in_=ret_tab.rearrange("a p q -> p a q")), writes=["rtab"])
        P.dma("sp", "ld_rcol", lambda e: e.dma_start(out=rcol[:], in_=ret_col), writes=["rcol"])
        maskT = P.sbuf("maskT", [128, 4, 128], F32)
        qdec = P.sbuf("qdec", [128, 4, 2, 128], F32)
        kdec = P.sbuf("kdec", [128, 4, 2], F32)
        cdec = P.sbuf("cdec", [128, 8], F32)
        rtmp = P.sbuf("rtmp", [128, 128], F32)
        for h in range(4):
            lf = lg_s[:, l, h:h + 1]
            lb = lg_s[:, l, 4 + h:5 + h]
            P.op("dve", lambda e, lf=lf: e.tensor_scalar(out=rtmp[:], in0=rtab[:, 0, :], scalar1=lf, scalar2=None, op0=ALU.mult),
                 reads=["rtab", "lg"], writes=["rtmp"])
            P.op("dve", lambda e, lb=lb: e.scalar_tensor_tensor(out=rtmp[:], in0=rtab[:, 1, :], scalar=lb, in1=rtmp[:], op0=ALU.mult, op1=ALU.add),
                 reads=["rtab", "lg", "rtmp"], writes=["rtmp"])
            P.op("act", lambda e, h=h: e.activation(out=maskT[:, h, :], in_=rtmp[:], func=AF.Exp), reads=["rtmp"], writes=["maskT"])
            P.op("act", lambda e, h=h, lf=lf: e.activation(out=qdec[:, h, 0, :], in_=rtab[:, 2, :], func=AF.Exp, scale=lf), reads=["rtab", "lg"], writes=["qdec"])
            P.op("act", lambda e, h=h, lb=lb: e.activation(out=qdec[:, h, 1, :], in_=rtab[:, 3, :], func=AF.Exp, scale=lb), reads=["rtab", "lg"], writes=["qdec"])
            P.op("act", lambda e, h=h, lf=lf: e.activation(out=kdec[:, h, 0:1], in_=rcol[:, 0:1], func=AF.Exp, scale=lf), reads=["rcol", "lg"], writes=["kdec"])
            P.op("act", lambda e, h=h, lb=lb: e.activation(out=kdec[:, h, 1:2], in_=rcol[:, 1:2], func=AF.Exp, scale=lb), reads=["rcol", "lg"], writes=["kdec"])
        P.op("act", lambda e: e.activation(out=cdec[:], in_=lg_s[:, l, :], func=AF.Exp, scale=128.0), reads=["lg"], writes=["cdec"])
        MAXC = S // 128
        qT = P.sbuf("rqT", [128, S], BF16)
        kT = P.sbuf("rkT", [128, S], BF16)
        ktm = P.sbuf("rktm", [128, MAXC, 128], BF16)
        vtm = P.sbuf("rvtm", [128, MAXC, 128], BF16)
        sb_all = P.sbuf("sb_all", [128, MAXC, 128], BF16)
        st_f = P.sbuf("st_f", [128, 128], F32)
        st_b = P.sbuf("st_b", [128, 128], F32)
        sf_bf = Ring(P, "sf_bf", 2, [128, 128], BF16)
        kd_ring = Ring(P, "kd", 3, [128, 128], BF16)
        pm_ring = Ring(P, "pm", 3, [128, 128], BF16)
        qf_ring = Ring(P, "qf", 3, [128, 2, 128], BF16)
        ro_ring = Ring(P, "ro", 2, [128, 512], F32)
        rsq_ring = Ring(P, "rsq", 2, [128, 512], BF16)
        rrs_ring = Ring(P, "rrs", 2, [128, 512], F32)
        rob_ring = Ring(P, "rob", 2, [128, 512], BF16)
        rgp_ring = Ring(P, "rgp", 2, [128, 512], BF16)
        ps_u = PsRing(psb, [0, 1])
        ps_i = PsRing(psb, [2, 3])
        ps_ro = PsRing(psb, [4, 5])
        ps_n = PsRing(psb, [6])
        for (sname, s0, slen) in SEQS:
            is_lat = sname == "lat"
            ncx = slen // 128
            for h in range(4):
                P.dma("pool" if "retpool" in SKIP else "sp", "ld_rqT", lambda e, h=h, s0=s0, slen=slen: e.dma_start(out=qT[:, 0:slen], in_=zfm_d[FM_SLOT["rq%d" % h], :, s0:s0 + slen]), writes=["rqT"])
                P.dma("pool" if "retpool" in SKIP else "sp", "ld_rkT", lambda e, h=h, s0=s0, slen=slen: e.dma_start(out=kT[:, 0:slen], in_=zfm_d[FM_SLOT["rk%d" % h], :, s0:s0 + slen]), writes=["rkT"])
                for c4 in range(0, ncx, 4):
                    n4 = min(4, ncx - c4)
                    for (nm_, dst_, zo_) in (("rktm", ktm, ZTM["rk"]), ("rvtm", vtm, ZTM["rv"])):
                        P.dma("sp", "ld_%s%d" % (nm_, (c4 // 4) % 2), lambda e, h=h, s0=s0, c4=c4, n4=n4, dst_=dst_, zo_=zo_: e.dma_start(
                            out=dst_[:, c4:c4 + n4, :],
                            in_=ztm_d[s0 + c4 * 128:s0 + (c4 + n4) * 128, zo_ + h * 128:zo_ + (h + 1) * 128].rearrange("(t p) c -> p t c", p=128)),
                            reads=[nm_], writes=[nm_])
                if is_lat:
                    P.dma("sp", "ld_stf", lambda e, h=h: e.dma_start(out=st_f[:], in_=c_st[l, 0, h]), writes=["st_f"])
                    P.dma("sp", "ld_stb", lambda e, h=h: e.dma_start(out=st_b[:], in_=c_st[l, 1, h]), writes=["st_b"])
                else:
                    P.op("dve", lambda e: e.memset(st_f[:], 0.0), writes=["st_f"])
                    P.op("dve", lambda e: e.memset(st_b[:], 0.0), writes=["st_b"])
                for c in range(ncx - 1, -1, -1):
                    P.op("act", lambda e, c=c: e.activation(out=sb_all[:, c, :], in_=st_b[:], func=AF.Copy), reads=["st_b"], writes=["sb_%d" % c])
                    kd, kdk = kd_ring.next()
                    P.op("dve", lambda e, kd=kd, c=c, h=h: e.tensor_scalar(out=kd[:], in0=ktm[:, c, :], scalar1=kdec[:, h, 1:2], scalar2=None, op0=ALU.mult),
                         reads=["rktm", "kdec"], writes=[kdk])
                    ps, pk = ps_u.next()
                    mm_group(ps[:, 0:128], [(kd[:], vtm[:, c, :])], [kdk, "rvtm"], [pk])
                    P.op("dve", lambda e, ps=ps, h=h: e.scalar_tensor_tensor(out=st_b[:], in0=st_b[:], scalar=cdec[:, 4 + h:5 + h], in1=ps[:, 0:128],
                                                                              op0=ALU.mult, op1=ALU.add), reads=["st_b", "cdec", pk], writes=["st_b"])
                if not is_lat:
                    cb = int(sname[3:])
                    out_toks.append(P.dma("sp", "st_sb", lambda e, cb=cb, h=h: e.dma_start(out=o_st[cb, l, 1, h], in_=st_b[:]), reads=["st_b"], writes=["o_stb"]))
                for c4 in range(0, ncx, 4):
                    ncg = min(4, ncx - c4)
                    nq = ncg * 128
                    t0 = s0 + c4 * 128
                    pso, psok = ps_ro.next()
                    for ci in range(ncg):
                        c = c4 + ci
                        sfb, sfk = sf_bf.next()
                        P.op("act", lambda e, sfb=sfb: e.activation(out=sfb[:], in_=st_f[:], func=AF.Copy), reads=["st_f"], writes=[sfk])
                        psi, psik = ps_i.next()
                        mm_group(psi[:, 0:128], [(kT[:, c * 128:(c + 1) * 128], qT[:, c * 128:(c + 1) * 128])], ["rkT", "rqT"], [psik])
                        pm, pmk = pm_ring.next()
                        P.op("dve", lambda e, psi=psi, pm=pm, h=h: e.tensor_tensor(out=pm[:], in0=psi[:, 0:128], in1=maskT[:, h, :], op=ALU.mult),
                             reads=[psik, "maskT"], writes=[pmk])
                        qf, qfk = qf_ring.next()
                        P.op("pool", lambda e, qf=qf, c=c, h=h: e.tensor_tensor(out=qf[:, 0, :], in0=qT[:, c * 128:(c + 1) * 128], in1=qdec[:, h, 0, :], op=ALU.mult),
                             reads=["rqT", "qdec"], writes=[qfk + "a"])
                        P.op("pool", lambda e, qf=qf, c=c, h=h: e.tensor_tensor(out=qf[:, 1, :], in0=qT[:, c * 128:(c + 1) * 128], in1=qdec[:, h, 1, :], op=ALU.mult),
                             reads=["rqT", "qdec"], writes=[qfk + "b"])
                        def omm(e, pso=pso, ci=ci, c=c, pm=pm, sfb=sfb, qf=qf):
                            o_ = pso[:, ci * 128:(ci + 1) * 128]
                            e.matmul(o_, vtm[:, c, :], pm[:], start=True, stop=False)
                            e.matmul(o_, sfb[:], qf[:, 0, :], start=False, stop=False)
                            return e.matmul(o_, sb_all[:, c, :], qf[:, 1, :], start=False, stop=True)
                        P.op("pe", omm, reads=["rvtm", pmk, sfk, qfk + "a", qfk + "b", "sb_%d" % c], writes=[psok])
                        kd, kdk = kd_ring.next()
                        P.op("dve", lambda e, kd=kd, c=c, h=h: e.tensor_scalar(out=kd[:], in0=ktm[:, c, :], scalar1=kdec[:, h, 0:1], scalar2=None, op0=ALU.mult),
                             reads=["rktm", "kdec"], writes=[kdk])
                        ps, pk = ps_u.next()
                        mm_group(ps[:, 0:128], [(kd[:], vtm[:, c, :])], [kdk, "rvtm"], [pk])
                        P.op("dve", lambda e, ps=ps, h=h: e.scalar_tensor_tensor(out=st_f[:], in0=st_f[:], scalar=cdec[:, h:h + 1], in1=ps[:, 0:128],
                                                                                  op0=ALU.mult, op1=ALU.add), reads=["st_f", "cdec", pk], writes=["st_f"])
                    ro, rok = ro_ring.next()
                    P.op("act", lambda e, ro=ro, pso=pso, nq=nq: e.activation(out=ro[:, 0:nq], in_=pso[:, 0:nq], func=AF.Copy), reads=[psok], writes=[rok])
                    rsq, rsqk = rsq_ring.next()
                    P.op("act", lambda e, ro=ro, rsq=rsq, nq=nq: e.activation(out=rsq[:, 0:nq], in_=ro[:, 0:nq], func=AF.Square), reads=[rok], writes=[rsqk])
                    psn, psnk = ps_n.next()
                    mm_group(psn[:, 0:nq], [(ones_bf[:], rsq[:, 0:nq])], [rsqk, "ones_bf"], [psnk])
                    rrs, rrsk = rrs_ring.next()
                    rstd_from_ssq(psn[:, 0:nq], rrs[:, 0:nq], 128, [psnk], [rrsk], None)
                    P.op("dve", lambda e, ro=ro, rrs=rrs, nq=nq, h=h: e.scalar_tensor_tensor(
                        out=ro[:, 0:nq], in0=ro[:, 0:nq], scalar=retn_s[:, l, h:h + 1], in1=rrs[:, 0:nq], op0=ALU.mult, op1=ALU.mult),
                        reads=[rok, rrsk, "retn"], writes=[rok])
                    gp, gpk = rgp_ring.next()
                    P.dma("sp", gpk, lambda e, gp=gp, h=h, t0=t0, nq=nq: e.dma_start(out=gp[:, 0:nq], in_=zfm_d[FM_SLOT["gp%d" % (4 + h)], :, t0:t0 + nq]), writes=[gpk])
                    rob, robk = rob_ring.next()
                    P.op("pool", lambda e, ro=ro, gp=gp, rob=rob, nq=nq: e.tensor_tensor(out=rob[:, 0:nq], in0=ro[:, 0:nq], in1=gp[:, 0:nq], op=ALU.mult),
                         reads=[rok, gpk], writes=[robk])
                    P.dma("sp", "st_" + robk, lambda e, rob=rob, h=h, t0=t0, nq=nq: e.dma_start(out=oT_d[4 + h, :, t0:t0 + nq], in_=rob[:, 0:nq]),
                          reads=[robk], writes=["oT_r%d_%d" % (h, t0)])
                if not is_lat:
                    cb = int(sname[3:])
                    out_toks.append(P.dma("sp", "st_sf", lambda e, cb=cb, h=h: e.dma_start(out=o_st[cb, l, 0, h], in_=st_f[:]), reads=["st_f"], writes=["o_stf"]))
        P.end()

        P.begin()
        last = (l == DEPTH - 1)
        RF = finish_rings(last)
        oT = Ring(P, "coT", 1, [128, 16, 512], BF16)
        gt = Ring(P, "cgt", 2, [128, 4, 512], BF16)
        wbr = Ring(P, "cwb", 2, [128, 4, 4, 128], BF16)
        wor = Ring(P, "cwo", 3, [128, 16, 128], BF16)
        tT = Ring(P, "ctT", 1, [128, 16, 512], BF16)
        tacc = Ring(P, "ctacc", 2, [128, 512], F32)
        ttmp = Ring(P, "cttmp", 3, [128, 512], F32)
        xb_ring = Ring(P, "cxb", 1, [128, 16, 512], F32)
        yb_ring = Ring(P, "cyb", 1, [128, 16, 512], F32)
        ysq = Ring(P, "cysq", 3, [128, 512], BF16)
        yrs = Ring(P, "cyrs", 2, [128, 512], F32)
        ps_u = PsRing(psb, [0, 1, 2])
        ps_y = PsRing(psb, [3, 4])
        ps_ys = PsRing(psb, [5])
        for b in range(NBLK):
            t0 = b * 512
            cond = 0 if b < NLB else 1
            o_, ok = oT.next()
            for q4 in range(4):
                P.dma("sp", ok + "_%d" % q4, lambda e, o_=o_, q4=q4, t0=t0: e.dma_start(
                    out=o_[:, q4 * 4:(q4 + 1) * 4, :], in_=oT_d[q4 * 4:(q4 + 1) * 4, :, t0:t0 + 512].rearrange("j p t -> p j t")),
                    writes=[ok + "_%d" % q4])
            xb, xk = xb_ring.next()
            for q4 in range(4):
                P.dma("sp", xk + "_%d" % q4, lambda e, xb=xb, q4=q4, t0=t0: e.dma_start(
                    out=xb[:, q4 * 4:(q4 + 1) * 4, :], in_=xT_d[q4 * 4:(q4 + 1) * 4, :, t0:t0 + 512].rearrange("j p t -> p j t")),
                    writes=[xk + "_t%d" % j for j in range(q4 * 4, q4 * 4 + 4)])
            t_, tk = tT.next()
            for j in range(16):
                g_, gk = gt.next()
                P.dma("sp", gk, lambda e, g_=g_, j=j, t0=t0: e.dma_start(
                    out=g_[:], in_=zfm_d[FM_SLOT["g0_0"]:FM_SLOT["g0_0"] + 64, :, t0:t0 + 512].rearrange("(n j) p t -> j p n t", j=16)[j]), writes=[gk])
                wb, wk = wbr.next()
                P.dma("pool", wk, lambda e, wb=wb, j=j: e.dma_start(out=wb[:], in_=w_br[l, j]), writes=[wk])
                ta, tak = tacc.next()
                for n in range(4):
                    ps, pk = ps_u.next()
                    mm_group(ps[:], [(wb[:, n, k, :], o_[:, n * 4 + k, :]) for k in range(4)], [wk, ok + "_%d" % n], [pk])
                    if n == 0:
                        P.op("dve", lambda e, ps=ps, g_=g_, ta=ta, n=n: e.tensor_tensor(out=ta[:], in0=ps[:], in1=g_[:, n, :], op=ALU.mult),
                             reads=[pk, gk], writes=[tak])
                    else:
                        tm, tmk = ttmp.next()
                        P.op("dve", lambda e, ps=ps, g_=g_, tm=tm, n=n: e.tensor_tensor(out=tm[:], in0=ps[:], in1=g_[:, n, :], op=ALU.mult),
                             reads=[pk, gk], writes=[tmk])
                        if n < 3:
                            P.op("pool", lambda e, ta=ta, tm=tm: e.tensor_tensor(out=ta[:], in0=ta[:], in1=tm[:], op=ALU.add), reads=[tak, tmk], writes=[tak])
                        else:
                            P.op("pool", lambda e, ta=ta, tm=tm, t_=t_, j=j: e.tensor_tensor(out=t_[:, j, :], in0=ta[:], in1=tm[:], op=ALU.add),
                                 reads=[tak, tmk], writes=[tk + "_%d" % j])
            yb, yk = yb_ring.next()
            pss, pssk = ps_ys.next()
            tkeys = [tk + "_%d" % j for j in range(16)]
            for i in range(16):
                wo, wok = wor.next()
                P.dma("pool", wok, lambda e, wo=wo, i=i: e.dma_start(out=wo[:], in_=w_o[l, i]), writes=[wok])
                ps, pk = ps_y.next()
                mm_group(ps[:], [(wo[:, j, :], t_[:, j, :]) for j in range(16)], [wok] + tkeys, [pk])
                P.op("dve", lambda e, ps=ps, yb=yb, i=i: e.tensor_copy(out=yb[:, i, :], in_=ps[:]), reads=[pk], writes=[yk + "_%d" % i])
                sq, sqk = ysq.next()
                P.op("act", lambda e, yb=yb, i=i, sq=sq: e.activation(out=sq[:], in_=yb[:, i, :], func=AF.Square), reads=[yk + "_%d" % i], writes=[sqk])
                P.op("pe", lambda e, sq=sq, i=i, pss=pss: e.matmul(pss[:], ones_bf[:], sq[:], start=(i == 0), stop=(i == 15)),
                     reads=[sqk, "ones_bf"], writes=[pssk])
            rs, rsk = yrs.next()
            rstd_from_ssq(pss[:], rs[:], D, [pssk], [rsk], None)
            for i in range(16):
                tm, tmk = ttmp.next()
                P.op("dve", lambda e, yb=yb, rs=rs, tm=tm, i=i: e.tensor_tensor(out=tm[:], in0=yb[:, i, :], in1=rs[:], op=ALU.mult),
                     reads=[yk + "_%d" % i, rsk], writes=[tmk])
                P.op("dve", lambda e, xb=xb, tm=tm, i=i, cond=cond: e.scalar_tensor_tensor(
                    out=xb[:, i, :], in0=tm[:], scalar=modG[:, l, cond, i:i + 1], in1=xb[:, i, :], op0=ALU.mult, op1=ALU.add),
                    reads=[tmk, xk + "_t%d" % i, "modG"], writes=[xk + "_t%d" % i])
            finish_block(xb, [xk + "_t%d" % i for i in range(16)], b, None if last else l + 1, RF, xk)
        P.end()

    P.close()
    return nc


def rope_tables(S, T):
    pos = np.arange(S)
    row = (pos // 64).astype(np.float32)
    col = (pos % 64).astype(np.float32)
    inv = (10000.0 ** (-np.arange(16, dtype=np.float32) / 16)).astype(np.float32)
    ang = np.concatenate([row[:, None] * inv[None], col[:, None] * inv[None]], axis=-1)
    cos = np.cos(ang).astype(np.float32)
    sin = np.sin(ang).astype(np.float32)
    cT = np.ones((128, T), np.float32)
    sT = np.zeros((128, T), np.float32)
    for p in range(128):
        d = p % 64
        f = d % 32
        cT[p, :S] = cos[:, f]
        sT[p, :S] = -sin[:, f] if d < 32 else sin[:, f]
    return cT, sT


def nat_bias_tables(rpb, rows):
    DEPTH = rpb.shape[0]
    nb = rows // 8
    out = np.full((DEPTH, 3, 8, 4, 128, 512), NEG, np.float32)
    for cls, b in ((0, 0), (1, 1), (2, nb - 1)):
        if cls == 1 and nb < 3:
            continue
        lo = max(0, 8 * b - 4)
        hi = min(rows, 8 * b + 12)
        i = np.arange(2)[:, None, None, None]
        kc = np.arange(64)[None, :, None, None]
        j = np.arange(8)[None, None, :, None]
        qc = np.arange(64)[None, None, None, :]
        for ti in range((hi - lo) // 2):
            kr = lo + 2 * ti + i
            r = 8 * b + j
            rs = np.clip(r - 4, 0, rows - 8)
            cs = np.clip(qc - 8, 0, 48)
            valid = (kr >= rs) & (kr < rs + 8) & (kc >= cs) & (kc < cs + 16)
            ro = np.clip(kr - r + 7, 0, 14)
            co = np.clip(kc - qc + 15, 0, 30)
            valid = np.broadcast_to(valid, (2, 64, 8, 64))
            ro = np.broadcast_to(ro, (2, 64, 8, 64))
            co = np.broadcast_to(co, (2, 64, 8, 64))
            g = rpb[:, :, ro, co]
            g = np.where(valid[None, None], g, np.float32(NEG))
            out[:, cls, ti] = g.reshape(DEPTH, 4, 128, 512)
    return out


def const_tables():
    k = np.arange(128)[:, None].astype(np.float32)
    q = np.arange(128)[None, :].astype(np.float32)
    Df = np.where(k <= q, q - k, 0.0).astype(np.float32)
    Ub = np.where(k > q, k - q, 0.0).astype(np.float32)
    i1 = np.broadcast_to(np.arange(128, dtype=np.float32)[None, :] + 1.0, (128, 128))
    i2 = np.broadcast_to(128.0 - np.arange(128, dtype=np.float32)[None, :], (128, 128))
    ret_tab = np.stack([Df, Ub, i1, i2]).astype(np.float32)
    ret_col = np.stack([127.0 - np.arange(128), np.arange(128)], axis=1).astype(np.float32)
    kk = np.arange(128)[:, None]
    qq = np.arange(512)[None, :]
    swa = np.stack([np.where(np.abs(qq - (o * 128 + kk)) <= 128, 0.0, NEG) for o in range(-1, 5)]).astype(np.float32)
    return ret_tab, ret_col, swa


def pcol(v, n):
    return np.ascontiguousarray(v.reshape(n, 128).T)


def host_weights(S, DEPTH, inp):
    T = S + 2 * L
    w = {}
    w_in = inp["w_in"]
    fm = np.empty((DEPTH, len(FM), 128, 16, 128), np.float32)
    for ti, (_, _, cols) in enumerate(FM):
        blk = w_in[:, :, cols]
        fm[:, ti] = blk.reshape(DEPTH, 16, 128, 128).transpose(0, 2, 1, 3)
    w["w_fm"] = fm
    tmcols = sum([c for _, c, _ in TMG], [])
    w["w_tm"] = np.ascontiguousarray(w_in[:, :, tmcols].reshape(DEPTH, 16, 128, TM_COLS).transpose(0, 2, 1, 3))
    w["w_mod"] = np.ascontiguousarray(inp["w_mod"].reshape(DEPTH, 16, 128, 24, 256).transpose(0, 3, 2, 1, 4))
    w["bmod"] = np.ascontiguousarray(np.stack([pcol(inp["b_mod"][l], 48) for l in range(DEPTH)], axis=1))
    w["npre"] = np.ascontiguousarray(np.stack([pcol(inp["norm_pre"][l], 16) for l in range(DEPTH)], axis=1))
    w["npost"] = np.ascontiguousarray(np.stack([pcol(inp["norm_post"][l], 16) for l in range(DEPTH)], axis=1))
    w["qn"] = np.ascontiguousarray(np.stack([pcol(inp["mla_q_norm"][l], 4) for l in range(DEPTH)], axis=1))
    w["kvn"] = np.ascontiguousarray(np.stack([pcol(inp["mla_kv_norm"][l], 2) for l in range(DEPTH)], axis=1))
    w["retn"] = np.ascontiguousarray(np.stack([pcol(inp["ret_norm"][l], 4) for l in range(DEPTH)], axis=1))
    w["kvn_row"] = np.ascontiguousarray(np.broadcast_to(inp["mla_kv_norm"][None], (128, DEPTH, 256)))
    w["decay"] = np.ascontiguousarray(np.broadcast_to(inp["ret_decay"].reshape(DEPTH, 8)[None], (128, DEPTH, 8)))
    w["sink"] = np.ascontiguousarray(np.broadcast_to(inp["swa_sink"][None], (128, DEPTH, 8)))
    qcols = []
    for h in range(4):
        b0 = h * 192
        qcols += list(range(b0, b0 + 128)) + list(range(b0 + 128, b0 + 192)) + _swap64(b0 + 128)
    w["w_qup"] = np.ascontiguousarray(inp["mla_w_q_up"][:, :, qcols].reshape(DEPTH, 4, 128, 1024).transpose(0, 2, 1, 3))
    kcols = []
    for h in range(4):
        kcols += list(range(h * 256, h * 256 + 128))
    for h in range(4):
        kcols += list(range(h * 256 + 128, h * 256 + 256))
    w["w_kvup"] = np.ascontiguousarray(inp["mla_w_kv_up"][:, :, kcols].reshape(DEPTH, 2, 128, 1024).transpose(0, 2, 1, 3))
    w["w_br"] = np.ascontiguousarray(inp["w_branch"].reshape(DEPTH, 4, 4, 128, 16, 128).transpose(0, 4, 3, 1, 2, 5))
    w["w_o"] = np.ascontiguousarray(inp["w_out"].reshape(DEPTH, 16, 128, 16, 128).transpose(0, 3, 2, 1, 4))
    w["nat_bias"] = nat_bias_tables(inp["nat_rpb"], S // 64)
    cT, sT = rope_tables(S, T)
    w["cosT"], w["sinT"] = cT, sT
    w["ident"] = np.eye(128, dtype=np.float32)
    rt, rc, sw = const_tables()
    w["ret_tab"], w["ret_col"], w["swa_mask"] = rt, rc, sw
    return w


def core_inputs(S, DEPTH, inp, shared, core):
    seq = core // 2
    m = dict(shared)
    m["x_lat"] = np.ascontiguousarray(inp["x_sample"][seq])
    m["x_ctx"] = np.ascontiguousarray(inp["x_prompt"][2 * core:2 * core + 2].reshape(2 * L, D))
    m["c_ckv"] = np.ascontiguousarray(inp["cache_mla_ckv"][seq])
    m["c_kr"] = np.ascontiguousarray(inp["cache_mla_krope"][seq])
    m["c_st"] = np.ascontiguousarray(inp["state_ret"][seq])
    m["c_nk"] = np.ascontiguousarray(inp["cache_nat_k"][seq].reshape(DEPTH, PAST, 512))
    m["c_nv"] = np.ascontiguousarray(inp["cache_nat_v"][seq].reshape(DEPTH, PAST, 512))
    m["c_sk"] = np.ascontiguousarray(inp["cache_swa_k"][seq].reshape(DEPTH, PAST, 128))
    m["c_sv"] = np.ascontiguousarray(inp["cache_swa_v"][seq].reshape(DEPTH, PAST, 128))
    cond = np.stack([inp["c"][seq], inp["c_ctx"]], axis=-1)
    m["condT"] = np.ascontiguousarray(cond.reshape(16, 128, 2).transpose(1, 0, 2))
    return m


def run(S, DEPTH, inp, n_cores=8, trace=False):
    inp = {k: np.asarray(v, dtype=np.float32) for k, v in inp.items()}
    nc = build(S, DEPTH)
    shared = host_weights(S, DEPTH, inp)
    in_maps = [core_inputs(S, DEPTH, inp, shared, c) for c in range(n_cores)]
    res = run_bass_kernel_spmd(nc, in_maps, core_ids=list(range(n_cores)), trace=trace)
    R = res.results
    nb = 2 * n_cores
    yp = np.concatenate([R[c]["y_ctx"].reshape(2, L, D) for c in range(n_cores)], axis=0)
    ys = np.stack([R[2 * i]["y_lat"] for i in range(n_cores // 2)], axis=0)
    ckv = np.concatenate([R[c]["o_ckv"] for c in range(n_cores)], axis=0)
    kr = np.concatenate([R[c]["o_kr"] for c in range(n_cores)], axis=0)
    st = np.concatenate([R[c]["o_st"] for c in range(n_cores)], axis=0)
    nk = np.concatenate([R[c]["o_nk"] for c in range(n_cores)], axis=0).reshape(nb, DEPTH, L, 4, 128)
    nv = np.concatenate([R[c]["o_nv"] for c in range(n_cores)], axis=0).reshape(nb, DEPTH, L, 4, 128)
    sk = np.concatenate([R[c]["o_sk"] for c in range(n_cores)], axis=0).reshape(nb, DEPTH, L, 2, 64)
    sv = np.concatenate([R[c]["o_sv"] for c in range(n_cores)], axis=0).reshape(nb, DEPTH, L, 2, 64)
    outs = (yp, ys, ckv, kr, st, nk, nv, sk, sv)
    return tuple(np.ascontiguousarray(o, dtype=np.float32) for o in outs), res


def kernel(**inputs):
    outs, _ = run(4096, 4, inputs)
    return outs
```

```python
import numpy as np
from contextlib import ExitStack
import concourse.bass as bass
import concourse.mybir as mybir
from concourse.bass_utils import run_bass_kernel_spmd

F32 = mybir.dt.float32
BF16 = mybir.dt.bfloat16
AF = mybir.ActivationFunctionType
ALU = mybir.AluOpType
ENGS = ("pe", "act", "dve", "pool", "sp")
D = 2048
L = 256
PAST = 512
EPS = 1e-6
NEG = -30000.0
SKIP = set()
MAXPHASE = 10 ** 9


class Prog:
    def __init__(self, nc):
        self.nc = nc
        self.base = ExitStack()
        self.stack = None
        self.ops = {e: [] for e in ENGS}
        self.cnt = {e: 0 for e in ENGS}
        self.sem = {e: self.base.enter_context(nc.semaphore("s_" + e)) for e in ENGS}
        self.dsem = {}
        self.physp = {}
        self.nused = {}
        self.kmap = {}
        self.last_w = {}
        self.reads = {}
        self.known = {e: {} for e in ENGS}
        self.phase_keys = set()
        self.n_ops = 0

    def sbuf(self, name, shape, dtype, persist=False):
        st = self.base if persist else self.stack
        self.uid = getattr(self, "uid", 0) + 1
        return st.enter_context(self.nc.sbuf_tensor("%s_u%d" % (name, self.uid), list(shape), dtype))

    def psum(self, name, shape, dtype=F32):
        return self.base.enter_context(self.nc.psum_tensor(name, list(shape), dtype))

    def dma_sem(self, key, eng):
        if key not in self.kmap:
            used = self.nused.setdefault(eng, 0)
            self.nused[eng] += 1
            pool = self.physp.setdefault(eng, [])
            if used >= len(pool):
                idx = len(self.dsem)
                s = self.base.enter_context(self.nc.semaphore("d_%d" % idx))
                self.dsem[idx] = [s, 0]
                pool.append(idx)
            self.kmap[key] = pool[used]
        return self.kmap[key]

    def _deps(self, eng, reads, writes):
        deps = []
        for r in reads:
            t = self.last_w.get(r)
            if t is not None:
                deps.append(t)
        for w in writes:
            t = self.last_w.get(w)
            if t is not None:
                deps.append(t)
            deps.extend(self.reads.get(w, {}).items())
        waits = {}
        kn = self.known[eng]
        for (src, val) in deps:
            if src == eng and eng in ("pe", "sp"):
                continue
            if kn.get(src, 0) >= val:
                continue
            if waits.get(src, 0) < val:
                waits[src] = val
        for src, val in waits.items():
            kn[src] = val
        return list(waits.items())

    def _commit(self, tok, reads, writes):
        for r in reads:
            d = self.reads.setdefault(r, {})
            if d.get(tok[0], 0) < tok[1]:
                d[tok[0]] = tok[1]
        for w in writes:
            self.last_w[w] = tok
            self.reads[w] = {}

    def op(self, eng, fn, reads=(), writes=()):
        if self.dead:
            return None
        waits = self._deps(eng, reads, writes)
        self.cnt[eng] += 1
        tok = (eng, self.cnt[eng])
        self.ops[eng].append((waits, fn, ("eng", eng)))
        self._commit(tok, reads, writes)
        self.n_ops += 1
        return tok

    def dma(self, eng, key, fn, reads=(), writes=()):
        if self.dead:
            return None
        key = self.dma_sem(key, eng)
        ds = self.dsem[key]
        self.phase_keys.add(key)
        waits = self._deps(eng, reads, writes)
        src = ("dma", key)
        kn = self.known[eng]
        if ds[1] > 0 and kn.get(src, 0) < ds[1]:
            waits = [w for w in waits if w[0] != src] + [(src, ds[1])]
            kn[src] = ds[1]
        ds[1] += 16
        tok = (src, ds[1])
        self.ops[eng].append((waits, fn, ("dma", key)))
        self._commit(tok, reads, writes)
        self.n_ops += 1
        return tok

    def _semof(self, src):
        if isinstance(src, tuple):
            return self.dsem[src[1]][0]
        return self.sem[src]

    def begin(self):
        self.stack = ExitStack()
        self.nphase = getattr(self, "nphase", 0) + 1
        self.dead = self.nphase > MAXPHASE

    def end(self):
        waits = [(("dma", k), self.dsem[k][1]) for k in sorted(self.phase_keys)]
        self.cnt["sp"] += 1
        self.ops["sp"].append((waits, lambda e: e.nop(), ("eng", "sp")))
        final = dict(self.cnt)
        for e in ENGS:
            w = [(f, final[f]) for f in ENGS if f != e and final[f] > self.known[e].get(f, 0)]
            self.ops[e].append((w, None, None))
        nc = self.nc
        with nc.Block() as block:
            def mk(e):
                def body(eng):
                    for (waits, fn, kind) in self.ops[e]:
                        for (src, val) in waits:
                            eng.wait_ge(self._semof(src), val)
                        if fn is None:
                            continue
                        ins = fn(eng)
                        if kind[0] == "eng":
                            ins.then_inc(self.sem[e], 1)
                        else:
                            ins.then_inc(self.dsem[kind[1]][0], 16)
                return body
            block.tensor(mk("pe"))
            block.scalar(mk("act"))
            block.vector(mk("dve"))
            block.gpsimd(mk("pool"))
            block.sync(mk("sp"))
        self.ops = {e: [] for e in ENGS}
        self.last_w = {}
        self.reads = {}
        for e in ENGS:
            for f in ENGS:
                self.known[e][f] = final[f]
            for k in self.dsem:
                self.known[e][("dma", k)] = self.dsem[k][1]
        self.phase_keys = set()
        self.kmap = {}
        self.nused = {}
        self.stack.close()
        self.stack = None

    def close(self):
        self.base.close()


class Ring:
    def __init__(self, P, name, n, shape, dtype):
        self.bufs = [P.sbuf("%s%d" % (name, i), shape, dtype) for i in range(n)]
        self.keys = ["%s%d" % (name, i) for i in range(n)]
        self.n = n
        self.i = 0

    def next(self):
        j = self.i % self.n
        self.i += 1
        return self.bufs[j], self.keys[j]


class PsRing:
    def __init__(self, banks, idxs):
        self.banks = [banks[i] for i in idxs]
        self.keys = ["ps%d" % i for i in idxs]
        self.n = len(idxs)
        self.i = 0

    def next(self):
        j = self.i % self.n
        self.i += 1
        return self.banks[j], self.keys[j]


OFF = dict(qa=0, kva=512, kr=768, rq=832, rk=1344, rv=1856, nq=2368, nk=2880, nv=3392, sq=3904, sk=4416, sv=4544,
           gp=4672, gate=6720)


def _swap64(base):
    return list(range(base + 32, base + 64)) + list(range(base, base + 32))


def fm_tiles():
    t = []
    for i in range(4):
        t.append(("qa%d" % i, "copy", list(range(OFF["qa"] + 128 * i, OFF["qa"] + 128 * (i + 1)))))
    for i in range(2):
        t.append(("kva%d" % i, "copy", list(range(OFF["kva"] + 128 * i, OFF["kva"] + 128 * (i + 1)))))
    kr = list(range(OFF["kr"], OFF["kr"] + 64))
    t.append(("kr", "rope", kr + kr))
    t.append(("kr_s", "swap", _swap64(OFF["kr"]) * 2))
    for h in range(4):
        t.append(("rq%d" % h, "copy", list(range(OFF["rq"] + 128 * h, OFF["rq"] + 128 * (h + 1)))))
    for h in range(4):
        t.append(("rk%d" % h, "copys", list(range(OFF["rk"] + 128 * h, OFF["rk"] + 128 * (h + 1)))))
    for h in range(4):
        t.append(("nq%d" % h, "copy", list(range(OFF["nq"] + 128 * h, OFF["nq"] + 128 * (h + 1)))))
    for h in range(4):
        t.append(("nk%d" % h, "copy", list(range(OFF["nk"] + 128 * h, OFF["nk"] + 128 * (h + 1)))))
    for i in range(4):
        b0, b1 = OFF["sq"] + 128 * i, OFF["sq"] + 128 * i + 64
        t.append(("sq%d" % i, "rope", list(range(b0, b0 + 128))))
        t.append(("sq%d_s" % i, "swap", _swap64(b0) + _swap64(b1)))
    for g in range(2):
        b0 = OFF["sk"] + 64 * g
        t.append(("sk%d" % g, "rope", list(range(b0, b0 + 64)) * 2))
        t.append(("sk%d_s" % g, "swap", _swap64(b0) * 2))
    for i in range(16):
        t.append(("gp%d" % i, "silu", list(range(OFF["gp"] + 128 * i, OFF["gp"] + 128 * (i + 1)))))
    for n in range(4):
        for j in range(16):
            b0 = OFF["gate"] + n * D + 128 * j
            t.append(("g%d_%d" % (n, j), "sig", list(range(b0, b0 + 128))))
    return t


FM = fm_tiles()
FM_SLOT = {}
for _n, _k, _c in FM:
    if _k != "swap":
        FM_SLOT[_n] = len(FM_SLOT)
NSLOT = len(FM_SLOT)

TMG = [("rk", list(range(OFF["rk"], OFF["rk"] + 512)), "all"),
       ("rv", list(range(OFF["rv"], OFF["rv"] + 512)), "all"),
       ("nv", list(range(OFF["nv"], OFF["nv"] + 512)), "all"),
       ("sv", list(range(OFF["sv"], OFF["sv"] + 128)), "all"),
       ("o1", list(range(OFF["kva"], OFF["kva"] + 256)) + list(range(OFF["kr"], OFF["kr"] + 64))
        + list(range(OFF["sk"], OFF["sk"] + 128)), "ctx"),
       ("nk", list(range(OFF["nk"], OFF["nk"] + 512)), "ctx")]
TM_OFF = {}
_o = 0
for _n, _c, _w in TMG:
    TM_OFF[_n] = (_o, len(_c))
    _o += len(_c)
TM_COLS = _o
ZTM = dict(rk=0, rv=512, nv=1024, sv=1536)
ZTM_COLS = 1664


def build(S, DEPTH):
    assert S % 512 == 0 and S >= 1024
    T = S + 2 * L
    NLB = S // 512
    NBLK = NLB + 1
    ROWS = S // 64
    NQB = S // 512
    nc = bass.Bass("TRN2", target_bir_lowering=False)

    def din(name, shape, dt=F32):
        return nc.dram_tensor(name, list(shape), dt, kind="ExternalInput").ap()

    def dout(name, shape):
        return nc.dram_tensor(name, list(shape), F32, kind="ExternalOutput").ap()

    def dscr(name, shape, dt):
        return nc.dram_tensor(name, list(shape), dt, kind="Internal").ap()

    x_lat = din("x_lat", [S, D])
    x_ctx = din("x_ctx", [2 * L, D])
    c_ckv = din("c_ckv", [DEPTH, PAST, 256])
    c_kr = din("c_kr", [DEPTH, PAST, 64])
    c_st = din("c_st", [DEPTH, 2, 4, 128, 128])
    c_nk = din("c_nk", [DEPTH, PAST, 512])
    c_nv = din("c_nv", [DEPTH, PAST, 512])
    c_sk = din("c_sk", [DEPTH, PAST, 128])
    c_sv = din("c_sv", [DEPTH, PAST, 128])
    condT = din("condT", [128, 16, 2])
    w_mod = din("w_mod", [DEPTH, 24, 128, 16, 256])
    bmod = din("bmod", [128, DEPTH, 48])
    npre = din("npre", [128, DEPTH, 16])
    npost = din("npost", [128, DEPTH, 16])
    w_fm = din("w_fm", [DEPTH, len(FM), 128, 16, 128])
    w_tm = din("w_tm", [DEPTH, 128, 16, TM_COLS])
    qn = din("qn", [128, DEPTH, 4])
    kvn = din("kvn", [128, DEPTH, 2])
    retn = din("retn", [128, DEPTH, 4])
    kvn_row = din("kvn_row", [128, DEPTH, 256])
    decay = din("decay", [128, DEPTH, 8])
    sink = din("sink", [128, DEPTH, 8])
    w_qup = din("w_qup", [DEPTH, 128, 4, 1024])
    w_kvup = din("w_kvup", [DEPTH, 128, 2, 1024])
    w_br = din("w_br", [DEPTH, 16, 128, 4, 4, 128])
    w_o = din("w_o", [DEPTH, 16, 128, 16, 128])
    nat_bias = din("nat_bias", [DEPTH, 3, 8, 4, 128, 512])
    cosT = din("cosT", [128, T])
    sinT = din("sinT", [128, T])
    ident_d = din("ident", [128, 128])
    ret_tab = din("ret_tab", [4, 128, 128])
    ret_col = din("ret_col", [128, 2])
    swa_mask = din("swa_mask", [6, 128, 512])

    y_lat = dout("y_lat", [S, D])
    y_ctx = dout("y_ctx", [2 * L, D])
    o_ckv = dout("o_ckv", [2, DEPTH, L, 256])
    o_kr = dout("o_kr", [2, DEPTH, L, 64])
    o_st = dout("o_st", [2, DEPTH, 2, 4, 128, 128])
    o_nk = dout("o_nk", [2, DEPTH, L, 512])
    o_nv = dout("o_nv", [2, DEPTH, L, 512])
    o_sk = dout("o_sk", [2, DEPTH, L, 128])
    o_sv = dout("o_sv", [2, DEPTH, L, 128])

    xT_d = dscr("xT_d", [16, 128, T], F32)
    hT_d = dscr("hT_d", [16, 128, T], BF16)
    zfm_d = dscr("zfm_d", [NSLOT, 128, T], BF16)
    ztm_d = dscr("ztm_d", [T, ZTM_COLS], BF16)
    oT_d = dscr("oT_d", [16, 128, T], BF16)

    P = Prog(nc)
    out_toks = []
    psb = [P.psum("psb%d" % i, [128, 512]) for i in range(8)]

    ident = P.sbuf("ident", [128, 128], F32, True)
    ones_bf = P.sbuf("ones_bf", [128, 128], BF16, True)
    ones_f = P.sbuf("ones_f", [128, 128], F32, True)
    modA = P.sbuf("modA", [128, DEPTH, 2, 16], F32, True)
    modB = P.sbuf("modB", [128, DEPTH, 2, 16], F32, True)
    modG = P.sbuf("modG", [128, DEPTH, 2, 16], F32, True)
    qn_s = P.sbuf("qn_s", [128, DEPTH, 4], F32, True)
    kvn_s = P.sbuf("kvn_s", [128, DEPTH, 2], F32, True)
    retn_s = P.sbuf("retn_s", [128, DEPTH, 4], F32, True)
    lg_s = P.sbuf("lg_s", [128, DEPTH, 8], F32, True)
    esink_s = P.sbuf("esink_s", [128, DEPTH, 8], F32, True)
    eps_t = P.sbuf("eps_t", [128, 1], F32, True)

    def mm_group(out_ap, pairs, reads, writes):
        def fn(e):
            n = len(pairs)
            ins = None
            for i, (l, r) in enumerate(pairs):
                ins = e.matmul(out_ap, l, r, start=(i == 0), stop=(i == n - 1))
            return ins
        return P.op("pe", fn, reads, writes)

    def rstd_from_ssq(ps_ap, out_ap, n_feat, reads, writes, tmpkey):
        P.op("act", lambda e: e.activation(out=out_ap, in_=ps_ap, func=AF.Sqrt, bias=eps_t[0:ps_ap.shape[0], :],
                                           scale=1.0 / n_feat), reads=reads, writes=writes)
        P.op("dve", lambda e: e.reciprocal(out=out_ap, in_=out_ap), reads=writes, writes=writes)

    P.begin()
    sc_t = P.sbuf("sc_t", [128, 16, 2], F32)
    npre_s = P.sbuf("npre_s", [128, DEPTH, 16], F32)
    npost_s = P.sbuf("npost_s", [128, DEPTH, 16], F32)
    bmod_s = P.sbuf("bmod_s", [128, DEPTH, 48], F32)
    dec_s = P.sbuf("dec_s", [128, DEPTH, 8], F32)
    mod_s = P.sbuf("mod_s", [128, 48, 2], F32)
    P.op("dve", lambda e: e.memset(ones_bf[:], 1.0), writes=["ones_bf"])
    P.op("dve", lambda e: e.memset(ones_f[:], 1.0), writes=["ones_f"])
    P.op("dve", lambda e: e.memset(eps_t[:], EPS), writes=["eps"])
    for nm, dst, src in (("ident", ident, ident_d), ("sc", sc_t, condT), ("npre", npre_s, npre), ("npost", npost_s, npost),
                         ("bmod", bmod_s, bmod), ("qn", qn_s, qn), ("kvn", kvn_s, kvn), ("retn", retn_s, retn),
                         ("dec", dec_s, decay), ("esink", esink_s, sink)):
        P.dma("sp", "ld_" + nm, (lambda e, dst=dst, src=src: e.dma_start(out=dst[:], in_=src)), writes=[nm])
    P.op("act", lambda e: e.activation(out=sc_t[:], in_=sc_t[:], func=AF.Silu), reads=["sc"], writes=["sc"])
    P.op("act", lambda e: e.activation(out=esink_s[:], in_=esink_s[:], func=AF.Exp), reads=["esink"], writes=["esink"])
    P.op("act", lambda e: e.activation(out=dec_s[:], in_=dec_s[:], func=AF.Exp, scale=-1.0), reads=["dec"], writes=["dec"])
    P.op("dve", lambda e: e.tensor_scalar(out=dec_s[:], in0=dec_s[:], scalar1=1.0, scalar2=None, op0=ALU.add),
         reads=["dec"], writes=["dec"])
    P.op("act", lambda e: e.activation(out=dec_s[:], in_=dec_s[:], func=AF.Ln), reads=["dec"], writes=["dec"])
    P.op("dve", lambda e: e.tensor_scalar(out=lg_s[:], in0=dec_s[:], scalar1=-1.0, scalar2=None, op0=ALU.mult),
         reads=["dec"], writes=["lg"])
    wm_ring = Ring(P, "wm", 3, [128, 16, 256], F32)
    for l in range(DEPTH):
        for wt in range(24):
            wb, wk = wm_ring.next()
            P.dma("sp", wk, (lambda e, wb=wb, l=l, wt=wt: e.dma_start(out=wb[:], in_=w_mod[l, wt])), writes=[wk])
            for half in range(2):
                ft = wt * 2 + half
                mm_group(psb[0][:, ft * 2:ft * 2 + 2],
                         [(wb[:, k, half * 128:(half + 1) * 128], sc_t[:, k, :]) for k in range(16)],
                         reads=[wk, "sc"], writes=["ps0"])
        for c in range(2):
            P.op("dve", lambda e, c=c, l=l: e.tensor_tensor(
                out=mod_s[:, :, c], in0=psb[0][:, 0:96].rearrange("p (f c) -> p f c", c=2)[:, :, c],
                in1=bmod_s[:, l, :], op=ALU.add), reads=["ps0", "bmod"], writes=["mod%d" % c])
        for c in range(2):
            P.op("dve", lambda e, c=c, l=l: e.scalar_tensor_tensor(
                out=modA[:, l, c, :], in0=mod_s[:, 16:32, c], scalar=1.0, in1=npre_s[:, l, :],
                op0=ALU.add, op1=ALU.mult), reads=["mod%d" % c, "npre"], writes=["modA"])
            P.op("dve", lambda e, c=c, l=l: e.tensor_copy(out=modB[:, l, c, :], in_=mod_s[:, 0:16, c]),
                 reads=["mod%d" % c], writes=["modB"])
            P.op("dve", lambda e, c=c, l=l: e.tensor_tensor(
                out=modG[:, l, c, :], in0=mod_s[:, 32:48, c], in1=npost_s[:, l, :], op=ALU.mult),
                reads=["mod%d" % c, "npost"], writes=["modG"])
    P.end()

    def finish_block(xblk, xkeys, b, lnext, R, bkey):
        cond = 0 if b < NLB else 1
        t0 = b * 512
        if lnext is not None:
            pss, psk = R["ps_ss"].next()
            for j in range(16):
                sq, sqk = R["sq"].next()
                P.op("act", lambda e, sq=sq, j=j: e.activation(out=sq[:], in_=xblk[:, j, :], func=AF.Square),
                     reads=[xkeys[j]], writes=[sqk])
                P.op("pe", lambda e, sq=sq, j=j, pss=pss: e.matmul(pss[:], ones_bf[:], sq[:], start=(j == 0), stop=(j == 15)),
                     reads=[sqk, "ones_bf"], writes=[psk])
            rs, rsk = R["rstd"].next()
            rstd_from_ssq(pss[:], rs[:], D, [psk], [rsk], None)
            hst, hk = R["hst"].next()
            for j in range(16):
                tmp, tk = R["tmp"].next()
                P.op("dve", lambda e, tmp=tmp, j=j, rs=rs: e.tensor_tensor(out=tmp[:], in0=xblk[:, j, :], in1=rs[:], op=ALU.mult),
                     reads=[xkeys[j], rsk], writes=[tk])
                P.op("act", lambda e, tmp=tmp, j=j, hst=hst: e.activation(
                    out=hst[:, j, :], in_=tmp[:], func=AF.Identity, scale=modA[:, lnext, cond, j:j + 1],
                    bias=modB[:, lnext, cond, j:j + 1]), reads=[tk, "modA", "modB"], writes=[hk + "_%d" % j])
            for q4 in range(4):
                P.dma("sp", "st_%s_%d" % (hk, q4), lambda e, hst=hst, q4=q4: e.dma_start(
                    out=hT_d[q4 * 4:q4 * 4 + 4, :, t0:t0 + 512].rearrange("j p t -> p j t"), in_=hst[:, q4 * 4:q4 * 4 + 4, :]),
                    reads=[hk + "_%d" % j for j in range(q4 * 4, q4 * 4 + 4)], writes=["hT_d%d_%d" % (b, q4)])
                P.dma("sp", "st_%s_%d" % (bkey, q4), lambda e, q4=q4: e.dma_start(
                    out=xT_d[q4 * 4:q4 * 4 + 4, :, t0:t0 + 512].rearrange("j p t -> p j t"), in_=xblk[:, q4 * 4:q4 * 4 + 4, :]),
                    reads=list(xkeys[q4 * 4:q4 * 4 + 4]), writes=["xT_d%d_%d" % (b, q4)])
        else:
            for tt in range(4):
                yo, yk = R["yo"].next()
                for q4 in range(4):
                    pst, ptk = R["ps_t"].next()
                    def tr(e, pst=pst, q4=q4, tt=tt):
                        ins = None
                        for jj in range(4):
                            j = q4 * 4 + jj
                            ins = e.transpose(out=pst[:, jj * 128:(jj + 1) * 128], in_=xblk[:, j, tt * 128:(tt + 1) * 128],
                                              identity=ident[:])
                        return ins
                    P.op("pe", tr, reads=list(xkeys[q4 * 4:q4 * 4 + 4]) + ["ident"], writes=[ptk])
                    eng = "act" if q4 % 2 == 0 else "dve"
                    if eng == "act":
                        P.op("act", lambda e, pst=pst, q4=q4, yo=yo: e.activation(out=yo[:, q4 * 512:(q4 + 1) * 512], in_=pst[:], func=AF.Copy),
                             reads=[ptk], writes=[yk + "_%d" % q4])
                    else:
                        P.op("dve", lambda e, pst=pst, q4=q4, yo=yo: e.tensor_copy(out=yo[:, q4 * 512:(q4 + 1) * 512], in_=pst[:]),
                             reads=[ptk], writes=[yk + "_%d" % q4])
                if b < NLB:
                    dst = y_lat[t0 + tt * 128:t0 + (tt + 1) * 128, :]
                else:
                    dst = y_ctx[tt * 128:(tt + 1) * 128, :]
                out_toks.append(P.dma("sp", "st_" + yk, lambda e, yo=yo, dst=dst: e.dma_start(out=dst, in_=yo[:]),
                                      reads=[yk + "_%d" % q for q in range(4)], writes=["y%d_%d" % (b, tt)]))

    def finish_rings(last):
        R = {}
        if not last:
            R["ps_ss"] = PsRing(psb, [6])
            R["sq"] = Ring(P, "fsq", 3, [128, 512], BF16)
            R["rstd"] = Ring(P, "frs", 2, [128, 512], F32)
            R["hst"] = Ring(P, "fhst", 1, [128, 16, 512], BF16)
            R["tmp"] = Ring(P, "ftmp", 3, [128, 512], F32)
        else:
            R["yo"] = Ring(P, "fyo", 2, [128, D], F32)
            R["ps_t"] = PsRing(psb, [6, 7])
        return R

    P.begin()
    R0 = finish_rings(False)
    xin_ring = Ring(P, "xin", 2, [128, 4, D], F32)
    xb_ring = Ring(P, "xblk", 1, [128, 16, 512], F32)
    ps_t0 = PsRing(psb, [0, 1, 2, 3])
    for b in range(NBLK):
        xin, xik = xin_ring.next()
        src = x_lat[b * 512:(b + 1) * 512, :] if b < NLB else x_ctx
        P.dma("sp", xik, lambda e, xin=xin, src=src: e.dma_start(out=xin[:], in_=src.rearrange("(t p) f -> p t f", p=128)),
              writes=[xik])
        xblk, xk = xb_ring.next()
        for j in range(16):
            pst, ptk = ps_t0.next()
            def tr(e, pst=pst, j=j, xin=xin):
                ins = None
                for tt in range(4):
                    ins = e.transpose(out=pst[:, tt * 128:(tt + 1) * 128], in_=xin[:, tt, j * 128:(j + 1) * 128], identity=ident[:])
                return ins
            P.op("pe", tr, reads=[xik, "ident"], writes=[ptk])
            if j % 2 == 0:
                P.op("act", lambda e, pst=pst, j=j, xblk=xblk: e.activation(out=xblk[:, j, :], in_=pst[:], func=AF.Copy),
                     reads=[ptk], writes=[xk + "_%d" % j])
            else:
                P.op("dve", lambda e, pst=pst, j=j, xblk=xblk: e.tensor_copy(out=xblk[:, j, :], in_=pst[:]),
                     reads=[ptk], writes=[xk + "_%d" % j])
        finish_block(xblk, [xk + "_%d" % j for j in range(16)], b, 0, R0, xk)
    P.end()

    GROUPS = []
    g0 = (NBLK + 1) // 2
    GROUPS.append(list(range(0, g0)))
    GROUPS.append(list(range(g0, NBLK)))

    for l in range(DEPTH):
        for grp in GROUPS:
            P.begin()
            G = len(grp) * 512
            tg0 = grp[0] * 512
            hT = P.sbuf("hT", [128, 16, G], BF16)
            cs_t = P.sbuf("cs_t", [128, G], F32)
            sn_t = P.sbuf("sn_t", [128, G], F32)
            for q4 in range(4):
                P.dma("sp", "ld_h%d" % q4, lambda e, q4=q4: e.dma_start(
                    out=hT[:, q4 * 4:(q4 + 1) * 4, :], in_=hT_d[q4 * 4:(q4 + 1) * 4, :, tg0:tg0 + G].rearrange("j p t -> p j t")),
                    writes=["hT"])
            P.dma("sp", "ld_cs", lambda e: e.dma_start(out=cs_t[:], in_=cosT[:, tg0:tg0 + G]), writes=["cs"])
            P.dma("sp", "ld_sn", lambda e: e.dma_start(out=sn_t[:], in_=sinT[:, tg0:tg0 + G]), writes=["sn"])
            wring = Ring(P, "wfm", 4, [128, 16, 128], BF16)
            stg = Ring(P, "stg", 3, [128, G], BF16)
            rt1 = Ring(P, "rt1", 2, [128, 512], F32)
            rt2 = Ring(P, "rt2", 2, [128, 512], F32)
            psr = PsRing(psb, [0, 1, 2, 3, 4, 5])
            ei = 0
            wi = 0
            while wi < len(FM):
                name, kind, _ = FM[wi]
                wb, wk = wring.next()
                P.dma("pool", wk, lambda e, wb=wb, wi=wi: e.dma_start(out=wb[:], in_=w_fm[l, wi]), writes=[wk])
                if kind == "rope":
                    wb2, wk2 = wring.next()
                    P.dma("pool", wk2, lambda e, wb2=wb2, wi=wi: e.dma_start(out=wb2[:], in_=w_fm[l, wi + 1]), writes=[wk2])
                st, sk_ = stg.next()
                for bi, b in enumerate(grp):
                    c0 = bi * 512
                    ps, pk = psr.next()
                    mm_group(ps[:], [(wb[:, k, :], hT[:, k, c0:c0 + 512]) for k in range(16)], [wk, "hT"], [pk])
                    wkey = sk_ + "_%d" % bi
                    if kind == "rope":
                        ps2, pk2 = psr.next()
                        mm_group(ps2[:], [(wb2[:, k, :], hT[:, k, c0:c0 + 512]) for k in range(16)], [wk2, "hT"], [pk2])
                        t1, t1k = rt1.next()
                        t2, t2k = rt2.next()
                        P.op("dve", lambda e, t1=t1, ps=ps, c0=c0: e.tensor_tensor(out=t1[:], in0=ps[:], in1=cs_t[:, c0:c0 + 512], op=ALU.mult),
                             reads=[pk, "cs"], writes=[t1k])
                        P.op("dve", lambda e, t2=t2, ps2=ps2, c0=c0: e.tensor_tensor(out=t2[:], in0=ps2[:], in1=sn_t[:, c0:c0 + 512], op=ALU.mult),
                             reads=[pk2, "sn"], writes=[t2k])
                        P.op("pool", lambda e, t1=t1, t2=t2, st=st, c0=c0: e.tensor_tensor(out=st[:, c0:c0 + 512], in0=t1[:], in1=t2[:], op=ALU.add),
                             reads=[t1k, t2k], writes=[wkey])
                    elif kind in ("silu", "sig"):
                        fn_ = AF.Silu if kind == "silu" else AF.Sigmoid
                        P.op("act", lambda e, ps=ps, st=st, c0=c0, fn_=fn_: e.activation(out=st[:, c0:c0 + 512], in_=ps[:], func=fn_),
                             reads=[pk], writes=[wkey])
                    else:
                        sc = (128.0 ** -0.5) if kind == "copys" else 1.0
                        if ei % 2 == 0:
                            P.op("dve", lambda e, ps=ps, st=st, c0=c0, sc=sc: e.tensor_scalar(
                                out=st[:, c0:c0 + 512], in0=ps[:], scalar1=sc, scalar2=None, op0=ALU.mult), reads=[pk], writes=[wkey])
                        else:
                            P.op("act", lambda e, ps=ps, st=st, c0=c0, sc=sc: e.activation(out=st[:, c0:c0 + 512], in_=ps[:], func=AF.Copy, scale=sc),
                                 reads=[pk], writes=[wkey])
                        ei += 1
                slot = FM_SLOT[name]
                P.dma("sp", "st_" + sk_, lambda e, st=st, slot=slot: e.dma_start(out=zfm_d[slot, :, tg0:tg0 + G], in_=st[:]),
                      reads=[sk_ + "_%d" % bi for bi in range(len(grp))], writes=["zfm%d" % slot])
                wi += 2 if kind == "rope" else 1
            wtm_ring = Ring(P, "wtm", 2, [128, 16, 512], BF16)
            tms = Ring(P, "tms", 2, [128, 4, 512], BF16)
            of_ring = Ring(P, "ofr", 3, [128, 512], F32)
            sm_ring = Ring(P, "smr", 4, [128, 2], F32)
            has_ctx = NLB in grp
            kvrow = None
            if has_ctx:
                kvrow = P.sbuf("kvrow", [128, 256], F32)
                P.dma("sp", "ld_kvrow", lambda e: e.dma_start(out=kvrow[:], in_=kvn_row[:, l, :]), writes=["kvrow"])
            for (gname, gcols, gwho) in TMG:
                if gwho == "ctx" and not has_ctx:
                    continue
                if gname in SKIP:
                    continue
                co, ncol = TM_OFF[gname]
                wb, wk = wtm_ring.next()
                P.dma("pool", wk, lambda e, wb=wb, co=co, ncol=ncol: e.dma_start(out=wb[:, :, 0:ncol], in_=w_tm[l, :, :, co:co + ncol]),
                      writes=[wk])
                blocks = grp if gwho == "all" else [NLB]
                for b in blocks:
                    bi = grp.index(b)
                    is_ctx = (b == NLB)
                    ts_, tsk = tms.next()
                    for tt in range(4):
                        c0 = bi * 512 + tt * 128
                        ps, pk = psr.next()
                        mm_group(ps[:, 0:ncol], [(hT[:, k, c0:c0 + 128], wb[:, k, 0:ncol]) for k in range(16)], [wk, "hT"], [pk])
                        ctx_out = is_ctx and gname != "rk" and gname != "rv" and "ctxout" not in SKIP and (gname + "_out") not in SKIP
                        if gwho == "all" and not ctx_out:
                            sc = (128.0 ** -0.5) if gname == "rk" else 1.0
                            if tt % 2 == 0:
                                P.op("dve", lambda e, ps=ps, ts_=ts_, tt=tt, sc=sc, ncol=ncol: e.tensor_scalar(
                                    out=ts_[:, tt, 0:ncol], in0=ps[:, 0:ncol], scalar1=sc, scalar2=None, op0=ALU.mult),
                                    reads=[pk], writes=[tsk + "_%d" % tt])
                            else:
                                P.op("act", lambda e, ps=ps, ts_=ts_, tt=tt, sc=sc, ncol=ncol: e.activation(
                                    out=ts_[:, tt, 0:ncol], in_=ps[:, 0:ncol], func=AF.Copy, scale=sc), reads=[pk], writes=[tsk + "_%d" % tt])
                        if ctx_out:
                            cb, r0 = tt // 2, (tt % 2) * 128
                            if gname == "o1":
                                of, ofk = of_ring.next()
                                sm, smk = sm_ring.next()
                                P.op("act", lambda e, ps=ps, of=of, sm=sm: e.activation(out=of[:, 0:256], in_=ps[:, 0:256], func=AF.Square,
                                                                                        accum_out=sm[:, 0:1]), reads=[pk], writes=[ofk, smk])
                                P.op("act", lambda e, sm=sm: e.activation(out=sm[:, 1:2], in_=sm[:, 0:1], func=AF.Sqrt, bias=eps_t[:, :], scale=1.0 / 256),
                                     reads=[smk], writes=[smk])
                                P.op("dve", lambda e, sm=sm: e.reciprocal(out=sm[:, 1:2], in_=sm[:, 1:2]), reads=[smk], writes=[smk])
                                P.op("dve", lambda e, ps=ps, of=of, sm=sm: e.scalar_tensor_tensor(
                                    out=of[:, 0:256], in0=ps[:, 0:256], scalar=sm[:, 1:2], in1=kvrow[:], op0=ALU.mult, op1=ALU.mult),
                                    reads=[pk, smk, "kvrow", ofk], writes=[ofk])
                                P.op("dve", lambda e, ps=ps, of=of: e.tensor_copy(out=of[:, 256:448], in_=ps[:, 256:448]),
                                     reads=[pk, ofk], writes=[ofk])
                                for (dst, a, w_) in ((o_ckv, 0, 256), (o_kr, 256, 64), (o_sk, 320, 128)):
                                    out_toks.append(P.dma("sp", "st_" + ofk, lambda e, of=of, dst=dst, a=a, w_=w_, cb=cb, r0=r0: e.dma_start(
                                        out=dst[cb, l, r0:r0 + 128, :], in_=of[:, a:a + w_]), reads=[ofk], writes=["o_%d_%d" % (a, tt)]))
                            else:
                                dst = dict(nk=o_nk, nv=o_nv, sv=o_sv)[gname]
                                of, ofk = of_ring.next()
                                P.op("act" if tt % 2 == 0 else "dve",
                                     (lambda e, ps=ps, of=of, ncol=ncol: e.activation(out=of[:, 0:ncol], in_=ps[:, 0:ncol], func=AF.Copy)) if tt % 2 == 0 else
                                     (lambda e, ps=ps, of=of, ncol=ncol: e.tensor_copy(out=of[:, 0:ncol], in_=ps[:, 0:ncol])),
                                     reads=[pk], writes=[ofk])
                                if gwho == "all":
                                    P.op("pool", lambda e, of=of, ts_=ts_, tt=tt, ncol=ncol: e.tensor_copy(out=ts_[:, tt, 0:ncol], in_=of[:, 0:ncol]),
                                         reads=[ofk], writes=[tsk + "_%d" % tt])
                                out_toks.append(P.dma("sp", "st_" + ofk, lambda e, of=of, dst=dst, ncol=ncol, cb=cb, r0=r0: e.dma_start(
                                    out=dst[cb, l, r0:r0 + 128, :], in_=of[:, 0:ncol]), reads=[ofk], writes=["o_%s_%d" % (gname, tt)]))
                    if gwho == "all":
                        zo = ZTM[gname]
                        P.dma("sp", "st_" + tsk, lambda e, ts_=ts_, b=b, zo=zo, ncol=ncol: e.dma_start(
                            out=ztm_d[b * 512:(b + 1) * 512, zo:zo + ncol].rearrange("(t p) c -> p t c", p=128), in_=ts_[:, :, 0:ncol]),
                            reads=[tsk + "_%d" % tt for tt in range(4)], writes=["ztm_%s_%d" % (gname, b)])
            P.end()

        SEQS = [("lat", 0, S)] + [("ctx%d" % i, S + i * L, L) for i in range(2)]

        def attention(tag, nq, q_parts, key_tiles, m_out, scale, out_ps, sum_ps, R, po=0):
            (ops, opk), (sps, spk) = out_ps, sum_ps
            nk = len(key_tiles)
            LA = 2
            pend = {}
            for kk in range(nk + LA):
                if kk < nk:
                    (kparts, vl, bias, rk_) = key_tiles[kk]
                    ps, pk = R["ps_s"].next()
                    mm_group(ps[:, 0:nq], [(kp, qp) for kp, (qp, _) in zip(kparts, q_parts)],
                             list(rk_) + [qk for _, qk in q_parts], [pk])
                    ex, exk = R["ex"].next()
                    if bias is None:
                        P.op("act", lambda e, ps=ps, ex=ex: e.activation(out=ex[:, 0:nq], in_=ps[:, 0:nq], func=AF.Exp, scale=scale),
                             reads=[pk], writes=[exk])
                    else:
                        bt, btk = bias
                        tb, tbk = R["tb"].next()
                        P.op("dve", lambda e, ps=ps, tb=tb, bt=bt: e.scalar_tensor_tensor(
                            out=tb[:, 0:nq], in0=ps[:, 0:nq], scalar=scale, in1=bt, op0=ALU.mult, op1=ALU.add),
                            reads=[pk, btk], writes=[tbk])
                        P.op("act", lambda e, tb=tb, ex=ex: e.activation(out=ex[:, 0:nq], in_=tb[:, 0:nq], func=AF.Exp),
                             reads=[tbk], writes=[exk])
                    pend[kk] = (ex, exk, vl, rk_)
                ki = kk - LA
                if ki >= 0:
                    (ex, exk, vl, rk_) = pend.pop(ki)
                    def pv(e, ex=ex, vl=vl, ki=ki):
                        e.matmul(ops[po:po + m_out, 0:nq], vl, ex[:, 0:nq], start=(ki == 0), stop=(ki == nk - 1))
                        return e.matmul(sps[po:po + m_out, 0:nq], ones_bf[:, 0:m_out], ex[:, 0:nq], start=(ki == 0), stop=(ki == nk - 1))
                    P.op("pe", pv, reads=[exk, "ones_bf"] + list(rk_), writes=[opk, spk])

        def finish_head(tag, nq, out_ps, sum_ps, gp_ap, gpk, dst_ap, R, esink_ap=None, rows=128, stkey=None):
            (ops, opk), (sps, spk) = out_ps, sum_ps
            rc, rck = R["rc"].next()
            if esink_ap is not None:
                for (p0, ea) in esink_ap:
                    P.op("dve", lambda e, rc=rc, p0=p0, ea=ea: e.tensor_scalar(out=rc[p0:p0 + 64, 0:nq], in0=sps[p0:p0 + 64, 0:nq],
                                                                             scalar1=ea, scalar2=None, op0=ALU.add),
                         reads=[spk, "esink"], writes=[rck])
                P.op("dve", lambda e, rc=rc: e.reciprocal(out=rc[0:rows, 0:nq], in_=rc[0:rows, 0:nq]), reads=[rck], writes=[rck])
            else:
                P.op("dve", lambda e, rc=rc: e.reciprocal(out=rc[0:rows, 0:nq], in_=sps[0:rows, 0:nq]), reads=[spk], writes=[rck])
            P.op("dve", lambda e, rc=rc: e.tensor_tensor(out=rc[0:rows, 0:nq], in0=ops[0:rows, 0:nq], in1=rc[0:rows, 0:nq], op=ALU.mult),
                 reads=[opk, rck], writes=[rck])
            ob, obk = R["ob"].next()
            P.op("pool", lambda e, rc=rc, ob=ob: e.tensor_tensor(out=ob[0:rows, 0:nq], in0=rc[0:rows, 0:nq], in1=gp_ap, op=ALU.mult),
                 reads=[rck, gpk], writes=[obk])
            P.dma("pool", "st_" + obk, lambda e, ob=ob: e.dma_start(out=dst_ap, in_=ob[0:rows, 0:nq]), reads=[obk], writes=[stkey])

        def load_gp(R, slot, t0, nq):
            gp, gpk = R["gp"].next()
            P.dma("sp", gpk, lambda e, gp=gp: e.dma_start(out=gp[:, 0:nq], in_=zfm_d[slot, :, t0:t0 + nq]), writes=[gpk])
            return gp, gpk

        def att_rings(P, with_tb):
            R = dict(ps_s=PsRing(psb, [0, 1, 2]), ex=Ring(P, "ex", 4, [128, 512], BF16), rc=Ring(P, "rc", 2, [128, 512], F32),
                     ob=Ring(P, "ob", 2, [128, 512], BF16), gp=Ring(P, "gp", 2, [128, 512], BF16),
                     ps_o=PsRing(psb, [3, 4]), ps_d=PsRing(psb, [5, 6]))
            if with_tb:
                R["tb"] = Ring(P, "tb", 3, [128, 512], F32)
            return R

        for (sname, s0, slen) in ([] if "mla" in SKIP else SEQS):
            is_lat = sname == "lat"
            nkey = slen + (PAST if is_lat else 0)
            nkt = nkey // 128
            P.begin()
            R = att_rings(P, False)
            wq = P.sbuf("wq", [128, 4, 1024], BF16)
            wkv = P.sbuf("wkv", [128, 2, 1024], BF16)
            ckvT = P.sbuf("ckvT", [128, 2, nkey], BF16)
            krT = P.sbuf("krT", [64, nkey], BF16)
            knT = P.sbuf("knT", [128, nkey], BF16)
            vall = P.sbuf("vall", [128, nkt, 512], BF16)
            P.dma("pool", "ld_wq", lambda e: e.dma_start(out=wq[:], in_=w_qup[l]), writes=["wq"])
            P.dma("pool", "ld_wkv", lambda e: e.dma_start(out=wkv[:], in_=w_kvup[l]), writes=["wkv"])
            P.dma("sp", "ld_kr", lambda e: e.dma_start(out=krT[:, 0:slen], in_=zfm_d[FM_SLOT["kr"], 0:64, s0:s0 + slen]), writes=["krT"])
            kva_ring = Ring(P, "kva", 2, [128, 2, 512], BF16)
            sq_ring = Ring(P, "msq", 2, [128, 4, 512], BF16)
            rs_ring = Ring(P, "mrs", 2, [128, 512], F32)
            psm = PsRing(psb, [7])
            nch = (slen + 511) // 512
            for ch in range(nch):
                n = min(512, slen - ch * 512)
                t0 = s0 + ch * 512
                kv, kvk = kva_ring.next()
                P.dma("sp", kvk, lambda e, kv=kv, t0=t0, n=n: e.dma_start(
                    out=kv[:, :, 0:n], in_=zfm_d[FM_SLOT["kva0"]:FM_SLOT["kva0"] + 2, :, t0:t0 + n].rearrange("j p t -> p j t")), writes=[kvk])
                sq, sqk = sq_ring.next()
                P.op("act", lambda e, kv=kv, sq=sq, n=n: e.activation(out=sq[:, 0:2, 0:n], in_=kv[:, :, 0:n], func=AF.Square), reads=[kvk], writes=[sqk])
                ps, pk = psm.next()
                mm_group(ps[:, 0:n], [(ones_bf[:], sq[:, j, 0:n]) for j in range(2)], [sqk, "ones_bf"], [pk])
                rs, rsk = rs_ring.next()
                rstd_from_ssq(ps[:, 0:n], rs[:, 0:n], 256, [pk], [rsk], None)
                for j in range(2):
                    P.op("dve", lambda e, kv=kv, rs=rs, j=j, n=n, ch=ch: e.scalar_tensor_tensor(
                        out=ckvT[:, j, ch * 512:ch * 512 + n], in0=kv[:, j, 0:n], scalar=kvn_s[:, l, j:j + 1], in1=rs[:, 0:n],
                        op0=ALU.mult, op1=ALU.mult), reads=[kvk, rsk, "kvn"], writes=["ckvT_%d" % ch])
            ckv_keys = ["ckvT_%d" % ch for ch in range(nch)]
            if is_lat:
                cc = P.sbuf("cc", [128, 4, 256], F32)
                ck = P.sbuf("ck", [128, 4, 64], F32)
                P.dma("sp", "ld_cc", lambda e: e.dma_start(out=cc[:], in_=c_ckv[l].rearrange("(t p) f -> p t f", p=128)), writes=["cc"])
                P.dma("sp", "ld_ck", lambda e: e.dma_start(out=ck[:], in_=c_kr[l].rearrange("(t p) f -> p t f", p=128)), writes=["ck"])
                for j in range(2):
                    ps, pk = psm.next()
                    def tr(e, ps=ps, j=j):
                        ins = None
                        for tt in range(4):
                            ins = e.transpose(out=ps[:, tt * 128:(tt + 1) * 128], in_=cc[:, tt, j * 128:(j + 1) * 128], identity=ident[:])
                        return ins
                    P.op("pe", tr, reads=["cc", "ident"], writes=[pk])
                    P.op("dve", lambda e, ps=ps, j=j: e.tensor_copy(out=ckvT[:, j, slen:slen + 512], in_=ps[:]), reads=[pk], writes=["ckvT_c%d" % j])
                ckv_keys += ["ckvT_c0", "ckvT_c1"]
                ps, pk = psm.next()
                def tr2(e, ps=ps):
                    ins = None
                    for tt in range(4):
                        ins = e.transpose(out=ps[0:64, tt * 128:(tt + 1) * 128], in_=ck[:, tt, :], identity=ident[:])
                    return ins
                P.op("pe", tr2, reads=["ck", "ident"], writes=[pk])
                P.op("dve", lambda e, ps=ps: e.tensor_copy(out=krT[:, slen:slen + 512], in_=ps[0:64, :]), reads=[pk, "krT"], writes=["krT"])
            for kt in range(nkt):
                ps, pk = psm.next()
                mm_group(ps[:], [(ckvT[:, j, kt * 128:(kt + 1) * 128], wkv[:, j, 512:1024]) for j in range(2)], ckv_keys + ["wkv"], [pk])
                if kt % 2 == 0:
                    P.op("act", lambda e, ps=ps, kt=kt: e.activation(out=vall[:, kt, :], in_=ps[:], func=AF.Copy), reads=[pk], writes=["vall_%d" % kt])
                else:
                    P.op("dve", lambda e, ps=ps, kt=kt: e.tensor_copy(out=vall[:, kt, :], in_=ps[:]), reads=[pk], writes=["vall_%d" % kt])
            vkeys = ["vall_%d" % kt for kt in range(nkt)]
            qa_ring = Ring(P, "qa", 2, [128, 4, 512], BF16)
            cq_ring = Ring(P, "cq", 2, [128, 4, 512], BF16)
            qn_ring = Ring(P, "qnp", 2, [128, 512], BF16)
            qr_ring = Ring(P, "qrp", 2, [64, 512], BF16)
            qt_ring = Ring(P, "qtp", 2, [64, 512], F32)
            cs2 = Ring(P, "cs2", 2, [64, 512], F32)
            sn2 = Ring(P, "sn2", 2, [64, 512], F32)
            nqb = (slen + 511) // 512
            for h in range(4):
                for ch in range((nkey + 511) // 512):
                    n = min(512, nkey - ch * 512)
                    ps, pk = psm.next()
                    mm_group(ps[:, 0:n], [(wkv[:, j, h * 128:(h + 1) * 128], ckvT[:, j, ch * 512:ch * 512 + n]) for j in range(2)],
                             ckv_keys + ["wkv"], [pk])
                    P.op("dve", lambda e, ps=ps, ch=ch, n=n: e.tensor_copy(out=knT[:, ch * 512:ch * 512 + n], in_=ps[:, 0:n]),
                         reads=[pk], writes=["knT"])
                for qb in range(nqb):
                    nq = min(512, slen - qb * 512)
                    t0 = s0 + qb * 512
                    qa, qak = qa_ring.next()
                    P.dma("sp", qak, lambda e, qa=qa, t0=t0, nq=nq: e.dma_start(
                        out=qa[:, :, 0:nq], in_=zfm_d[0:4, :, t0:t0 + nq].rearrange("j p t -> p j t")), writes=[qak])
                    sq, sqk = sq_ring.next()
                    P.op("act", lambda e, qa=qa, sq=sq, nq=nq: e.activation(out=sq[:, :, 0:nq], in_=qa[:, :, 0:nq], func=AF.Square), reads=[qak], writes=[sqk])
                    ps, pk = psm.next()
                    mm_group(ps[:, 0:nq], [(ones_bf[:], sq[:, j, 0:nq]) for j in range(4)], [sqk, "ones_bf"], [pk])
                    rs, rsk = rs_ring.next()
                    rstd_from_ssq(ps[:, 0:nq], rs[:, 0:nq], 512, [pk], [rsk], None)
                    cq, cqk = cq_ring.next()
                    for j in range(4):
                        P.op("dve", lambda e, qa=qa, cq=cq, rs=rs, j=j, nq=nq: e.scalar_tensor_tensor(
                            out=cq[:, j, 0:nq], in0=qa[:, j, 0:nq], scalar=qn_s[:, l, j:j + 1], in1=rs[:, 0:nq], op0=ALU.mult, op1=ALU.mult),
                            reads=[qak, rsk, "qn"], writes=[cqk])
                    c0 = h * 256
                    ps, pk = psm.next()
                    mm_group(ps[:, 0:nq], [(wq[:, j, c0:c0 + 128], cq[:, j, 0:nq]) for j in range(4)], [cqk, "wq"], [pk])
                    qnp, qnk = qn_ring.next()
                    P.op("act", lambda e, ps=ps, qnp=qnp, nq=nq: e.activation(out=qnp[:, 0:nq], in_=ps[:, 0:nq], func=AF.Copy), reads=[pk], writes=[qnk])
                    ps, pk = psm.next()
                    mm_group(ps[0:64, 0:nq], [(wq[:, j, c0 + 128:c0 + 192], cq[:, j, 0:nq]) for j in range(4)], [cqk, "wq"], [pk])
                    qrp, qrk = qr_ring.next()
                    if is_lat:
                        cs, csk = cs2.next()
                        sn, snk = sn2.next()
                        P.dma("sp", csk, lambda e, cs=cs, t0=t0, nq=nq: e.dma_start(out=cs[:, 0:nq], in_=cosT[0:64, t0:t0 + nq]), writes=[csk])
                        P.dma("sp", snk, lambda e, sn=sn, t0=t0, nq=nq: e.dma_start(out=sn[:, 0:nq], in_=sinT[0:64, t0:t0 + nq]), writes=[snk])
                        qt, qtk = qt_ring.next()
                        P.op("dve", lambda e, ps=ps, qt=qt, cs=cs, nq=nq: e.tensor_tensor(out=qt[:, 0:nq], in0=ps[0:64, 0:nq], in1=cs[:, 0:nq], op=ALU.mult),
                             reads=[pk, csk], writes=[qtk])
                        ps, pk = psm.next()
                        mm_group(ps[0:64, 0:nq], [(wq[:, j, c0 + 192:c0 + 256], cq[:, j, 0:nq]) for j in range(4)], [cqk, "wq"], [pk])
                        qt2, qt2k = qt_ring.next()
                        P.op("dve", lambda e, ps=ps, qt2=qt2, sn=sn, nq=nq: e.tensor_tensor(out=qt2[:, 0:nq], in0=ps[0:64, 0:nq], in1=sn[:, 0:nq], op=ALU.mult),
                             reads=[pk, snk], writes=[qt2k])
                        P.op("pool", lambda e, qt=qt, qt2=qt2, qrp=qrp, nq=nq: e.tensor_tensor(out=qrp[:, 0:nq], in0=qt[:, 0:nq], in1=qt2[:, 0:nq], op=ALU.add),
                             reads=[qtk, qt2k], writes=[qrk])
                    else:
                        P.op("dve", lambda e, ps=ps, qrp=qrp, nq=nq: e.tensor_copy(out=qrp[:, 0:nq], in_=ps[0:64, 0:nq]), reads=[pk], writes=[qrk])
                    ops_ = R["ps_o"].next()
                    sps_ = R["ps_d"].next()
                    kts = []
                    for kt in range(nkt):
                        kts.append(([knT[:, kt * 128:(kt + 1) * 128], krT[:, kt * 128:(kt + 1) * 128]],
                                    vall[:, kt, h * 128:(h + 1) * 128], None, ["knT", "krT", "vall_%d" % kt]))
                    attention("mla", nq, [(qnp[:, 0:nq], qnk), (qrp[:, 0:nq], qrk)], kts, 128, 192.0 ** -0.5, ops_, sps_, R)
                    gp, gpk = load_gp(R, FM_SLOT["gp%d" % h], t0, nq)
                    finish_head("mla", nq, ops_, sps_, gp[:, 0:nq], gpk, oT_d[h, :, t0:t0 + nq], R, stkey="oT_%d_%d" % (h, t0))
            P.end()

        for (sname, s0, slen) in ([] if "nat" in SKIP else SEQS):
            is_lat = sname == "lat"
            nkey = slen + (PAST if is_lat else 0)
            nkt = nkey // 128
            P.begin()
            R = att_rings(P, is_lat)
            kT = P.sbuf("nkT", [128, nkey], BF16)
            vt = P.sbuf("nvt", [128, nkt, 128], BF16)
            q_ring = Ring(P, "nq", 2, [128, 512], BF16)
            psm = PsRing(psb, [7])
            if is_lat:
                cK = P.sbuf("cK", [128, 4, 512], F32)
                P.dma("sp", "ld_cK", lambda e: e.dma_start(out=cK[:], in_=c_nk[l].rearrange("(t p) f -> p t f", p=128)), writes=["cK"])
                b_ring = Ring(P, "nb", 8, [128, 512], F32)
            nqb = (slen + 511) // 512
            for h in range(4):
                P.dma("sp", "ld_nkT", lambda e, h=h: e.dma_start(out=kT[:, 0:slen], in_=zfm_d[FM_SLOT["nk%d" % h], :, s0:s0 + slen]), writes=["nkT"])
                for c4 in range(0, slen // 128, 4):
                    n4 = min(4, slen // 128 - c4)
                    P.dma("sp", "ld_nvt%d" % ((c4 // 4) % 2), lambda e, h=h, c4=c4, n4=n4: e.dma_start(
                        out=vt[:, c4:c4 + n4, :],
                        in_=ztm_d[s0 + c4 * 128:s0 + (c4 + n4) * 128, ZTM["nv"] + h * 128:ZTM["nv"] + (h + 1) * 128].rearrange("(t p) c -> p t c", p=128)),
                        reads=["nvt"], writes=["nvt"])
                if is_lat:
                    P.dma("pool", "ld_nvc", lambda e, h=h: e.dma_start(
                        out=vt[:, slen // 128:nkt, :], in_=c_nv[l, :, h * 128:(h + 1) * 128].rearrange("(t p) c -> p t c", p=128)),
                        reads=["nvt"], writes=["nvt"])
                    ps, pk = psm.next()
                    def tr(e, ps=ps, h=h):
                        ins = None
                        for tt in range(4):
                            ins = e.transpose(out=ps[:, tt * 128:(tt + 1) * 128], in_=cK[:, tt, h * 128:(h + 1) * 128], identity=ident[:])
                        return ins
                    P.op("pe", tr, reads=["cK", "ident"], writes=[pk])
                    P.op("dve", lambda e, ps=ps: e.tensor_copy(out=kT[:, slen:slen + 512], in_=ps[:]), reads=[pk, "nkT"], writes=["nkT"])
                for qb in range(nqb):
                    nq = min(512, slen - qb * 512)
                    t0 = s0 + qb * 512
                    q, qk = q_ring.next()
                    P.dma("sp", qk, lambda e, q=q, h=h, t0=t0, nq=nq: e.dma_start(out=q[:, 0:nq], in_=zfm_d[FM_SLOT["nq%d" % h], :, t0:t0 + nq]), writes=[qk])
                    kts = []
                    if is_lat:
                        lo = max(0, 8 * qb - 4)
                        hi = min(ROWS, 8 * qb + 12)
                        cls = 0 if qb == 0 else (2 if qb == NQB - 1 else 1)
                        for ti in range((hi - lo) // 2):
                            k0 = (lo + 2 * ti) * 64
                            bt, btk = b_ring.next()
                            P.dma("sp", btk, lambda e, bt=bt, cls=cls, ti=ti, h=h: e.dma_start(out=bt[:], in_=nat_bias[l, cls, ti, h]), writes=[btk])
                            kts.append(([kT[:, k0:k0 + 128]], vt[:, k0 // 128, :], (bt[:], btk), ["nkT", "nvt"]))
                        for ti in range(4):
                            k0 = slen + ti * 128
                            kts.append(([kT[:, k0:k0 + 128]], vt[:, k0 // 128, :], None, ["nkT", "nvt"]))
                    else:
                        for kt in range(nkt):
                            kts.append(([kT[:, kt * 128:(kt + 1) * 128]], vt[:, kt, :], None, ["nkT", "nvt"]))
                    ops_ = R["ps_o"].next()
                    sps_ = R["ps_d"].next()
                    attention("nat", nq, [(q[:, 0:nq], qk)], kts, 128, 128.0 ** -0.5, ops_, sps_, R)
                    gp, gpk = load_gp(R, FM_SLOT["gp%d" % (8 + h)], t0, nq)
                    finish_head("nat", nq, ops_, sps_, gp[:, 0:nq], gpk, oT_d[8 + h, :, t0:t0 + nq], R, stkey="oT_%d_%d" % (8 + h, t0))
            P.end()

        for (sname, s0, slen) in ([] if "swa" in SKIP else SEQS):
            is_lat = sname == "lat"
            nkey = slen + (PAST if is_lat else 0)
            nkt = nkey // 128
            P.begin()
            R = att_rings(P, is_lat)
            kT = P.sbuf("skT", [128, nkey], BF16)
            vt = P.sbuf("svt", [128, nkt, 64], BF16)
            q_ring = Ring(P, "sq", 2, [128, 512], BF16)
            psm = PsRing(psb, [7])
            if is_lat:
                cK = P.sbuf("scK", [128, 4, 2, 2, 64], F32)
                for dd in range(2):
                    for g_ in range(2):
                        P.dma("sp", "ld_scK%d" % dd, lambda e, dd=dd, g_=g_: e.dma_start(
                            out=cK[:, :, g_, dd, :], in_=c_sk[l, :, g_ * 64:(g_ + 1) * 64].rearrange("(t p) c -> p t c", p=128)),
                            reads=["scK%d" % dd], writes=["scK%d" % dd])
                mk = P.sbuf("smask", [128, 6, 512], F32)
                P.dma("sp", "ld_smask", lambda e: e.dma_start(out=mk[:], in_=swa_mask.rearrange("o p q -> p o q")), writes=["smask"])
            nqb = (slen + 511) // 512
            for g in range(2):
                P.dma("sp", "ld_skT", lambda e, g=g: e.dma_start(out=kT[:, 0:slen], in_=zfm_d[FM_SLOT["sk%d" % g], :, s0:s0 + slen]), writes=["skT"])
                for c4 in range(0, slen // 128, 4):
                    n4 = min(4, slen // 128 - c4)
                    P.dma("sp", "ld_svt%d" % ((c4 // 4) % 2), lambda e, g=g, c4=c4, n4=n4: e.dma_start(
                        out=vt[:, c4:c4 + n4, :],
                        in_=ztm_d[s0 + c4 * 128:s0 + (c4 + n4) * 128, ZTM["sv"] + g * 64:ZTM["sv"] + (g + 1) * 64].rearrange("(t p) c -> p t c", p=128)),
                        reads=["svt"], writes=["svt"])
                if is_lat:
                    P.dma("pool", "ld_svc", lambda e, g=g: e.dma_start(
                        out=vt[:, slen // 128:nkt, :], in_=c_sv[l, :, g * 64:(g + 1) * 64].rearrange("(t p) c -> p t c", p=128)),
                        reads=["svt"], writes=["svt"])
                    ps, pk = psm.next()
                    def tr(e, ps=ps, g=g):
                        ins = None
                        for tt in range(4):
                            ins = e.transpose(out=ps[:, tt * 128:(tt + 1) * 128], in_=cK[:, tt, g].rearrange("p d c -> p (d c)"), identity=ident[:])
                        return ins
                    P.op("pe", tr, reads=["scK0", "scK1", "ident"], writes=[pk])
                    P.op("dve", lambda e, ps=ps: e.tensor_copy(out=kT[:, slen:slen + 512], in_=ps[:]), reads=[pk, "skT"], writes=["skT"])
                for ti2 in range(2):
                    tq = g * 2 + ti2
                    for qb in range(nqb):
                        nq = min(512, slen - qb * 512)
                        t0 = s0 + qb * 512
                        q, qk = q_ring.next()
                        P.dma("sp", qk, lambda e, q=q, tq=tq, t0=t0, nq=nq: e.dma_start(out=q[:, 0:nq], in_=zfm_d[FM_SLOT["sq%d" % tq], :, t0:t0 + nq]), writes=[qk])
                        ops_ = R["ps_o"].next()
                        sps_ = R["ps_d"].next()
                        for hh in range(2):
                            p0 = hh * 64
                            kts = []
                            if is_lat:
                                for o in range(-1, 5):
                                    kb = 4 * qb + o
                                    if kb < 0 or kb >= slen // 128:
                                        continue
                                    kts.append(([kT[p0:p0 + 64, kb * 128:(kb + 1) * 128]], vt[:, kb, :], (mk[:, o + 1, :], "smask"), ["skT", "svt"]))
                                for ti in range(4):
                                    kb = slen // 128 + ti
                                    kts.append(([kT[p0:p0 + 64, kb * 128:(kb + 1) * 128]], vt[:, kb, :], None, ["skT", "svt"]))
                            else:
                                for kt in range(nkt):
                                    kts.append(([kT[p0:p0 + 64, kt * 128:(kt + 1) * 128]], vt[:, kt, :], None, ["skT", "svt"]))
                            attention("swa", nq, [(q[p0:p0 + 64, 0:nq], qk)], kts, 64, 64.0 ** -0.5, ops_, sps_, R, po=p0)
                        gp, gpk = load_gp(R, FM_SLOT["gp%d" % (12 + tq)], t0, nq)
                        es = [(0, esink_s[0:64, l, 2 * tq:2 * tq + 1]), (64, esink_s[64:128, l, 2 * tq + 1:2 * tq + 2])]
                        finish_head("swa", nq, ops_, sps_, gp[:, 0:nq], gpk, oT_d[12 + tq, :, t0:t0 + nq], R, esink_ap=es,
                                    stkey="oT_%d_%d" % (12 + tq, t0))
            P.end()

        P.begin()
        rtab = P.sbuf("rtab", [128, 4, 128], F32)
        rcol = P.sbuf("rcol", [128, 2], F32)
        P.dma("sp", "ld_rtab", lambda e: e.dma_start(out=rtab[:], in_=ret_tab.rearrange("a p q -> p a q")), writes=["rtab"])
        P.dma("sp", "ld_rcol", lambda e: e.dma_start(out=rcol[:], in_=ret_col), writes=["rcol"])
        maskT = P.sbuf("maskT", [128, 4, 128], F32)
        qdec = P.sbuf("qdec", [128, 4, 2, 128], F32)
        kdec = P.sbuf("kdec", [128, 4, 2], F32)
        cdec = P.sbuf("cdec", [128, 8], F32)
        rtmp = P.sbuf("rtmp", [128, 128], F32)
        for h in range(4):
            lf = lg_s[:, l, h:h + 1]
            lb = lg_s[:, l, 4 + h:5 + h]
            P.op("dve", lambda e, lf=lf: e.tensor_scalar(out=rtmp[:], in0=rtab[:, 0, :], scalar1=lf, scalar2=None, op0=ALU.mult),
                 reads=["rtab", "lg"], writes=["rtmp"])
            P.op("dve", lambda e, lb=lb: e.scalar_tensor_tensor(out=rtmp[:], in0=rtab[:, 1, :], scalar=lb, in1=rtmp[:], op0=ALU.mult, op1=ALU.add),
                 reads=["rtab", "lg", "rtmp"], writes=["rtmp"])
            P.op("act", lambda e, h=h: e.activation(out=maskT[:, h, :], in_=rtmp[:], func=AF.Exp), reads=["rtmp"], writes=["maskT"])
            P.op("act", lambda e, h=h, lf=lf: e.activation(out=qdec[:, h, 0, :], in_=rtab[:, 2, :], func=AF.Exp, scale=lf), reads=["rtab", "lg"], writes=["qdec"])
            P.op("act", lambda e, h=h, lb=lb: e.activation(out=qdec[:, h, 1, :], in_=rtab[:, 3, :], func=AF.Exp, scale=lb), reads=["rtab", "lg"], writes=["qdec"])
            P.op("act", lambda e, h=h, lf=lf: e.activation(out=kdec[:, h, 0:1], in_=rcol[:, 0:1], func=AF.Exp, scale=lf), reads=["rcol", "lg"], writes=["kdec"])
            P.op("act", lambda e, h=h, lb=lb: e.activation(out=kdec[:, h, 1:2], in_=rcol[:, 1:2], func=AF.Exp, scale=lb), reads=["rcol", "lg"], writes=["kdec"])
        P.op("act", lambda e: e.activation(out=cdec[:], in_=lg_s[:, l, :], func=AF.Exp, scale=128.0), reads=["lg"], writes=["cdec"])
        MAXC = S // 128
        qT = P.sbuf("rqT", [128, S], BF16)
        kT = P.sbuf("rkT", [128, S], BF16)
        ktm = P.sbuf("rktm", [128, MAXC, 128], BF16)
        vtm = P.sbuf("rvtm", [128, MAXC, 128], BF16)
        sb_all = P.sbuf("sb_all", [128, MAXC, 128], BF16)
        st_f = P.sbuf("st_f", [128, 128], F32)
        st_b = P.sbuf("st_b", [128, 128], F32)
        sf_bf = Ring(P, "sf_bf", 2, [128, 128], BF16)
        kd_ring = Ring(P, "kd", 3, [128, 128], BF16)
        pm_ring = Ring(P, "pm", 3, [128, 128], BF16)
        qf_ring = Ring(P, "qf", 3, [128, 2, 128], BF16)
        ro_ring = Ring(P, "ro", 2, [128, 512], F32)
        rsq_ring = Ring(P, "rsq", 2, [128, 512], BF16)
        rrs_ring = Ring(P, "rrs", 2, [128, 512], F32)
        rob_ring = Ring(P, "rob", 2, [128, 512], BF16)
        rgp_ring = Ring(P, "rgp", 2, [128, 512], BF16)
        ps_u = PsRing(psb, [0, 1])
        ps_i = PsRing(psb, [2, 3])
        ps_ro = PsRing(psb, [4, 5])
        ps_n = PsRing(psb, [6])
        for (sname, s0, slen) in SEQS:
            is_lat = sname == "lat"
            ncx = slen // 128
            for h in range(4):
                P.dma("pool" if "retpool" in SKIP else "sp", "ld_rqT", lambda e, h=h, s0=s0, slen=slen: e.dma_start(out=qT[:, 0:slen], in_=zfm_d[FM_SLOT["rq%d" % h], :, s0:s0 + slen]), writes=["rqT"])
                P.dma("pool" if "retpool" in SKIP else "sp", "ld_rkT", lambda e, h=h, s0=s0, slen=slen: e.dma_start(out=kT[:, 0:slen], in_=zfm_d[FM_SLOT["rk%d" % h], :, s0:s0 + slen]), writes=["rkT"])
                for c4 in range(0, ncx, 4):
                    n4 = min(4, ncx - c4)
                    for (nm_, dst_, zo_) in (("rktm", ktm, ZTM["rk"]), ("rvtm", vtm, ZTM["rv"])):
                        P.dma("sp", "ld_%s%d" % (nm_, (c4 // 4) % 2), lambda e, h=h, s0=s0, c4=c4, n4=n4, dst_=dst_, zo_=zo_: e.dma_start(
                            out=dst_[:, c4:c4 + n4, :],
                            in_=ztm_d[s0 + c4 * 128:s0 + (c4 + n4) * 128, zo_ + h * 128:zo_ + (h + 1) * 128].rearrange("(t p) c -> p t c", p=128)),
                            reads=[nm_], writes=[nm_])
                if is_lat:
                    P.dma("sp", "ld_stf", lambda e, h=h: e.dma_start(out=st_f[:], in_=c_st[l, 0, h]), writes=["st_f"])
                    P.dma("sp", "ld_stb", lambda e, h=h: e.dma_start(out=st_b[:], in_=c_st[l, 1, h]), writes=["st_b"])
                else:
                    P.op("dve", lambda e: e.memset(st_f[:], 0.0), writes=["st_f"])
                    P.op("dve", lambda e: e.memset(st_b[:], 0.0), writes=["st_b"])
                for c in range(ncx - 1, -1, -1):
                    P.op("act", lambda e, c=c: e.activation(out=sb_all[:, c, :], in_=st_b[:], func=AF.Copy), reads=["st_b"], writes=["sb_%d" % c])
                    kd, kdk = kd_ring.next()
                    P.op("dve", lambda e, kd=kd, c=c, h=h: e.tensor_scalar(out=kd[:], in0=ktm[:, c, :], scalar1=kdec[:, h, 1:2], scalar2=None, op0=ALU.mult),
                         reads=["rktm", "kdec"], writes=[kdk])
                    ps, pk = ps_u.next()
                    mm_group(ps[:, 0:128], [(kd[:], vtm[:, c, :])], [kdk, "rvtm"], [pk])
                    P.op("dve", lambda e, ps=ps, h=h: e.scalar_tensor_tensor(out=st_b[:], in0=st_b[:], scalar=cdec[:, 4 + h:5 + h], in1=ps[:, 0:128],
                                                                              op0=ALU.mult, op1=ALU.add), reads=["st_b", "cdec", pk], writes=["st_b"])
                if not is_lat:
                    cb = int(sname[3:])
                    out_toks.append(P.dma("sp", "st_sb", lambda e, cb=cb, h=h: e.dma_start(out=o_st[cb, l, 1, h], in_=st_b[:]), reads=["st_b"], writes=["o_stb"]))
                for c4 in range(0, ncx, 4):
                    ncg = min(4, ncx - c4)
                    nq = ncg * 128
                    t0 = s0 + c4 * 128
                    pso, psok = ps_ro.next()
                    for ci in range(ncg):
                        c = c4 + ci
                        sfb, sfk = sf_bf.next()
                        P.op("act", lambda e, sfb=sfb: e.activation(out=sfb[:], in_=st_f[:], func=AF.Copy), reads=["st_f"], writes=[sfk])
                        psi, psik = ps_i.next()
                        mm_group(psi[:, 0:128], [(kT[:, c * 128:(c + 1) * 128], qT[:, c * 128:(c + 1) * 128])], ["rkT", "rqT"], [psik])
                        pm, pmk = pm_ring.next()
                        P.op("dve", lambda e, psi=psi, pm=pm, h=h: e.tensor_tensor(out=pm[:], in0=psi[:, 0:128], in1=maskT[:, h, :], op=ALU.mult),
                             reads=[psik, "maskT"], writes=[pmk])
                        qf, qfk = qf_ring.next()
                        P.op("pool", lambda e, qf=qf, c=c, h=h: e.tensor_tensor(out=qf[:, 0, :], in0=qT[:, c * 128:(c + 1) * 128], in1=qdec[:, h, 0, :], op=ALU.mult),
                             reads=["rqT", "qdec"], writes=[qfk + "a"])
                        P.op("pool", lambda e, qf=qf, c=c, h=h: e.tensor_tensor(out=qf[:, 1, :], in0=qT[:, c * 128:(c + 1) * 128], in1=qdec[:, h, 1, :], op=ALU.mult),
                             reads=["rqT", "qdec"], writes=[qfk + "b"])
                        def omm(e, pso=pso, ci=ci, c=c, pm=pm, sfb=sfb, qf=qf):
                            o_ = pso[:, ci * 128:(ci + 1) * 128]
                            e.matmul(o_, vtm[:, c, :], pm[:], start=True, stop=False)
                            e.matmul(o_, sfb[:], qf[:, 0, :], start=False, stop=False)
                            return e.matmul(o_, sb_all[:, c, :], qf[:, 1, :], start=False, stop=True)
                        P.op("pe", omm, reads=["rvtm", pmk, sfk, qfk + "a", qfk + "b", "sb_%d" % c], writes=[psok])
                        kd, kdk = kd_ring.next()
                        P.op("dve", lambda e, kd=kd, c=c, h=h: e.tensor_scalar(out=kd[:], in0=ktm[:, c, :], scalar1=kdec[:, h, 0:1], scalar2=None, op0=ALU.mult),
                             reads=["rktm", "kdec"], writes=[kdk])
                        ps, pk = ps_u.next()
                        mm_group(ps[:, 0:128], [(kd[:], vtm[:, c, :])], [kdk, "rvtm"], [pk])
                        P.op("dve", lambda e, ps=ps, h=h: e.scalar_tensor_tensor(out=st_f[:], in0=st_f[:], scalar=cdec[:, h:h + 1], in1=ps[:, 0:128],
                                                                                  op0=ALU.mult, op1=ALU.add), reads=["st_f", "cdec", pk], writes=["st_f"])
                    ro, rok = ro_ring.next()
                    P.op("act", lambda e, ro=ro, pso=pso, nq=nq: e.activation(out=ro[:, 0:nq], in_=pso[:, 0:nq], func=AF.Copy), reads=[psok], writes=[rok])
                    rsq, rsqk = rsq_ring.next()
                    P.op("act", lambda e, ro=ro, rsq=rsq, nq=nq: e.activation(out=rsq[:, 0:nq], in_=ro[:, 0:nq], func=AF.Square), reads=[rok], writes=[rsqk])
                    psn, psnk = ps_n.next()
                    mm_group(psn[:, 0:nq], [(ones_bf[:], rsq[:, 0:nq])], [rsqk, "ones_bf"], [psnk])
                    rrs, rrsk = rrs_ring.next()
                    rstd_from_ssq(psn[:, 0:nq], rrs[:, 0:nq], 128, [psnk], [rrsk], None)
                    P.op("dve", lambda e, ro=ro, rrs=rrs, nq=nq, h=h: e.scalar_tensor_tensor(
                        out=ro[:, 0:nq], in0=ro[:, 0:nq], scalar=retn_s[:, l, h:h + 1], in1=rrs[:, 0:nq], op0=ALU.mult, op1=ALU.mult),
                        reads=[rok, rrsk, "retn"], writes=[rok])
                    gp, gpk = rgp_ring.next()
                    P.dma("sp", gpk, lambda e, gp=gp, h=h, t0=t0, nq=nq: e.dma_start(out=gp[:, 0:nq], in_=zfm_d[FM_SLOT["gp%d" % (4 + h)], :, t0:t0 + nq]), writes=[gpk])
                    rob, robk = rob_ring.next()
                    P.op("pool", lambda e, ro=ro, gp=gp, rob=rob, nq=nq: e.tensor_tensor(out=rob[:, 0:nq], in0=ro[:, 0:nq], in1=gp[:, 0:nq], op=ALU.mult),
                         reads=[rok, gpk], writes=[robk])
                    P.dma("sp", "st_" + robk, lambda e, rob=rob, h=h, t0=t0, nq=nq: e.dma_start(out=oT_d[4 + h, :, t0:t0 + nq], in_=rob[:, 0:nq]),
                          reads=[robk], writes=["oT_r%d_%d" % (h, t0)])
                if not is_lat:
                    cb = int(sname[3:])
                    out_toks.append(P.dma("sp", "st_sf", lambda e, cb=cb, h=h: e.dma_start(out=o_st[cb, l, 0, h], in_=st_f[:]), reads=["st_f"], writes=["o_stf"]))
        P.end()

        P.begin()
        last = (l == DEPTH - 1)
        RF = finish_rings(last)
        oT = Ring(P, "coT", 1, [128, 16, 512], BF16)
        gt = Ring(P, "cgt", 2, [128, 4, 512], BF16)
        wbr = Ring(P, "cwb", 3, [128, 4, 4, 128], BF16)
        wor = Ring(P, "cwo", 4, [128, 16, 128], BF16)
        tT = Ring(P, "ctT", 1, [128, 16, 512], BF16)
        tacc = Ring(P, "ctacc", 2, [128, 512], F32)
        ttmp = Ring(P, "cttmp", 3, [128, 512], F32)
        xb_ring = Ring(P, "cxb", 1, [128, 16, 512], F32)
        yb_ring = Ring(P, "cyb", 1, [128, 16, 512], F32)
        ysq = Ring(P, "cysq", 3, [128, 512], BF16)
        yrs = Ring(P, "cyrs", 2, [128, 512], F32)
        ps_u = PsRing(psb, [0, 1, 2])
        ps_y = PsRing(psb, [3, 4])
        ps_ys = PsRing(psb, [5])
        for b in range(NBLK):
            t0 = b * 512
            cond = 0 if b < NLB else 1
            o_, ok = oT.next()
            for q4 in range(4):
                P.dma("sp", ok + "_%d" % q4, lambda e, o_=o_, q4=q4, t0=t0: e.dma_start(
                    out=o_[:, q4 * 4:(q4 + 1) * 4, :], in_=oT_d[q4 * 4:(q4 + 1) * 4, :, t0:t0 + 512].rearrange("j p t -> p j t")),
                    writes=[ok + "_%d" % q4])
            xb, xk = xb_ring.next()
            for q4 in range(4):
                P.dma("sp", xk + "_%d" % q4, lambda e, xb=xb, q4=q4, t0=t0: e.dma_start(
                    out=xb[:, q4 * 4:(q4 + 1) * 4, :], in_=xT_d[q4 * 4:(q4 + 1) * 4, :, t0:t0 + 512].rearrange("j p t -> p j t")),
                    writes=[xk + "_t%d" % j for j in range(q4 * 4, q4 * 4 + 4)])
            t_, tk = tT.next()
            wb_q = {}
            wo_q = {}

            def issue_wb(j):
                wb, wk = wbr.next()
                P.dma("pool", wk, lambda e, wb=wb, j=j: e.dma_start(out=wb[:], in_=w_br[l, j]), writes=[wk])
                wb_q[j] = (wb, wk)

            def issue_wo(i):
                wo, wok = wor.next()
                P.dma("pool", wok, lambda e, wo=wo, i=i: e.dma_start(out=wo[:], in_=w_o[l, i]), writes=[wok])
                wo_q[i] = (wo, wok)
            issue_wb(0)
            issue_wb(1)
            for j in range(16):
                g_, gk = gt.next()
                P.dma("sp", gk, lambda e, g_=g_, j=j, t0=t0: e.dma_start(
                    out=g_[:], in_=zfm_d[FM_SLOT["g0_0"]:FM_SLOT["g0_0"] + 64, :, t0:t0 + 512].rearrange("(n j) p t -> j p n t", j=16)[j]), writes=[gk])
                wb, wk = wb_q.pop(j)
                ta, tak = tacc.next()
                for n in range(4):
                    ps, pk = ps_u.next()
                    mm_group(ps[:], [(wb[:, n, k, :], o_[:, n * 4 + k, :]) for k in range(4)], [wk, ok + "_%d" % n], [pk])
                    if n == 0:
                        P.op("dve", lambda e, ps=ps, g_=g_, ta=ta, n=n: e.tensor_tensor(out=ta[:], in0=ps[:], in1=g_[:, n, :], op=ALU.mult),
                             reads=[pk, gk], writes=[tak])
                    else:
                        tm, tmk = ttmp.next()
                        P.op("dve", lambda e, ps=ps, g_=g_, tm=tm, n=n: e.tensor_tensor(out=tm[:], in0=ps[:], in1=g_[:, n, :], op=ALU.mult),
                             reads=[pk, gk], writes=[tmk])
                        if n < 3:
                            P.op("pool", lambda e, ta=ta, tm=tm: e.tensor_tensor(out=ta[:], in0=ta[:], in1=tm[:], op=ALU.add), reads=[tak, tmk], writes=[tak])
                        else:
                            P.op("pool", lambda e, ta=ta, tm=tm, t_=t_, j=j: e.tensor_tensor(out=t_[:, j, :], in0=ta[:], in1=tm[:], op=ALU.add),
                                 reads=[tak, tmk], writes=[tk + "_%d" % j])
                if j + 2 < 16:
                    issue_wb(j + 2)
                elif j == 14:
                    issue_wo(0)
                    issue_wo(1)
                else:
                    issue_wo(2)
            yb, yk = yb_ring.next()
            pss, pssk = ps_ys.next()
            tkeys = [tk + "_%d" % j for j in range(16)]
            for i in range(16):
                wo, wok = wo_q.pop(i)
                if i + 3 < 16:
                    issue_wo(i + 3)
                ps, pk = ps_y.next()
                mm_group(ps[:], [(wo[:, j, :], t_[:, j, :]) for j in range(16)], [wok] + tkeys, [pk])
                P.op("dve", lambda e, ps=ps, yb=yb, i=i: e.tensor_copy(out=yb[:, i, :], in_=ps[:]), reads=[pk], writes=[yk + "_%d" % i])
                sq, sqk = ysq.next()
                P.op("act", lambda e, yb=yb, i=i, sq=sq: e.activation(out=sq[:], in_=yb[:, i, :], func=AF.Square), reads=[yk + "_%d" % i], writes=[sqk])
                P.op("pe", lambda e, sq=sq, i=i, pss=pss: e.matmul(pss[:], ones_bf[:], sq[:], start=(i == 0), stop=(i == 15)),
                     reads=[sqk, "ones_bf"], writes=[pssk])
            rs, rsk = yrs.next()
            rstd_from_ssq(pss[:], rs[:], D, [pssk], [rsk], None)
            for i in range(16):
                tm, tmk = ttmp.next()
                P.op("dve", lambda e, yb=yb, rs=rs, tm=tm, i=i: e.tensor_tensor(out=tm[:], in0=yb[:, i, :], in1=rs[:], op=ALU.mult),
                     reads=[yk + "_%d" % i, rsk], writes=[tmk])
                P.op("dve", lambda e, xb=xb, tm=tm, i=i, cond=cond: e.scalar_tensor_tensor(
                    out=xb[:, i, :], in0=tm[:], scalar=modG[:, l, cond, i:i + 1], in1=xb[:, i, :], op0=ALU.mult, op1=ALU.add),
                    reads=[tmk, xk + "_t%d" % i, "modG"], writes=[xk + "_t%d" % i])
            finish_block(xb, [xk + "_t%d" % i for i in range(16)], b, None if last else l + 1, RF, xk)
        P.end()

    P.close()
    return nc


def rope_tables(S, T):
    pos = np.arange(S)
    row = (pos // 64).astype(np.float32)
    col = (pos % 64).astype(np.float32)
    inv = (10000.0 ** (-np.arange(16, dtype=np.float32) / 16)).astype(np.float32)
    ang = np.concatenate([row[:, None] * inv[None], col[:, None] * inv[None]], axis=-1)
    cos = np.cos(ang).astype(np.float32)
    sin = np.sin(ang).astype(np.float32)
    cT = np.ones((128, T), np.float32)
    sT = np.zeros((128, T), np.float32)
    for p in range(128):
        d = p % 64
        f = d % 32
        cT[p, :S] = cos[:, f]
        sT[p, :S] = -sin[:, f] if d < 32 else sin[:, f]
    return cT, sT


def nat_bias_tables(rpb, rows):
    DEPTH = rpb.shape[0]
    nb = rows // 8
    out = np.full((DEPTH, 3, 8, 4, 128, 512), NEG, np.float32)
    for cls, b in ((0, 0), (1, 1), (2, nb - 1)):
        if cls == 1 and nb < 3:
            continue
        lo = max(0, 8 * b - 4)
        hi = min(rows, 8 * b + 12)
        i = np.arange(2)[:, None, None, None]
        kc = np.arange(64)[None, :, None, None]
        j = np.arange(8)[None, None, :, None]
        qc = np.arange(64)[None, None, None, :]
        for ti in range((hi - lo) // 2):
            kr = lo + 2 * ti + i
            r = 8 * b + j
            rs = np.clip(r - 4, 0, rows - 8)
            cs = np.clip(qc - 8, 0, 48)
            valid = (kr >= rs) & (kr < rs + 8) & (kc >= cs) & (kc < cs + 16)
            ro = np.clip(kr - r + 7, 0, 14)
            co = np.clip(kc - qc + 15, 0, 30)
            valid = np.broadcast_to(valid, (2, 64, 8, 64))
            ro = np.broadcast_to(ro, (2, 64, 8, 64))
            co = np.broadcast_to(co, (2, 64, 8, 64))
            g = rpb[:, :, ro, co]
            g = np.where(valid[None, None], g, np.float32(NEG))
            out[:, cls, ti] = g.reshape(DEPTH, 4, 128, 512)
    return out


def const_tables():
    k = np.arange(128)[:, None].astype(np.float32)
    q = np.arange(128)[None, :].astype(np.float32)
    Df = np.where(k <= q, q - k, 0.0).astype(np.float32)
    Ub = np.where(k > q, k - q, 0.0).astype(np.float32)
    i1 = np.broadcast_to(np.arange(128, dtype=np.float32)[None, :] + 1.0, (128, 128))
    i2 = np.broadcast_to(128.0 - np.arange(128, dtype=np.float32)[None, :], (128, 128))
    ret_tab = np.stack([Df, Ub, i1, i2]).astype(np.float32)
    ret_col = np.stack([127.0 - np.arange(128), np.arange(128)], axis=1).astype(np.float32)
    kk = np.arange(128)[:, None]
    qq = np.arange(512)[None, :]
    swa = np.stack([np.where(np.abs(qq - (o * 128 + kk)) <= 128, 0.0, NEG) for o in range(-1, 5)]).astype(np.float32)
    return ret_tab, ret_col, swa


def pcol(v, n):
    return np.ascontiguousarray(v.reshape(n, 128).T)


def host_weights(S, DEPTH, inp):
    T = S + 2 * L
    w = {}
    w_in = inp["w_in"]
    fm = np.empty((DEPTH, len(FM), 128, 16, 128), np.float32)
    for ti, (_, _, cols) in enumerate(FM):
        blk = w_in[:, :, cols]
        fm[:, ti] = blk.reshape(DEPTH, 16, 128, 128).transpose(0, 2, 1, 3)
    w["w_fm"] = fm
    tmcols = sum([c for _, c, _ in TMG], [])
    w["w_tm"] = np.ascontiguousarray(w_in[:, :, tmcols].reshape(DEPTH, 16, 128, TM_COLS).transpose(0, 2, 1, 3))
    w["w_mod"] = np.ascontiguousarray(inp["w_mod"].reshape(DEPTH, 16, 128, 24, 256).transpose(0, 3, 2, 1, 4))
    w["bmod"] = np.ascontiguousarray(np.stack([pcol(inp["b_mod"][l], 48) for l in range(DEPTH)], axis=1))
    w["npre"] = np.ascontiguousarray(np.stack([pcol(inp["norm_pre"][l], 16) for l in range(DEPTH)], axis=1))
    w["npost"] = np.ascontiguousarray(np.stack([pcol(inp["norm_post"][l], 16) for l in range(DEPTH)], axis=1))
    w["qn"] = np.ascontiguousarray(np.stack([pcol(inp["mla_q_norm"][l], 4) for l in range(DEPTH)], axis=1))
    w["kvn"] = np.ascontiguousarray(np.stack([pcol(inp["mla_kv_norm"][l], 2) for l in range(DEPTH)], axis=1))
    w["retn"] = np.ascontiguousarray(np.stack([pcol(inp["ret_norm"][l], 4) for l in range(DEPTH)], axis=1))
    w["kvn_row"] = np.ascontiguousarray(np.broadcast_to(inp["mla_kv_norm"][None], (128, DEPTH, 256)))
    w["decay"] = np.ascontiguousarray(np.broadcast_to(inp["ret_decay"].reshape(DEPTH, 8)[None], (128, DEPTH, 8)))
    w["sink"] = np.ascontiguousarray(np.broadcast_to(inp["swa_sink"][None], (128, DEPTH, 8)))
    qcols = []
    for h in range(4):
        b0 = h * 192
        qcols += list(range(b0, b0 + 128)) + list(range(b0 + 128, b0 + 192)) + _swap64(b0 + 128)
    w["w_qup"] = np.ascontiguousarray(inp["mla_w_q_up"][:, :, qcols].reshape(DEPTH, 4, 128, 1024).transpose(0, 2, 1, 3))
    kcols = []
    for h in range(4):
        kcols += list(range(h * 256, h * 256 + 128))
    for h in range(4):
        kcols += list(range(h * 256 + 128, h * 256 + 256))
    w["w_kvup"] = np.ascontiguousarray(inp["mla_w_kv_up"][:, :, kcols].reshape(DEPTH, 2, 128, 1024).transpose(0, 2, 1, 3))
    w["w_br"] = np.ascontiguousarray(inp["w_branch"].reshape(DEPTH, 4, 4, 128, 16, 128).transpose(0, 4, 3, 1, 2, 5))
    w["w_o"] = np.ascontiguousarray(inp["w_out"].reshape(DEPTH, 16, 128, 16, 128).transpose(0, 3, 2, 1, 4))
    w["nat_bias"] = nat_bias_tables(inp["nat_rpb"], S // 64)
    cT, sT = rope_tables(S, T)
    w["cosT"], w["sinT"] = cT, sT
    w["ident"] = np.eye(128, dtype=np.float32)
    rt, rc, sw = const_tables()
    w["ret_tab"], w["ret_col"], w["swa_mask"] = rt, rc, sw
    return w


def core_inputs(S, DEPTH, inp, shared, core):
    seq = core // 2
    m = dict(shared)
    m["x_lat"] = np.ascontiguousarray(inp["x_sample"][seq])
    m["x_ctx"] = np.ascontiguousarray(inp["x_prompt"][2 * core:2 * core + 2].reshape(2 * L, D))
    m["c_ckv"] = np.ascontiguousarray(inp["cache_mla_ckv"][seq])
    m["c_kr"] = np.ascontiguousarray(inp["cache_mla_krope"][seq])
    m["c_st"] = np.ascontiguousarray(inp["state_ret"][seq])
    m["c_nk"] = np.ascontiguousarray(inp["cache_nat_k"][seq].reshape(DEPTH, PAST, 512))
    m["c_nv"] = np.ascontiguousarray(inp["cache_nat_v"][seq].reshape(DEPTH, PAST, 512))
    m["c_sk"] = np.ascontiguousarray(inp["cache_swa_k"][seq].reshape(DEPTH, PAST, 128))
    m["c_sv"] = np.ascontiguousarray(inp["cache_swa_v"][seq].reshape(DEPTH, PAST, 128))
    cond = np.stack([inp["c"][seq], inp["c_ctx"]], axis=-1)
    m["condT"] = np.ascontiguousarray(cond.reshape(16, 128, 2).transpose(1, 0, 2))
    return m


def run(S, DEPTH, inp, n_cores=8, trace=False):
    inp = {k: np.asarray(v, dtype=np.float32) for k, v in inp.items()}
    nc = build(S, DEPTH)
    shared = host_weights(S, DEPTH, inp)
    in_maps = [core_inputs(S, DEPTH, inp, shared, c) for c in range(n_cores)]
    res = run_bass_kernel_spmd(nc, in_maps, core_ids=list(range(n_cores)), trace=trace)
    R = res.results
    nb = 2 * n_cores
    yp = np.concatenate([R[c]["y_ctx"].reshape(2, L, D) for c in range(n_cores)], axis=0)
    ys = np.stack([R[2 * i]["y_lat"] for i in range(n_cores // 2)], axis=0)
    ckv = np.concatenate([R[c]["o_ckv"] for c in range(n_cores)], axis=0)
    kr = np.concatenate([R[c]["o_kr"] for c in range(n_cores)], axis=0)
    st = np.concatenate([R[c]["o_st"] for c in range(n_cores)], axis=0)
    nk = np.concatenate([R[c]["o_nk"] for c in range(n_cores)], axis=0).reshape(nb, DEPTH, L, 4, 128)
    nv = np.concatenate([R[c]["o_nv"] for c in range(n_cores)], axis=0).reshape(nb, DEPTH, L, 4, 128)
    sk = np.concatenate([R[c]["o_sk"] for c in range(n_cores)], axis=0).reshape(nb, DEPTH, L, 2, 64)
    sv = np.concatenate([R[c]["o_sv"] for c in range(n_cores)], axis=0).reshape(nb, DEPTH, L, 2, 64)
    outs = (yp, ys, ckv, kr, st, nk, nv, sk, sv)
    return tuple(np.ascontiguousarray(o, dtype=np.float32) for o in outs), res


def kernel(**inputs):
    outs, _ = run(4096, 4, inputs)
    return outs
```

```python
import numpy as np
from contextlib import ExitStack
import concourse.bass as bass
import concourse.mybir as mybir
from concourse.bass_utils import run_bass_kernel_spmd

F32 = mybir.dt.float32
BF16 = mybir.dt.bfloat16
AF = mybir.ActivationFunctionType
ALU = mybir.AluOpType
ENGS = ("pe", "act", "dve", "pool", "sp")
D = 2048
L = 256
PAST = 512
EPS = 1e-6
NEG = -30000.0
SKIP = set()
MAXPHASE = 10 ** 9


class Prog:
    def __init__(self, nc):
        self.nc = nc
        self.base = ExitStack()
        self.stack = None
        self.ops = {e: [] for e in ENGS}
        self.cnt = {e: 0 for e in ENGS}
        self.sem = {e: self.base.enter_context(nc.semaphore("s_" + e)) for e in ENGS}
        self.dsem = {}
        self.physp = {}
        self.nused = {}
        self.kmap = {}
        self.last_w = {}
        self.reads = {}
        self.known = {e: {} for e in ENGS}
        self.phase_keys = set()
        self.n_ops = 0

    def sbuf(self, name, shape, dtype, persist=False):
        st = self.base if persist else self.stack
        self.uid = getattr(self, "uid", 0) + 1
        return st.enter_context(self.nc.sbuf_tensor("%s_u%d" % (name, self.uid), list(shape), dtype))

    def psum(self, name, shape, dtype=F32):
        return self.base.enter_context(self.nc.psum_tensor(name, list(shape), dtype))

    def dma_sem(self, key, eng):
        if key not in self.kmap:
            used = self.nused.setdefault(eng, 0)
            self.nused[eng] += 1
            pool = self.physp.setdefault(eng, [])
            if used >= len(pool):
                idx = len(self.dsem)
                s = self.base.enter_context(self.nc.semaphore("d_%d" % idx))
                self.dsem[idx] = [s, 0]
                pool.append(idx)
            self.kmap[key] = pool[used]
        return self.kmap[key]

    def _deps(self, eng, reads, writes):
        deps = []
        for r in reads:
            t = self.last_w.get(r)
            if t is not None:
                deps.append(t)
        for w in writes:
            t = self.last_w.get(w)
            if t is not None:
                deps.append(t)
            deps.extend(self.reads.get(w, {}).items())
        waits = {}
        kn = self.known[eng]
        for (src, val) in deps:
            if src == eng and eng in ("pe", "sp"):
                continue
            if kn.get(src, 0) >= val:
                continue
            if waits.get(src, 0) < val:
                waits[src] = val
        for src, val in waits.items():
            kn[src] = val
        return list(waits.items())

    def _commit(self, tok, reads, writes):
        for r in reads:
            d = self.reads.setdefault(r, {})
            if d.get(tok[0], 0) < tok[1]:
                d[tok[0]] = tok[1]
        for w in writes:
            self.last_w[w] = tok
            self.reads[w] = {}

    def op(self, eng, fn, reads=(), writes=()):
        if self.dead:
            return None
        waits = self._deps(eng, reads, writes)
        self.cnt[eng] += 1
        tok = (eng, self.cnt[eng])
        self.ops[eng].append((waits, fn, ("eng", eng)))
        self._commit(tok, reads, writes)
        self.n_ops += 1
        return tok

    def dma(self, eng, key, fn, reads=(), writes=()):
        if self.dead:
            return None
        key = self.dma_sem(key, eng)
        ds = self.dsem[key]
        self.phase_keys.add(key)
        waits = self._deps(eng, reads, writes)
        src = ("dma", key)
        kn = self.known[eng]
        if ds[1] > 0 and kn.get(src, 0) < ds[1]:
            waits = [w for w in waits if w[0] != src] + [(src, ds[1])]
            kn[src] = ds[1]
        ds[1] += 16
        tok = (src, ds[1])
        self.ops[eng].append((waits, fn, ("dma", key)))
        self._commit(tok, reads, writes)
        self.n_ops += 1
        return tok

    def _semof(self, src):
        if isinstance(src, tuple):
            return self.dsem[src[1]][0]
        return self.sem[src]

    def begin(self):
        self.stack = ExitStack()
        self.nphase = getattr(self, "nphase", 0) + 1
        self.dead = self.nphase > MAXPHASE

    def end(self):
        waits = [(("dma", k), self.dsem[k][1]) for k in sorted(self.phase_keys)]
        self.cnt["sp"] += 1
        self.ops["sp"].append((waits, lambda e: e.nop(), ("eng", "sp")))
        final = dict(self.cnt)
        for e in ENGS:
            w = [(f, final[f]) for f in ENGS if f != e and final[f] > self.known[e].get(f, 0)]
            self.ops[e].append((w, None, None))
        nc = self.nc
        with nc.Block() as block:
            def mk(e):
                def body(eng):
                    for (waits, fn, kind) in self.ops[e]:
                        for (src, val) in waits:
                            eng.wait_ge(self._semof(src), val)
                        if fn is None:
                            continue
                        ins = fn(eng)
                        if kind[0] == "eng":
                            ins.then_inc(self.sem[e], 1)
                        else:
                            ins.then_inc(self.dsem[kind[1]][0], 16)
                return body
            block.tensor(mk("pe"))
            block.scalar(mk("act"))
            block.vector(mk("dve"))
            block.gpsimd(mk("pool"))
            block.sync(mk("sp"))
        self.ops = {e: [] for e in ENGS}
        self.last_w = {}
        self.reads = {}
        for e in ENGS:
            for f in ENGS:
                self.known[e][f] = final[f]
            for k in self.dsem:
                self.known[e][("dma", k)] = self.dsem[k][1]
        self.phase_keys = set()
        self.kmap = {}
        self.nused = {}
        self.stack.close()
        self.stack = None

    def close(self):
        self.base.close()


class Ring:
    def __init__(self, P, name, n, shape, dtype):
        self.bufs = [P.sbuf("%s%d" % (name, i), shape, dtype) for i in range(n)]
        self.keys = ["%s%d" % (name, i) for i in range(n)]
        self.n = n
        self.i = 0

    def next(self):
        j = self.i % self.n
        self.i += 1
        return self.bufs[j], self.keys[j]


class PsRing:
    def __init__(self, banks, idxs):
        self.banks = [banks[i] for i in idxs]
        self.keys = ["ps%d" % i for i in idxs]
        self.n = len(idxs)
        self.i = 0

    def next(self):
        j = self.i % self.n
        self.i += 1
        return self.banks[j], self.keys[j]


OFF = dict(qa=0, kva=512, kr=768, rq=832, rk=1344, rv=1856, nq=2368, nk=2880, nv=3392, sq=3904, sk=4416, sv=4544,
           gp=4672, gate=6720)


def _swap64(base):
    return list(range(base + 32, base + 64)) + list(range(base, base + 32))


def fm_tiles():
    t = []
    for i in range(4):
        t.append(("qa%d" % i, "copy", list(range(OFF["qa"] + 128 * i, OFF["qa"] + 128 * (i + 1)))))
    for i in range(2):
        t.append(("kva%d" % i, "copy", list(range(OFF["kva"] + 128 * i, OFF["kva"] + 128 * (i + 1)))))
    kr = list(range(OFF["kr"], OFF["kr"] + 64))
    t.append(("kr", "rope", kr + kr))
    t.append(("kr_s", "swap", _swap64(OFF["kr"]) * 2))
    for h in range(4):
        t.append(("rq%d" % h, "copy", list(range(OFF["rq"] + 128 * h, OFF["rq"] + 128 * (h + 1)))))
    for h in range(4):
        t.append(("rk%d" % h, "copys", list(range(OFF["rk"] + 128 * h, OFF["rk"] + 128 * (h + 1)))))
    for h in range(4):
        t.append(("nq%d" % h, "copy", list(range(OFF["nq"] + 128 * h, OFF["nq"] + 128 * (h + 1)))))
    for h in range(4):
        t.append(("nk%d" % h, "copy", list(range(OFF["nk"] + 128 * h, OFF["nk"] + 128 * (h + 1)))))
    for i in range(4):
        b0, b1 = OFF["sq"] + 128 * i, OFF["sq"] + 128 * i + 64
        t.append(("sq%d" % i, "rope", list(range(b0, b0 + 128))))
        t.append(("sq%d_s" % i, "swap", _swap64(b0) + _swap64(b1)))
    for g in range(2):
        b0 = OFF["sk"] + 64 * g
        t.append(("sk%d" % g, "rope", list(range(b0, b0 + 64)) * 2))
        t.append(("sk%d_s" % g, "swap", _swap64(b0) * 2))
    for i in range(16):
        t.append(("gp%d" % i, "silu", list(range(OFF["gp"] + 128 * i, OFF["gp"] + 128 * (i + 1)))))
    for n in range(4):
        for j in range(16):
            b0 = OFF["gate"] + n * D + 128 * j
            t.append(("g%d_%d" % (n, j), "sig", list(range(b0, b0 + 128))))
    return t


FM = fm_tiles()
FM_SLOT = {}
for _n, _k, _c in FM:
    if _k != "swap":
        FM_SLOT[_n] = len(FM_SLOT)
NSLOT = len(FM_SLOT)

TMG = [("rk", list(range(OFF["rk"], OFF["rk"] + 512)), "all"),
       ("rv", list(range(OFF["rv"], OFF["rv"] + 512)), "all"),
       ("nv", list(range(OFF["nv"], OFF["nv"] + 512)), "all"),
       ("sv", list(range(OFF["sv"], OFF["sv"] + 128)), "all"),
       ("o1", list(range(OFF["kva"], OFF["kva"] + 256)) + list(range(OFF["kr"], OFF["kr"] + 64))
        + list(range(OFF["sk"], OFF["sk"] + 128)), "ctx"),
       ("nk", list(range(OFF["nk"], OFF["nk"] + 512)), "ctx")]
TM_OFF = {}
_o = 0
for _n, _c, _w in TMG:
    TM_OFF[_n] = (_o, len(_c))
    _o += len(_c)
TM_COLS = _o
ZTM = dict(rk=0, rv=512, nv=1024, sv=1536)
ZTM_COLS = 1664


def build(S, DEPTH):
    assert S % 512 == 0 and S >= 1024
    T = S + 2 * L
    NLB = S // 512
    NBLK = NLB + 1
    ROWS = S // 64
    NQB = S // 512
    nc = bass.Bass("TRN2", target_bir_lowering=False)

    def din(name, shape, dt=F32):
        return nc.dram_tensor(name, list(shape), dt, kind="ExternalInput").ap()

    def dout(name, shape):
        return nc.dram_tensor(name, list(shape), F32, kind="ExternalOutput").ap()

    def dscr(name, shape, dt):
        return nc.dram_tensor(name, list(shape), dt, kind="Internal").ap()

    x_lat = din("x_lat", [S, D])
    x_ctx = din("x_ctx", [2 * L, D])
    c_ckv = din("c_ckv", [DEPTH, PAST, 256])
    c_kr = din("c_kr", [DEPTH, PAST, 64])
    c_st = din("c_st", [DEPTH, 2, 4, 128, 128])
    c_nk = din("c_nk", [DEPTH, PAST, 512])
    c_nv = din("c_nv", [DEPTH, PAST, 512])
    c_sk = din("c_sk", [DEPTH, PAST, 128])
    c_sv = din("c_sv", [DEPTH, PAST, 128])
    condT = din("condT", [128, 16, 2])
    w_mod = din("w_mod", [DEPTH, 24, 128, 16, 256])
    bmod = din("bmod", [128, DEPTH, 48])
    npre = din("npre", [128, DEPTH, 16])
    npost = din("npost", [128, DEPTH, 16])
    w_fm = din("w_fm", [DEPTH, len(FM), 128, 16, 128])
    w_tm = din("w_tm", [DEPTH, 128, 16, TM_COLS])
    qn = din("qn", [128, DEPTH, 4])
    kvn = din("kvn", [128, DEPTH, 2])
    retn = din("retn", [128, DEPTH, 4])
    kvn_row = din("kvn_row", [128, DEPTH, 256])
    decay = din("decay", [128, DEPTH, 8])
    sink = din("sink", [128, DEPTH, 8])
    w_qup = din("w_qup", [DEPTH, 128, 4, 1024])
    w_kvup = din("w_kvup", [DEPTH, 128, 2, 1024])
    w_br = din("w_br", [DEPTH, 16, 128, 4, 4, 128])
    w_o = din("w_o", [DEPTH, 16, 128, 16, 128])
    nat_bias = din("nat_bias", [DEPTH, 3, 8, 4, 128, 512])
    cosT = din("cosT", [128, T])
    sinT = din("sinT", [128, T])
    ident_d = din("ident", [128, 128])
    ret_tab = din("ret_tab", [4, 128, 128])
    ret_col = din("ret_col", [128, 2])
    swa_mask = din("swa_mask", [6, 128, 512])

    y_lat = dout("y_lat", [S, D])
    y_ctx = dout("y_ctx", [2 * L, D])
    o_ckv = dout("o_ckv", [2, DEPTH, L, 256])
    o_kr = dout("o_kr", [2, DEPTH, L, 64])
    o_st = dout("o_st", [2, DEPTH, 2, 4, 128, 128])
    o_nk = dout("o_nk", [2, DEPTH, L, 512])
    o_nv = dout("o_nv", [2, DEPTH, L, 512])
    o_sk = dout("o_sk", [2, DEPTH, L, 128])
    o_sv = dout("o_sv", [2, DEPTH, L, 128])

    xT_d = dscr("xT_d", [16, 128, T], F32)
    hT_d = dscr("hT_d", [16, 128, T], BF16)
    zfm_d = dscr("zfm_d", [NSLOT, 128, T], BF16)
    ztm_d = dscr("ztm_d", [T, ZTM_COLS], BF16)
    oT_d = dscr("oT_d", [16, 128, T], BF16)

    P = Prog(nc)
    out_toks = []
    psb = [P.psum("psb%d" % i, [128, 512]) for i in range(8)]

    ident = P.sbuf("ident", [128, 128], F32, True)
    ones_bf = P.sbuf("ones_bf", [128, 128], BF16, True)
    ones_f = P.sbuf("ones_f", [128, 128], F32, True)
    modA = P.sbuf("modA", [128, DEPTH, 2, 16], F32, True)
    modB = P.sbuf("modB", [128, DEPTH, 2, 16], F32, True)
    modG = P.sbuf("modG", [128, DEPTH, 2, 16], F32, True)
    qn_s = P.sbuf("qn_s", [128, DEPTH, 4], F32, True)
    kvn_s = P.sbuf("kvn_s", [128, DEPTH, 2], F32, True)
    retn_s = P.sbuf("retn_s", [128, DEPTH, 4], F32, True)
    lg_s = P.sbuf("lg_s", [128, DEPTH, 8], F32, True)
    esink_s = P.sbuf("esink_s", [128, DEPTH, 8], F32, True)
    eps_t = P.sbuf("eps_t", [128, 1], F32, True)

    def mm_group(out_ap, pairs, reads, writes):
        def fn(e):
            n = len(pairs)
            ins = None
            for i, (l, r) in enumerate(pairs):
                ins = e.matmul(out_ap, l, r, start=(i == 0), stop=(i == n - 1))
            return ins
        return P.op("pe", fn, reads, writes)

    def rstd_from_ssq(ps_ap, out_ap, n_feat, reads, writes, tmpkey):
        P.op("act", lambda e: e.activation(out=out_ap, in_=ps_ap, func=AF.Sqrt, bias=eps_t[0:ps_ap.shape[0], :],
                                           scale=1.0 / n_feat), reads=reads, writes=writes)
        P.op("dve", lambda e: e.reciprocal(out=out_ap, in_=out_ap), reads=writes, writes=writes)

    P.begin()
    sc_t = P.sbuf("sc_t", [128, 16, 2], F32)
    npre_s = P.sbuf("npre_s", [128, DEPTH, 16], F32)
    npost_s = P.sbuf("npost_s", [128, DEPTH, 16], F32)
    bmod_s = P.sbuf("bmod_s", [128, DEPTH, 48], F32)
    dec_s = P.sbuf("dec_s", [128, DEPTH, 8], F32)
    mod_s = P.sbuf("mod_s", [128, 48, 2], F32)
    P.op("dve", lambda e: e.memset(ones_bf[:], 1.0), writes=["ones_bf"])
    P.op("dve", lambda e: e.memset(ones_f[:], 1.0), writes=["ones_f"])
    P.op("dve", lambda e: e.memset(eps_t[:], EPS), writes=["eps"])
    for nm, dst, src in (("ident", ident, ident_d), ("sc", sc_t, condT), ("npre", npre_s, npre), ("npost", npost_s, npost),
                         ("bmod", bmod_s, bmod), ("qn", qn_s, qn), ("kvn", kvn_s, kvn), ("retn", retn_s, retn),
                         ("dec", dec_s, decay), ("esink", esink_s, sink)):
        P.dma("sp", "ld_" + nm, (lambda e, dst=dst, src=src: e.dma_start(out=dst[:], in_=src)), writes=[nm])
    P.op("act", lambda e: e.activation(out=sc_t[:], in_=sc_t[:], func=AF.Silu), reads=["sc"], writes=["sc"])
    P.op("act", lambda e: e.activation(out=esink_s[:], in_=esink_s[:], func=AF.Exp), reads=["esink"], writes=["esink"])
    P.op("act", lambda e: e.activation(out=dec_s[:], in_=dec_s[:], func=AF.Exp, scale=-1.0), reads=["dec"], writes=["dec"])
    P.op("dve", lambda e: e.tensor_scalar(out=dec_s[:], in0=dec_s[:], scalar1=1.0, scalar2=None, op0=ALU.add),
         reads=["dec"], writes=["dec"])
    P.op("act", lambda e: e.activation(out=dec_s[:], in_=dec_s[:], func=AF.Ln), reads=["dec"], writes=["dec"])
    P.op("dve", lambda e: e.tensor_scalar(out=lg_s[:], in0=dec_s[:], scalar1=-1.0, scalar2=None, op0=ALU.mult),
         reads=["dec"], writes=["lg"])
    wm_ring = Ring(P, "wm", 3, [128, 16, 256], F32)
    for l in range(DEPTH):
        for wt in range(24):
            wb, wk = wm_ring.next()
            P.dma("sp", wk, (lambda e, wb=wb, l=l, wt=wt: e.dma_start(out=wb[:], in_=w_mod[l, wt])), writes=[wk])
            for half in range(2):
                ft = wt * 2 + half
                mm_group(psb[0][:, ft * 2:ft * 2 + 2],
                         [(wb[:, k, half * 128:(half + 1) * 128], sc_t[:, k, :]) for k in range(16)],
                         reads=[wk, "sc"], writes=["ps0"])
        for c in range(2):
            P.op("dve", lambda e, c=c, l=l: e.tensor_tensor(
                out=mod_s[:, :, c], in0=psb[0][:, 0:96].rearrange("p (f c) -> p f c", c=2)[:, :, c],
                in1=bmod_s[:, l, :], op=ALU.add), reads=["ps0", "bmod"], writes=["mod%d" % c])
        for c in range(2):
            P.op("dve", lambda e, c=c, l=l: e.scalar_tensor_tensor(
                out=modA[:, l, c, :], in0=mod_s[:, 16:32, c], scalar=1.0, in1=npre_s[:, l, :],
                op0=ALU.add, op1=ALU.mult), reads=["mod%d" % c, "npre"], writes=["modA"])
            P.op("dve", lambda e, c=c, l=l: e.tensor_copy(out=modB[:, l, c, :], in_=mod_s[:, 0:16, c]),
                 reads=["mod%d" % c], writes=["modB"])
            P.op("dve", lambda e, c=c, l=l: e.tensor_tensor(
                out=modG[:, l, c, :], in0=mod_s[:, 32:48, c], in1=npost_s[:, l, :], op=ALU.mult),
                reads=["mod%d" % c, "npost"], writes=["modG"])
    P.end()

    def finish_block(xblk, xkeys, b, lnext, R, bkey):
        cond = 0 if b < NLB else 1
        t0 = b * 512
        if lnext is not None:
            pss, psk = R["ps_ss"].next()
            for j in range(16):
                sq, sqk = R["sq"].next()
                P.op("act", lambda e, sq=sq, j=j: e.activation(out=sq[:], in_=xblk[:, j, :], func=AF.Square),
                     reads=[xkeys[j]], writes=[sqk])
                P.op("pe", lambda e, sq=sq, j=j, pss=pss: e.matmul(pss[:], ones_bf[:], sq[:], start=(j == 0), stop=(j == 15)),
                     reads=[sqk, "ones_bf"], writes=[psk])
            rs, rsk = R["rstd"].next()
            rstd_from_ssq(pss[:], rs[:], D, [psk], [rsk], None)
            hst, hk = R["hst"].next()
            for j in range(16):
                tmp, tk = R["tmp"].next()
                P.op("dve", lambda e, tmp=tmp, j=j, rs=rs: e.tensor_tensor(out=tmp[:], in0=xblk[:, j, :], in1=rs[:], op=ALU.mult),
                     reads=[xkeys[j], rsk], writes=[tk])
                P.op("act", lambda e, tmp=tmp, j=j, hst=hst: e.activation(
                    out=hst[:, j, :], in_=tmp[:], func=AF.Identity, scale=modA[:, lnext, cond, j:j + 1],
                    bias=modB[:, lnext, cond, j:j + 1]), reads=[tk, "modA", "modB"], writes=[hk + "_%d" % j])
            for q4 in range(4):
                P.dma("sp", "st_%s_%d" % (hk, q4), lambda e, hst=hst, q4=q4: e.dma_start(
                    out=hT_d[q4 * 4:q4 * 4 + 4, :, t0:t0 + 512].rearrange("j p t -> p j t"), in_=hst[:, q4 * 4:q4 * 4 + 4, :]),
                    reads=[hk + "_%d" % j for j in range(q4 * 4, q4 * 4 + 4)], writes=["hT_d%d_%d" % (b, q4)])
                P.dma("sp", "st_%s_%d" % (bkey, q4), lambda e, q4=q4: e.dma_start(
                    out=xT_d[q4 * 4:q4 * 4 + 4, :, t0:t0 + 512].rearrange("j p t -> p j t"), in_=xblk[:, q4 * 4:q4 * 4 + 4, :]),
                    reads=list(xkeys[q4 * 4:q4 * 4 + 4]), writes=["xT_d%d_%d" % (b, q4)])
        else:
            for tt in range(4):
                yo, yk = R["yo"].next()
                for q4 in range(4):
                    pst, ptk = R["ps_t"].next()
                    def tr(e, pst=pst, q4=q4, tt=tt):
                        ins = None
                        for jj in range(4):
                            j = q4 * 4 + jj
                            ins = e.transpose(out=pst[:, jj * 128:(jj + 1) * 128], in_=xblk[:, j, tt * 128:(tt + 1) * 128],
                                              identity=ident[:])
                        return ins
                    P.op("pe", tr, reads=list(xkeys[q4 * 4:q4 * 4 + 4]) + ["ident"], writes=[ptk])
                    eng = "act" if q4 % 2 == 0 else "dve"
                    if eng == "act":
                        P.op("act", lambda e, pst=pst, q4=q4, yo=yo: e.activation(out=yo[:, q4 * 512:(q4 + 1) * 512], in_=pst[:], func=AF.Copy),
                             reads=[ptk], writes=[yk + "_%d" % q4])
                    else:
                        P.op("dve", lambda e, pst=pst, q4=q4, yo=yo: e.tensor_copy(out=yo[:, q4 * 512:(q4 + 1) * 512], in_=pst[:]),
                             reads=[ptk], writes=[yk + "_%d" % q4])
                if b < NLB:
                    dst = y_lat[t0 + tt * 128:t0 + (tt + 1) * 128, :]
                else:
                    dst = y_ctx[tt * 128:(tt + 1) * 128, :]
                out_toks.append(P.dma("sp", "st_" + yk, lambda e, yo=yo, dst=dst: e.dma_start(out=dst, in_=yo[:]),
                                      reads=[yk + "_%d" % q for q in range(4)], writes=["y%d_%d" % (b, tt)]))

    def finish_rings(last):
        R = {}
        if not last:
            R["ps_ss"] = PsRing(psb, [6])
            R["sq"] = Ring(P, "fsq", 3, [128, 512], BF16)
            R["rstd"] = Ring(P, "frs", 2, [128, 512], F32)
            R["hst"] = Ring(P, "fhst", 1, [128, 16, 512], BF16)
            R["tmp"] = Ring(P, "ftmp", 3, [128, 512], F32)
        else:
            R["yo"] = Ring(P, "fyo", 2, [128, D], F32)
            R["ps_t"] = PsRing(psb, [6, 7])
        return R

    P.begin()
    R0 = finish_rings(False)
    xin_ring = Ring(P, "xin", 2, [128, 4, D], F32)
    xb_ring = Ring(P, "xblk", 1, [128, 16, 512], F32)
    ps_t0 = PsRing(psb, [0, 1, 2, 3])
    for b in range(NBLK):
        xin, xik = xin_ring.next()
        src = x_lat[b * 512:(b + 1) * 512, :] if b < NLB else x_ctx
        P.dma("sp", xik, lambda e, xin=xin, src=src: e.dma_start(out=xin[:], in_=src.rearrange("(t p) f -> p t f", p=128)),
              writes=[xik])
        xblk, xk = xb_ring.next()
        for j in range(16):
            pst, ptk = ps_t0.next()
            def tr(e, pst=pst, j=j, xin=xin):
                ins = None
                for tt in range(4):
                    ins = e.transpose(out=pst[:, tt * 128:(tt + 1) * 128], in_=xin[:, tt, j * 128:(j + 1) * 128], identity=ident[:])
                return ins
            P.op("pe", tr, reads=[xik, "ident"], writes=[ptk])
            if j % 2 == 0:
                P.op("act", lambda e, pst=pst, j=j, xblk=xblk: e.activation(out=xblk[:, j, :], in_=pst[:], func=AF.Copy),
                     reads=[ptk], writes=[xk + "_%d" % j])
            else:
                P.op("dve", lambda e, pst=pst, j=j, xblk=xblk: e.tensor_copy(out=xblk[:, j, :], in_=pst[:]),
                     reads=[ptk], writes=[xk + "_%d" % j])
        finish_block(xblk, [xk + "_%d" % j for j in range(16)], b, 0, R0, xk)
    P.end()

    GROUPS = []
    g0 = (NBLK + 1) // 2
    GROUPS.append(list(range(0, g0)))
    GROUPS.append(list(range(g0, NBLK)))

    for l in range(DEPTH):
        for grp in GROUPS:
            P.begin()
            G = len(grp) * 512
            tg0 = grp[0] * 512
            hT = P.sbuf("hT", [128, 16, G], BF16)
            cs_t = P.sbuf("cs_t", [128, G], F32)
            sn_t = P.sbuf("sn_t", [128, G], F32)
            for q4 in range(4):
                P.dma("sp", "ld_h%d" % q4, lambda e, q4=q4: e.dma_start(
                    out=hT[:, q4 * 4:(q4 + 1) * 4, :], in_=hT_d[q4 * 4:(q4 + 1) * 4, :, tg0:tg0 + G].rearrange("j p t -> p j t")),
                    writes=["hT"])
            P.dma("sp", "ld_cs", lambda e: e.dma_start(out=cs_t[:], in_=cosT[:, tg0:tg0 + G]), writes=["cs"])
            P.dma("sp", "ld_sn", lambda e: e.dma_start(out=sn_t[:], in_=sinT[:, tg0:tg0 + G]), writes=["sn"])
            wring = Ring(P, "wfm", 4, [128, 16, 128], BF16)
            stg = Ring(P, "stg", 3, [128, G], BF16)
            rt1 = Ring(P, "rt1", 2, [128, 512], F32)
            rt2 = Ring(P, "rt2", 2, [128, 512], F32)
            psr = PsRing(psb, [0, 1, 2, 3, 4, 5])
            ei = 0
            wi = 0
            while wi < len(FM):
                name, kind, _ = FM[wi]
                wb, wk = wring.next()
                P.dma("pool", wk, lambda e, wb=wb, wi=wi: e.dma_start(out=wb[:], in_=w_fm[l, wi]), writes=[wk])
                if kind == "rope":
                    wb2, wk2 = wring.next()
                    P.dma("pool", wk2, lambda e, wb2=wb2, wi=wi: e.dma_start(out=wb2[:], in_=w_fm[l, wi + 1]), writes=[wk2])
                st, sk_ = stg.next()
                for bi, b in enumerate(grp):
                    c0 = bi * 512
                    ps, pk = psr.next()
                    mm_group(ps[:], [(wb[:, k, :], hT[:, k, c0:c0 + 512]) for k in range(16)], [wk, "hT"], [pk])
                    wkey = sk_ + "_%d" % bi
                    if kind == "rope":
                        ps2, pk2 = psr.next()
                        mm_group(ps2[:], [(wb2[:, k, :], hT[:, k, c0:c0 + 512]) for k in range(16)], [wk2, "hT"], [pk2])
                        t1, t1k = rt1.next()
                        t2, t2k = rt2.next()
                        P.op("dve", lambda e, t1=t1, ps=ps, c0=c0: e.tensor_tensor(out=t1[:], in0=ps[:], in1=cs_t[:, c0:c0 + 512], op=ALU.mult),
                             reads=[pk, "cs"], writes=[t1k])
                        P.op("dve", lambda e, t2=t2, ps2=ps2, c0=c0: e.tensor_tensor(out=t2[:], in0=ps2[:], in1=sn_t[:, c0:c0 + 512], op=ALU.mult),
                             reads=[pk2, "sn"], writes=[t2k])
                        P.op("pool", lambda e, t1=t1, t2=t2, st=st, c0=c0: e.tensor_tensor(out=st[:, c0:c0 + 512], in0=t1[:], in1=t2[:], op=ALU.add),
                             reads=[t1k, t2k], writes=[wkey])
                    elif kind in ("silu", "sig"):
                        fn_ = AF.Silu if kind == "silu" else AF.Sigmoid
                        P.op("act", lambda e, ps=ps, st=st, c0=c0, fn_=fn_: e.activation(out=st[:, c0:c0 + 512], in_=ps[:], func=fn_),
                             reads=[pk], writes=[wkey])
                    else:
                        sc = (128.0 ** -0.5) if kind == "copys" else 1.0
                        if ei % 2 == 0:
                            P.op("dve", lambda e, ps=ps, st=st, c0=c0, sc=sc: e.tensor_scalar(
                                out=st[:, c0:c0 + 512], in0=ps[:], scalar1=sc, scalar2=None, op0=ALU.mult), reads=[pk], writes=[wkey])
                        else:
                            P.op("act", lambda e, ps=ps, st=st, c0=c0, sc=sc: e.activation(out=st[:, c0:c0 + 512], in_=ps[:], func=AF.Copy, scale=sc),
                                 reads=[pk], writes=[wkey])
                        ei += 1
                slot = FM_SLOT[name]
                P.dma("sp", "st_" + sk_, lambda e, st=st, slot=slot: e.dma_start(out=zfm_d[slot, :, tg0:tg0 + G], in_=st[:]),
                      reads=[sk_ + "_%d" % bi for bi in range(len(grp))], writes=["zfm%d" % slot])
                wi += 2 if kind == "rope" else 1
            wtm_ring = Ring(P, "wtm", 2, [128, 16, 512], BF16)
            tms = Ring(P, "tms", 2, [128, 4, 512], BF16)
            of_ring = Ring(P, "ofr", 3, [128, 512], F32)
            sm_ring = Ring(P, "smr", 4, [128, 2], F32)
            has_ctx = NLB in grp
            kvrow = None
            if has_ctx:
                kvrow = P.sbuf("kvrow", [128, 256], F32)
                P.dma("sp", "ld_kvrow", lambda e: e.dma_start(out=kvrow[:], in_=kvn_row[:, l, :]), writes=["kvrow"])
            for (gname, gcols, gwho) in TMG:
                if gwho == "ctx" and not has_ctx:
                    continue
                if gname in SKIP:
                    continue
                co, ncol = TM_OFF[gname]
                wb, wk = wtm_ring.next()
                P.dma("pool", wk, lambda e, wb=wb, co=co, ncol=ncol: e.dma_start(out=wb[:, :, 0:ncol], in_=w_tm[l, :, :, co:co + ncol]),
                      writes=[wk])
                blocks = grp if gwho == "all" else [NLB]
                for b in blocks:
                    bi = grp.index(b)
                    is_ctx = (b == NLB)
                    ts_, tsk = tms.next()
                    for tt in range(4):
                        c0 = bi * 512 + tt * 128
                        ps, pk = psr.next()
                        mm_group(ps[:, 0:ncol], [(hT[:, k, c0:c0 + 128], wb[:, k, 0:ncol]) for k in range(16)], [wk, "hT"], [pk])
                        ctx_out = is_ctx and gname != "rk" and gname != "rv" and "ctxout" not in SKIP and (gname + "_out") not in SKIP
                        if gwho == "all" and not ctx_out:
                            sc = (128.0 ** -0.5) if gname == "rk" else 1.0
                            if tt % 2 == 0:
                                P.op("dve", lambda e, ps=ps, ts_=ts_, tt=tt, sc=sc, ncol=ncol: e.tensor_scalar(
                                    out=ts_[:, tt, 0:ncol], in0=ps[:, 0:ncol], scalar1=sc, scalar2=None, op0=ALU.mult),
                                    reads=[pk], writes=[tsk + "_%d" % tt])
                            else:
                                P.op("act", lambda e, ps=ps, ts_=ts_, tt=tt, sc=sc, ncol=ncol: e.activation(
                                    out=ts_[:, tt, 0:ncol], in_=ps[:, 0:ncol], func=AF.Copy, scale=sc), reads=[pk], writes=[tsk + "_%d" % tt])
                        if ctx_out:
                            cb, r0 = tt // 2, (tt % 2) * 128
                            if gname == "o1":
                                of, ofk = of_ring.next()
                                sm, smk = sm_ring.next()
                                P.op("act", lambda e, ps=ps, of=of, sm=sm: e.activation(out=of[:, 0:256], in_=ps[:, 0:256], func=AF.Square,
                                                                                        accum_out=sm[:, 0:1]), reads=[pk], writes=[ofk, smk])
                                P.op("act", lambda e, sm=sm: e.activation(out=sm[:, 1:2], in_=sm[:, 0:1], func=AF.Sqrt, bias=eps_t[:, :], scale=1.0 / 256),
                                     reads=[smk], writes=[smk])
                                P.op("dve", lambda e, sm=sm: e.reciprocal(out=sm[:, 1:2], in_=sm[:, 1:2]), reads=[smk], writes=[smk])
                                P.op("dve", lambda e, ps=ps, of=of, sm=sm: e.scalar_tensor_tensor(
                                    out=of[:, 0:256], in0=ps[:, 0:256], scalar=sm[:, 1:2], in1=kvrow[:], op0=ALU.mult, op1=ALU.mult),
                                    reads=[pk, smk, "kvrow", ofk], writes=[ofk])
                                P.op("dve", lambda e, ps=ps, of=of: e.tensor_copy(out=of[:, 256:448], in_=ps[:, 256:448]),
                                     reads=[pk, ofk], writes=[ofk])
                                for (dst, a, w_) in ((o_ckv, 0, 256), (o_kr, 256, 64), (o_sk, 320, 128)):
                                    out_toks.append(P.dma("sp", "st_" + ofk, lambda e, of=of, dst=dst, a=a, w_=w_, cb=cb, r0=r0: e.dma_start(
                                        out=dst[cb, l, r0:r0 + 128, :], in_=of[:, a:a + w_]), reads=[ofk], writes=["o_%d_%d" % (a, tt)]))
                            else:
                                dst = dict(nk=o_nk, nv=o_nv, sv=o_sv)[gname]
                                of, ofk = of_ring.next()
                                P.op("act" if tt % 2 == 0 else "dve",
                                     (lambda e, ps=ps, of=of, ncol=ncol: e.activation(out=of[:, 0:ncol], in_=ps[:, 0:ncol], func=AF.Copy)) if tt % 2 == 0 else
                                     (lambda e, ps=ps, of=of, ncol=ncol: e.tensor_copy(out=of[:, 0:ncol], in_=ps[:, 0:ncol])),
                                     reads=[pk], writes=[ofk])
                                if gwho == "all":
                                    P.op("pool", lambda e, of=of, ts_=ts_, tt=tt, ncol=ncol: e.tensor_copy(out=ts_[:, tt, 0:ncol], in_=of[:, 0:ncol]),
                                         reads=[ofk], writes=[tsk + "_%d" % tt])
                                out_toks.append(P.dma("sp", "st_" + ofk, lambda e, of=of, dst=dst, ncol=ncol, cb=cb, r0=r0: e.dma_start(
                                    out=dst[cb, l, r0:r0 + 128, :], in_=of[:, 0:ncol]), reads=[ofk], writes=["o_%s_%d" % (gname, tt)]))
                    if gwho == "all":
                        zo = ZTM[gname]
                        P.dma("sp", "st_" + tsk, lambda e, ts_=ts_, b=b, zo=zo, ncol=ncol: e.dma_start(
                            out=ztm_d[b * 512:(b + 1) * 512, zo:zo + ncol].rearrange("(t p) c -> p t c", p=128), in_=ts_[:, :, 0:ncol]),
                            reads=[tsk + "_%d" % tt for tt in range(4)], writes=["ztm_%s_%d" % (gname, b)])
            P.end()

        SEQS = [("lat", 0, S)] + [("ctx%d" % i, S + i * L, L) for i in range(2)]

        def attention(tag, nq, q_parts, key_tiles, m_out, scale, out_ps, sum_ps, R, po=0):
            (ops, opk), (sps, spk) = out_ps, sum_ps
            nk = len(key_tiles)
            LA = 2
            pend = {}
            for kk in range(nk + LA):
                if kk < nk:
                    (kparts, vl, bias, rk_) = key_tiles[kk]
                    ps, pk = R["ps_s"].next()
                    mm_group(ps[:, 0:nq], [(kp, qp) for kp, (qp, _) in zip(kparts, q_parts)],
                             list(rk_) + [qk for _, qk in q_parts], [pk])
                    ex, exk = R["ex"].next()
                    if bias is None:
                        P.op("act", lambda e, ps=ps, ex=ex: e.activation(out=ex[:, 0:nq], in_=ps[:, 0:nq], func=AF.Exp, scale=scale),
                             reads=[pk], writes=[exk])
                    else:
                        bt, btk = bias
                        tb, tbk = R["tb"].next()
                        P.op("dve", lambda e, ps=ps, tb=tb, bt=bt: e.scalar_tensor_tensor(
                            out=tb[:, 0:nq], in0=ps[:, 0:nq], scalar=scale, in1=bt, op0=ALU.mult, op1=ALU.add),
                            reads=[pk, btk], writes=[tbk])
                        P.op("act", lambda e, tb=tb, ex=ex: e.activation(out=ex[:, 0:nq], in_=tb[:, 0:nq], func=AF.Exp),
                             reads=[tbk], writes=[exk])
                    pend[kk] = (ex, exk, vl, rk_)
                ki = kk - LA
                if ki >= 0:
                    (ex, exk, vl, rk_) = pend.pop(ki)
                    def pv(e, ex=ex, vl=vl, ki=ki):
                        e.matmul(ops[po:po + m_out, 0:nq], vl, ex[:, 0:nq], start=(ki == 0), stop=(ki == nk - 1))
                        return e.matmul(sps[po:po + m_out, 0:nq], ones_bf[:, 0:m_out], ex[:, 0:nq], start=(ki == 0), stop=(ki == nk - 1))
                    P.op("pe", pv, reads=[exk, "ones_bf"] + list(rk_), writes=[opk, spk])

        def finish_head(tag, nq, out_ps, sum_ps, gp_ap, gpk, dst_ap, R, esink_ap=None, rows=128, stkey=None):
            (ops, opk), (sps, spk) = out_ps, sum_ps
            rc, rck = R["rc"].next()
            if esink_ap is not None:
                for (p0, ea) in esink_ap:
                    P.op("dve", lambda e, rc=rc, p0=p0, ea=ea: e.tensor_scalar(out=rc[p0:p0 + 64, 0:nq], in0=sps[p0:p0 + 64, 0:nq],
                                                                             scalar1=ea, scalar2=None, op0=ALU.add),
                         reads=[spk, "esink"], writes=[rck])
                P.op("dve", lambda e, rc=rc: e.reciprocal(out=rc[0:rows, 0:nq], in_=rc[0:rows, 0:nq]), reads=[rck], writes=[rck])
            else:
                P.op("dve", lambda e, rc=rc: e.reciprocal(out=rc[0:rows, 0:nq], in_=sps[0:rows, 0:nq]), reads=[spk], writes=[rck])
            P.op("dve", lambda e, rc=rc: e.tensor_tensor(out=rc[0:rows, 0:nq], in0=ops[0:rows, 0:nq], in1=rc[0:rows, 0:nq], op=ALU.mult),
                 reads=[opk, rck], writes=[rck])
            ob, obk = R["ob"].next()
            P.op("pool", lambda e, rc=rc, ob=ob: e.tensor_tensor(out=ob[0:rows, 0:nq], in0=rc[0:rows, 0:nq], in1=gp_ap, op=ALU.mult),
                 reads=[rck, gpk], writes=[obk])
            P.dma("pool", "st_" + obk, lambda e, ob=ob: e.dma_start(out=dst_ap, in_=ob[0:rows, 0:nq]), reads=[obk], writes=[stkey])

        def load_gp(R, slot, t0, nq):
            gp, gpk = R["gp"].next()
            P.dma("sp", gpk, lambda e, gp=gp: e.dma_start(out=gp[:, 0:nq], in_=zfm_d[slot, :, t0:t0 + nq]), writes=[gpk])
            return gp, gpk

        def att_rings(P, with_tb):
            R = dict(ps_s=PsRing(psb, [0, 1, 2]), ex=Ring(P, "ex", 4, [128, 512], BF16), rc=Ring(P, "rc", 2, [128, 512], F32),
                     ob=Ring(P, "ob", 2, [128, 512], BF16), gp=Ring(P, "gp", 2, [128, 512], BF16),
                     ps_o=PsRing(psb, [3, 4]), ps_d=PsRing(psb, [5, 6]))
            if with_tb:
                R["tb"] = Ring(P, "tb", 3, [128, 512], F32)
            return R

        for (sname, s0, slen) in ([] if "mla" in SKIP else SEQS):
            is_lat = sname == "lat"
            nkey = slen + (PAST if is_lat else 0)
            nkt = nkey // 128
            P.begin()
            R = att_rings(P, False)
            wq = P.sbuf("wq", [128, 4, 1024], BF16)
            wkv = P.sbuf("wkv", [128, 2, 1024], BF16)
            ckvT = P.sbuf("ckvT", [128, 2, nkey], BF16)
            krT = P.sbuf("krT", [64, nkey], BF16)
            knT = P.sbuf("knT", [128, nkey], BF16)
            vall = P.sbuf("vall", [128, nkt, 512], BF16)
            P.dma("pool", "ld_wq", lambda e: e.dma_start(out=wq[:], in_=w_qup[l]), writes=["wq"])
            P.dma("pool", "ld_wkv", lambda e: e.dma_start(out=wkv[:], in_=w_kvup[l]), writes=["wkv"])
            P.dma("sp", "ld_kr", lambda e: e.dma_start(out=krT[:, 0:slen], in_=zfm_d[FM_SLOT["kr"], 0:64, s0:s0 + slen]), writes=["krT"])
            kva_ring = Ring(P, "kva", 2, [128, 2, 512], BF16)
            sq_ring = Ring(P, "msq", 2, [128, 4, 512], BF16)
            rs_ring = Ring(P, "mrs", 2, [128, 512], F32)
            psm = PsRing(psb, [7])
            nch = (slen + 511) // 512
            for ch in range(nch):
                n = min(512, slen - ch * 512)
                t0 = s0 + ch * 512
                kv, kvk = kva_ring.next()
                P.dma("sp", kvk, lambda e, kv=kv, t0=t0, n=n: e.dma_start(
                    out=kv[:, :, 0:n], in_=zfm_d[FM_SLOT["kva0"]:FM_SLOT["kva0"] + 2, :, t0:t0 + n].rearrange("j p t -> p j t")), writes=[kvk])
                sq, sqk = sq_ring.next()
                P.op("act", lambda e, kv=kv, sq=sq, n=n: e.activation(out=sq[:, 0:2, 0:n], in_=kv[:, :, 0:n], func=AF.Square), reads=[kvk], writes=[sqk])
                ps, pk = psm.next()
                mm_group(ps[:, 0:n], [(ones_bf[:], sq[:, j, 0:n]) for j in range(2)], [sqk, "ones_bf"], [pk])
                rs, rsk = rs_ring.next()
                rstd_from_ssq(ps[:, 0:n], rs[:, 0:n], 256, [pk], [rsk], None)
                for j in range(2):
                    P.op("dve", lambda e, kv=kv, rs=rs, j=j, n=n, ch=ch: e.scalar_tensor_tensor(
                        out=ckvT[:, j, ch * 512:ch * 512 + n], in0=kv[:, j, 0:n], scalar=kvn_s[:, l, j:j + 1], in1=rs[:, 0:n],
                        op0=ALU.mult, op1=ALU.mult), reads=[kvk, rsk, "kvn"], writes=["ckvT_%d" % ch])
            ckv_keys = ["ckvT_%d" % ch for ch in range(nch)]
            if is_lat:
                cc = P.sbuf("cc", [128, 4, 256], F32)
                ck = P.sbuf("ck", [128, 4, 64], F32)
                P.dma("sp", "ld_cc", lambda e: e.dma_start(out=cc[:], in_=c_ckv[l].rearrange("(t p) f -> p t f", p=128)), writes=["cc"])
                P.dma("sp", "ld_ck", lambda e: e.dma_start(out=ck[:], in_=c_kr[l].rearrange("(t p) f -> p t f", p=128)), writes=["ck"])
                for j in range(2):
                    ps, pk = psm.next()
                    def tr(e, ps=ps, j=j):
                        ins = None
                        for tt in range(4):
                            ins = e.transpose(out=ps[:, tt * 128:(tt + 1) * 128], in_=cc[:, tt, j * 128:(j + 1) * 128], identity=ident[:])
                        return ins
                    P.op("pe", tr, reads=["cc", "ident"], writes=[pk])
                    P.op("dve", lambda e, ps=ps, j=j: e.tensor_copy(out=ckvT[:, j, slen:slen + 512], in_=ps[:]), reads=[pk], writes=["ckvT_c%d" % j])
                ckv_keys += ["ckvT_c0", "ckvT_c1"]
                ps, pk = psm.next()
                def tr2(e, ps=ps):
                    ins = None
                    for tt in range(4):
                        ins = e.transpose(out=ps[0:64, tt * 128:(tt + 1) * 128], in_=ck[:, tt, :], identity=ident[:])
                    return ins
                P.op("pe", tr2, reads=["ck", "ident"], writes=[pk])
                P.op("dve", lambda e, ps=ps: e.tensor_copy(out=krT[:, slen:slen + 512], in_=ps[0:64, :]), reads=[pk, "krT"], writes=["krT"])
            for kt in range(nkt):
                ps, pk = psm.next()
                mm_group(ps[:], [(ckvT[:, j, kt * 128:(kt + 1) * 128], wkv[:, j, 512:1024]) for j in range(2)], ckv_keys + ["wkv"], [pk])
                if kt % 2 == 0:
                    P.op("act", lambda e, ps=ps, kt=kt: e.activation(out=vall[:, kt, :], in_=ps[:], func=AF.Copy), reads=[pk], writes=["vall_%d" % kt])
                else:
                    P.op("dve", lambda e, ps=ps, kt=kt: e.tensor_copy(out=vall[:, kt, :], in_=ps[:]), reads=[pk], writes=["vall_%d" % kt])
            vkeys = ["vall_%d" % kt for kt in range(nkt)]
            qa_ring = Ring(P, "qa", 2, [128, 4, 512], BF16)
            cq_ring = Ring(P, "cq", 2, [128, 4, 512], BF16)
            qn_ring = Ring(P, "qnp", 2, [128, 512], BF16)
            qr_ring = Ring(P, "qrp", 2, [64, 512], BF16)
            qt_ring = Ring(P, "qtp", 2, [64, 512], F32)
            cs2 = Ring(P, "cs2", 2, [64, 512], F32)
            sn2 = Ring(P, "sn2", 2, [64, 512], F32)
            nqb = (slen + 511) // 512
            for h in range(4):
                for ch in range((nkey + 511) // 512):
                    n = min(512, nkey - ch * 512)
                    ps, pk = psm.next()
                    mm_group(ps[:, 0:n], [(wkv[:, j, h * 128:(h + 1) * 128], ckvT[:, j, ch * 512:ch * 512 + n]) for j in range(2)],
                             ckv_keys + ["wkv"], [pk])
                    P.op("dve", lambda e, ps=ps, ch=ch, n=n: e.tensor_copy(out=knT[:, ch * 512:ch * 512 + n], in_=ps[:, 0:n]),
                         reads=[pk], writes=["knT"])
                for qb in range(nqb):
                    nq = min(512, slen - qb * 512)
                    t0 = s0 + qb * 512
                    qa, qak = qa_ring.next()
                    P.dma("sp", qak, lambda e, qa=qa, t0=t0, nq=nq: e.dma_start(
                        out=qa[:, :, 0:nq], in_=zfm_d[0:4, :, t0:t0 + nq].rearrange("j p t -> p j t")), writes=[qak])
                    sq, sqk = sq_ring.next()
                    P.op("act", lambda e, qa=qa, sq=sq, nq=nq: e.activation(out=sq[:, :, 0:nq], in_=qa[:, :, 0:nq], func=AF.Square), reads=[qak], writes=[sqk])
                    ps, pk = psm.next()
                    mm_group(ps[:, 0:nq], [(ones_bf[:], sq[:, j, 0:nq]) for j in range(4)], [sqk, "ones_bf"], [pk])
                    rs, rsk = rs_ring.next()
                    rstd_from_ssq(ps[:, 0:nq], rs[:, 0:nq], 512, [pk], [rsk], None)
                    cq, cqk = cq_ring.next()
                    for j in range(4):
                        P.op("dve", lambda e, qa=qa, cq=cq, rs=rs, j=j, nq=nq: e.scalar_tensor_tensor(
                            out=cq[:, j, 0:nq], in0=qa[:, j, 0:nq], scalar=qn_s[:, l, j:j + 1], in1=rs[:, 0:nq], op0=ALU.mult, op1=ALU.mult),
                            reads=[qak, rsk, "qn"], writes=[cqk])
                    c0 = h * 256
                    ps, pk = psm.next()
                    mm_group(ps[:, 0:nq], [(wq[:, j, c0:c0 + 128], cq[:, j, 0:nq]) for j in range(4)], [cqk, "wq"], [pk])
                    qnp, qnk = qn_ring.next()
                    P.op("act", lambda e, ps=ps, qnp=qnp, nq=nq: e.activation(out=qnp[:, 0:nq], in_=ps[:, 0:nq], func=AF.Copy), reads=[pk], writes=[qnk])
                    ps, pk = psm.next()
                    mm_group(ps[0:64, 0:nq], [(wq[:, j, c0 + 128:c0 + 192], cq[:, j, 0:nq]) for j in range(4)], [cqk, "wq"], [pk])
                    qrp, qrk = qr_ring.next()
                    if is_lat:
                        cs, csk = cs2.next()
                        sn, snk = sn2.next()
                        P.dma("sp", csk, lambda e, cs=cs, t0=t0, nq=nq: e.dma_start(out=cs[:, 0:nq], in_=cosT[0:64, t0:t0 + nq]), writes=[csk])
                        P.dma("sp", snk, lambda e, sn=sn, t0=t0, nq=nq: e.dma_start(out=sn[:, 0:nq], in_=sinT[0:64, t0:t0 + nq]), writes=[snk])
                        qt, qtk = qt_ring.next()
                        P.op("dve", lambda e, ps=ps, qt=qt, cs=cs, nq=nq: e.tensor_tensor(out=qt[:, 0:nq], in0=ps[0:64, 0:nq], in1=cs[:, 0:nq], op=ALU.mult),
                             reads=[pk, csk], writes=[qtk])
                        ps, pk = psm.next()
                        mm_group(ps[0:64, 0:nq], [(wq[:, j, c0 + 192:c0 + 256], cq[:, j, 0:nq]) for j in range(4)], [cqk, "wq"], [pk])
                        qt2, qt2k = qt_ring.next()
                        P.op("dve", lambda e, ps=ps, qt2=qt2, sn=sn, nq=nq: e.tensor_tensor(out=qt2[:, 0:nq], in0=ps[0:64, 0:nq], in1=sn[:, 0:nq], op=ALU.mult),
                             reads=[pk, snk], writes=[qt2k])
                        P.op("pool", lambda e, qt=qt, qt2=qt2, qrp=qrp, nq=nq: e.tensor_tensor(out=qrp[:, 0:nq], in0=qt[:, 0:nq], in1=qt2[:, 0:nq], op=ALU.add),
                             reads=[qtk, qt2k], writes=[qrk])
                    else:
                        P.op("dve", lambda e, ps=ps, qrp=qrp, nq=nq: e.tensor_copy(out=qrp[:, 0:nq], in_=ps[0:64, 0:nq]), reads=[pk], writes=[qrk])
                    ops_ = R["ps_o"].next()
                    sps_ = R["ps_d"].next()
                    kts = []
                    for kt in range(nkt):
                        kts.append(([knT[:, kt * 128:(kt + 1) * 128], krT[:, kt * 128:(kt + 1) * 128]],
                                    vall[:, kt, h * 128:(h + 1) * 128], None, ["knT", "krT", "vall_%d" % kt]))
                    attention("mla", nq, [(qnp[:, 0:nq], qnk), (qrp[:, 0:nq], qrk)], kts, 128, 192.0 ** -0.5, ops_, sps_, R)
                    gp, gpk = load_gp(R, FM_SLOT["gp%d" % h], t0, nq)
                    finish_head("mla", nq, ops_, sps_, gp[:, 0:nq], gpk, oT_d[h, :, t0:t0 + nq], R, stkey="oT_%d_%d" % (h, t0))
            P.end()

        for (sname, s0, slen) in ([] if "nat" in SKIP else SEQS):
            is_lat = sname == "lat"
            nkey = slen + (PAST if is_lat else 0)
            nkt = nkey // 128
            P.begin()
            R = att_rings(P, is_lat)
            kT = P.sbuf("nkT", [128, nkey], BF16)
            vt = P.sbuf("nvt", [128, nkt, 128], BF16)
            q_ring = Ring(P, "nq", 2, [128, 512], BF16)
            psm = PsRing(psb, [7])
            if is_lat:
                cK = P.sbuf("cK", [128, 4, 512], F32)
                P.dma("sp", "ld_cK", lambda e: e.dma_start(out=cK[:], in_=c_nk[l].rearrange("(t p) f -> p t f", p=128)), writes=["cK"])
                b_ring = Ring(P, "nb", 16, [128, 512], F32)
            nqb = (slen + 511) // 512
            for h in range(4):
                P.dma("sp", "ld_nkT", lambda e, h=h: e.dma_start(out=kT[:, 0:slen], in_=zfm_d[FM_SLOT["nk%d" % h], :, s0:s0 + slen]), writes=["nkT"])
                for c4 in range(0, slen // 128, 4):
                    n4 = min(4, slen // 128 - c4)
                    P.dma("sp", "ld_nvt%d" % ((c4 // 4) % 2), lambda e, h=h, c4=c4, n4=n4: e.dma_start(
                        out=vt[:, c4:c4 + n4, :],
                        in_=ztm_d[s0 + c4 * 128:s0 + (c4 + n4) * 128, ZTM["nv"] + h * 128:ZTM["nv"] + (h + 1) * 128].rearrange("(t p) c -> p t c", p=128)),
                        reads=["nvt"], writes=["nvt"])
                if is_lat:
                    P.dma("pool", "ld_nvc", lambda e, h=h: e.dma_start(
                        out=vt[:, slen // 128:nkt, :], in_=c_nv[l, :, h * 128:(h + 1) * 128].rearrange("(t p) c -> p t c", p=128)),
                        reads=["nvt"], writes=["nvt"])
                    ps, pk = psm.next()
                    def tr(e, ps=ps, h=h):
                        ins = None
                        for tt in range(4):
                            ins = e.transpose(out=ps[:, tt * 128:(tt + 1) * 128], in_=cK[:, tt, h * 128:(h + 1) * 128], identity=ident[:])
                        return ins
                    P.op("pe", tr, reads=["cK", "ident"], writes=[pk])
                    P.op("dve", lambda e, ps=ps: e.tensor_copy(out=kT[:, slen:slen + 512], in_=ps[:]), reads=[pk, "nkT"], writes=["nkT"])
                for qb in range(nqb):
                    nq = min(512, slen - qb * 512)
                    t0 = s0 + qb * 512
                    q, qk = q_ring.next()
                    P.dma("sp", qk, lambda e, q=q, h=h, t0=t0, nq=nq: e.dma_start(out=q[:, 0:nq], in_=zfm_d[FM_SLOT["nq%d" % h], :, t0:t0 + nq]), writes=[qk])
                    kts = []
                    if is_lat:
                        lo = max(0, 8 * qb - 4)
                        hi = min(ROWS, 8 * qb + 12)
                        cls = 0 if qb == 0 else (2 if qb == NQB - 1 else 1)
                        for ti in range((hi - lo) // 2):
                            k0 = (lo + 2 * ti) * 64
                            bt, btk = b_ring.next()
                            P.dma("sp", btk, lambda e, bt=bt, cls=cls, ti=ti, h=h: e.dma_start(out=bt[:], in_=nat_bias[l, cls, ti, h]), writes=[btk])
                            kts.append(([kT[:, k0:k0 + 128]], vt[:, k0 // 128, :], (bt[:], btk), ["nkT", "nvt"]))
                        for ti in range(4):
                            k0 = slen + ti * 128
                            kts.append(([kT[:, k0:k0 + 128]], vt[:, k0 // 128, :], None, ["nkT", "nvt"]))
                    else:
                        for kt in range(nkt):
                            kts.append(([kT[:, kt * 128:(kt + 1) * 128]], vt[:, kt, :], None, ["nkT", "nvt"]))
                    ops_ = R["ps_o"].next()
                    sps_ = R["ps_d"].next()
                    attention("nat", nq, [(q[:, 0:nq], qk)], kts, 128, 128.0 ** -0.5, ops_, sps_, R)
                    gp, gpk = load_gp(R, FM_SLOT["gp%d" % (8 + h)], t0, nq)
                    finish_head("nat", nq, ops_, sps_, gp[:, 0:nq], gpk, oT_d[8 + h, :, t0:t0 + nq], R, stkey="oT_%d_%d" % (8 + h, t0))
            P.end()

        for (sname, s0, slen) in ([] if "swa" in SKIP else SEQS):
            is_lat = sname == "lat"
            nkey = slen + (PAST if is_lat else 0)
            nkt = nkey // 128
            P.begin()
            R = att_rings(P, is_lat)
            kT = P.sbuf("skT", [128, nkey], BF16)
            vt = P.sbuf("svt", [128, nkt, 64], BF16)
            q_ring = Ring(P, "sq", 2, [128, 512], BF16)
            psm = PsRing(psb, [7])
            if is_lat:
                cK = P.sbuf("scK", [128, 4, 2, 2, 64], F32)
                for dd in range(2):
                    for g_ in range(2):
                        P.dma("sp", "ld_scK%d" % dd, lambda e, dd=dd, g_=g_: e.dma_start(
                            out=cK[:, :, g_, dd, :], in_=c_sk[l, :, g_ * 64:(g_ + 1) * 64].rearrange("(t p) c -> p t c", p=128)),
                            reads=["scK%d" % dd], writes=["scK%d" % dd])
                mk = P.sbuf("smask", [128, 6, 512], F32)
                P.dma("sp", "ld_smask", lambda e: e.dma_start(out=mk[:], in_=swa_mask.rearrange("o p q -> p o q")), writes=["smask"])
            nqb = (slen + 511) // 512
            for g in range(2):
                P.dma("sp", "ld_skT", lambda e, g=g: e.dma_start(out=kT[:, 0:slen], in_=zfm_d[FM_SLOT["sk%d" % g], :, s0:s0 + slen]), writes=["skT"])
                for c4 in range(0, slen // 128, 4):
                    n4 = min(4, slen // 128 - c4)
                    P.dma("sp", "ld_svt%d" % ((c4 // 4) % 2), lambda e, g=g, c4=c4, n4=n4: e.dma_start(
                        out=vt[:, c4:c4 + n4, :],
                        in_=ztm_d[s0 + c4 * 128:s0 + (c4 + n4) * 128, ZTM["sv"] + g * 64:ZTM["sv"] + (g + 1) * 64].rearrange("(t p) c -> p t c", p=128)),
                        reads=["svt"], writes=["svt"])
                if is_lat:
                    P.dma("pool", "ld_svc", lambda e, g=g: e.dma_start(
                        out=vt[:, slen // 128:nkt, :], in_=c_sv[l, :, g * 64:(g + 1) * 64].rearrange("(t p) c -> p t c", p=128)),
                        reads=["svt"], writes=["svt"])
                    ps, pk = psm.next()
                    def tr(e, ps=ps, g=g):
                        ins = None
                        for tt in range(4):
                            ins = e.transpose(out=ps[:, tt * 128:(tt + 1) * 128], in_=cK[:, tt, g].rearrange("p d c -> p (d c)"), identity=ident[:])
                        return ins
                    P.op("pe", tr, reads=["scK0", "scK1", "ident"], writes=[pk])
                    P.op("dve", lambda e, ps=ps: e.tensor_copy(out=kT[:, slen:slen + 512], in_=ps[:]), reads=[pk, "skT"], writes=["skT"])
                for ti2 in range(2):
                    tq = g * 2 + ti2
                    for qb in range(nqb):
                        nq = min(512, slen - qb * 512)
                        t0 = s0 + qb * 512
                        q, qk = q_ring.next()
                        P.dma("sp", qk, lambda e, q=q, tq=tq, t0=t0, nq=nq: e.dma_start(out=q[:, 0:nq], in_=zfm_d[FM_SLOT["sq%d" % tq], :, t0:t0 + nq]), writes=[qk])
                        ops_ = R["ps_o"].next()
                        sps_ = R["ps_d"].next()
                        for hh in range(2):
                            p0 = hh * 64
                            kts = []
                            if is_lat:
                                for o in range(-1, 5):
                                    kb = 4 * qb + o
                                    if kb < 0 or kb >= slen // 128:
                                        continue
                                    kts.append(([kT[p0:p0 + 64, kb * 128:(kb + 1) * 128]], vt[:, kb, :], (mk[:, o + 1, :], "smask"), ["skT", "svt"]))
                                for ti in range(4):
                                    kb = slen // 128 + ti
                                    kts.append(([kT[p0:p0 + 64, kb * 128:(kb + 1) * 128]], vt[:, kb, :], None, ["skT", "svt"]))
                            else:
                                for kt in range(nkt):
                                    kts.append(([kT[p0:p0 + 64, kt * 128:(kt + 1) * 128]], vt[:, kt, :], None, ["skT", "svt"]))
                            attention("swa", nq, [(q[p0:p0 + 64, 0:nq], qk)], kts, 64, 64.0 ** -0.5, ops_, sps_, R, po=p0)
                        gp, gpk = load_gp(R, FM_SLOT["gp%d" % (12 + tq)], t0, nq)
                        es = [(0, esink_s[0:64, l, 2 * tq:2 * tq + 1]), (64, esink_s[64:128, l, 2 * tq + 1:2 * tq + 2])]
                        finish_head("swa", nq, ops_, sps_, gp[:, 0:nq], gpk, oT_d[12 + tq, :, t0:t0 + nq], R, esink_ap=es,
                                    stkey="oT_%d_%d" % (12 + tq, t0))
            P.end()

        P.begin()
        rtab = P.sbuf("rtab", [128, 4, 128], F32)
        rcol = P.sbuf("rcol", [128, 2], F32)
        P.dma("sp", "ld_rtab", lambda e: e.dma_start(out=rtab[:], in_=ret_tab.rearrange("a p q -> p a q")), writes=["rtab"])
        P.dma("sp", "ld_rcol", lambda e: e.dma_start(out=rcol[:], in_=ret_col), writes=["rcol"])
        maskT = P.sbuf("maskT", [128, 4, 128], F32)
        qdec = P.sbuf("qdec", [128, 4, 2, 128], F32)
        kdec = P.sbuf("kdec", [128, 4, 2], F32)
        cdec = P.sbuf("cdec", [128, 8], F32)
        rtmp = P.sbuf("rtmp", [128, 128], F32)
        for h in range(4):
            lf = lg_s[:, l, h:h + 1]
            lb = lg_s[:, l, 4 + h:5 + h]
            P.op("dve", lambda e, lf=lf: e.tensor_scalar(out=rtmp[:], in0=rtab[:, 0, :], scalar1=lf, scalar2=None, op0=ALU.mult),
                 reads=["rtab", "lg"], writes=["rtmp"])
            P.op("dve", lambda e, lb=lb: e.scalar_tensor_tensor(out=rtmp[:], in0=rtab[:, 1, :], scalar=lb, in1=rtmp[:], op0=ALU.mult, op1=ALU.add),
                 reads=["rtab", "lg", "rtmp"], writes=["rtmp"])
            P.op("act", lambda e, h=h: e.activation(out=maskT[:, h, :], in_=rtmp[:], func=AF.Exp), reads=["rtmp"], writes=["maskT"])
            P.op("act", lambda e, h=h, lf=lf: e.activation(out=qdec[:, h, 0, :], in_=rtab[:, 2, :], func=AF.Exp, scale=lf), reads=["rtab", "lg"], writes=["qdec"])
            P.op("act", lambda e, h=h, lb=lb: e.activation(out=qdec[:, h, 1, :], in_=rtab[:, 3, :], func=AF.Exp, scale=lb), reads=["rtab", "lg"], writes=["qdec"])
            P.op("act", lambda e, h=h, lf=lf: e.activation(out=kdec[:, h, 0:1], in_=rcol[:, 0:1], func=AF.Exp, scale=lf), reads=["rcol", "lg"], writes=["kdec"])
            P.op("act", lambda e, h=h, lb=lb: e.activation(out=kdec[:, h, 1:2], in_=rcol[:, 1:2], func=AF.Exp, scale=lb), reads=["rcol", "lg"], writes=["kdec"])
        P.op("act", lambda e: e.activation(out=cdec[:], in_=lg_s[:, l, :], func=AF.Exp, scale=128.0), reads=["lg"], writes=["cdec"])
        MAXC = S // 128
        qT = P.sbuf("rqT", [128, S], BF16)
        kT = P.sbuf("rkT", [128, S], BF16)
        ktm = P.sbuf("rktm", [128, MAXC, 128], BF16)
        vtm = P.sbuf("rvtm", [128, MAXC, 128], BF16)
        sb_all = P.sbuf("sb_all", [128, MAXC, 128], BF16)
        st_f = P.sbuf("st_f", [128, 128], F32)
        st_b = P.sbuf("st_b", [128, 128], F32)
        sf_bf = Ring(P, "sf_bf", 2, [128, 128], BF16)
        kd_ring = Ring(P, "kd", 3, [128, 128], BF16)
        pm_ring = Ring(P, "pm", 3, [128, 128], BF16)
        qf_ring = Ring(P, "qf", 3, [128, 2, 128], BF16)
        ro_ring = Ring(P, "ro", 2, [128, 512], F32)
        rsq_ring = Ring(P, "rsq", 2, [128, 512], BF16)
        rrs_ring = Ring(P, "rrs", 2, [128, 512], F32)
        rob_ring = Ring(P, "rob", 2, [128, 512], BF16)
        rgp_ring = Ring(P, "rgp", 2, [128, 512], BF16)
        ps_u = PsRing(psb, [0, 1])
        ps_i = PsRing(psb, [2, 3])
        ps_ro = PsRing(psb, [4, 5])
        ps_n = PsRing(psb, [6])
        for (sname, s0, slen) in SEQS:
            is_lat = sname == "lat"
            ncx = slen // 128
            for h in range(4):
                P.dma("pool" if "retpool" in SKIP else "sp", "ld_rqT", lambda e, h=h, s0=s0, slen=slen: e.dma_start(out=qT[:, 0:slen], in_=zfm_d[FM_SLOT["rq%d" % h], :, s0:s0 + slen]), writes=["rqT"])
                P.dma("pool" if "retpool" in SKIP else "sp", "ld_rkT", lambda e, h=h, s0=s0, slen=slen: e.dma_start(out=kT[:, 0:slen], in_=zfm_d[FM_SLOT["rk%d" % h], :, s0:s0 + slen]), writes=["rkT"])
                for c4 in range(0, ncx, 4):
                    n4 = min(4, ncx - c4)
                    for (nm_, dst_, zo_) in (("rktm", ktm, ZTM["rk"]), ("rvtm", vtm, ZTM["rv"])):
                        P.dma("sp", "ld_%s%d" % (nm_, (c4 // 4) % 2), lambda e, h=h, s0=s0, c4=c4, n4=n4, dst_=dst_, zo_=zo_: e.dma_start(
                            out=dst_[:, c4:c4 + n4, :],
                            in_=ztm_d[s0 + c4 * 128:s0 + (c4 + n4) * 128, zo_ + h * 128:zo_ + (h + 1) * 128].rearrange("(t p) c -> p t c", p=128)),
                            reads=[nm_], writes=[nm_])
                if is_lat:
                    P.dma("sp", "ld_stf", lambda e, h=h: e.dma_start(out=st_f[:], in_=c_st[l, 0, h]), writes=["st_f"])
                    P.dma("sp", "ld_stb", lambda e, h=h: e.dma_start(out=st_b[:], in_=c_st[l, 1, h]), writes=["st_b"])
                else:
                    P.op("dve", lambda e: e.memset(st_f[:], 0.0), writes=["st_f"])
                    P.op("dve", lambda e: e.memset(st_b[:], 0.0), writes=["st_b"])
                for c in range(ncx - 1, -1, -1):
                    P.op("act", lambda e, c=c: e.activation(out=sb_all[:, c, :], in_=st_b[:], func=AF.Copy), reads=["st_b"], writes=["sb_%d" % c])
                    kd, kdk = kd_ring.next()
                    P.op("dve", lambda e, kd=kd, c=c, h=h: e.tensor_scalar(out=kd[:], in0=ktm[:, c, :], scalar1=kdec[:, h, 1:2], scalar2=None, op0=ALU.mult),
                         reads=["rktm", "kdec"], writes=[kdk])
                    ps, pk = ps_u.next()
                    mm_group(ps[:, 0:128], [(kd[:], vtm[:, c, :])], [kdk, "rvtm"], [pk])
                    P.op("dve", lambda e, ps=ps, h=h: e.scalar_tensor_tensor(out=st_b[:], in0=st_b[:], scalar=cdec[:, 4 + h:5 + h], in1=ps[:, 0:128],
                                                                              op0=ALU.mult, op1=ALU.add), reads=["st_b", "cdec", pk], writes=["st_b"])
                if not is_lat:
                    cb = int(sname[3:])
                    out_toks.append(P.dma("sp", "st_sb", lambda e, cb=cb, h=h: e.dma_start(out=o_st[cb, l, 1, h], in_=st_b[:]), reads=["st_b"], writes=["o_stb"]))
                for c4 in range(0, ncx, 4):
                    ncg = min(4, ncx - c4)
                    nq = ncg * 128
                    t0 = s0 + c4 * 128
                    pso, psok = ps_ro.next()
                    for ci in range(ncg):
                        c = c4 + ci
                        sfb, sfk = sf_bf.next()
                        P.op("act", lambda e, sfb=sfb: e.activation(out=sfb[:], in_=st_f[:], func=AF.Copy), reads=["st_f"], writes=[sfk])
                        psi, psik = ps_i.next()
                        mm_group(psi[:, 0:128], [(kT[:, c * 128:(c + 1) * 128], qT[:, c * 128:(c + 1) * 128])], ["rkT", "rqT"], [psik])
                        pm, pmk = pm_ring.next()
                        P.op("dve", lambda e, psi=psi, pm=pm, h=h: e.tensor_tensor(out=pm[:], in0=psi[:, 0:128], in1=maskT[:, h, :], op=ALU.mult),
                             reads=[psik, "maskT"], writes=[pmk])
                        qf, qfk = qf_ring.next()
                        P.op("pool", lambda e, qf=qf, c=c, h=h: e.tensor_tensor(out=qf[:, 0, :], in0=qT[:, c * 128:(c + 1) * 128], in1=qdec[:, h, 0, :], op=ALU.mult),
                             reads=["rqT", "qdec"], writes=[qfk + "a"])
                        P.op("pool", lambda e, qf=qf, c=c, h=h: e.tensor_tensor(out=qf[:, 1, :], in0=qT[:, c * 128:(c + 1) * 128], in1=qdec[:, h, 1, :], op=ALU.mult),
                             reads=["rqT", "qdec"], writes=[qfk + "b"])
                        def omm(e, pso=pso, ci=ci, c=c, pm=pm, sfb=sfb, qf=qf):
                            o_ = pso[:, ci * 128:(ci + 1) * 128]
                            e.matmul(o_, vtm[:, c, :], pm[:], start=True, stop=False)
                            e.matmul(o_, sfb[:], qf[:, 0, :], start=False, stop=False)
                            return e.matmul(o_, sb_all[:, c, :], qf[:, 1, :], start=False, stop=True)
                        P.op("pe", omm, reads=["rvtm", pmk, sfk, qfk + "a", qfk + "b", "sb_%d" % c], writes=[psok])
                        kd, kdk = kd_ring.next()
                        P.op("dve", lambda e, kd=kd, c=c, h=h: e.tensor_scalar(out=kd[:], in0=ktm[:, c, :], scalar1=kdec[:, h, 0:1], scalar2=None, op0=ALU.mult),
                             reads=["rktm", "kdec"], writes=[kdk])
                        ps, pk = ps_u.next()
                        mm_group(ps[:, 0:128], [(kd[:], vtm[:, c, :])], [kdk, "rvtm"], [pk])
                        P.op("dve", lambda e, ps=ps, h=h: e.scalar_tensor_tensor(out=st_f[:], in0=st_f[:], scalar=cdec[:, h:h + 1], in1=ps[:, 0:128],
                                                                                  op0=ALU.mult, op1=ALU.add), reads=["st_f", "cdec", pk], writes=["st_f"])
                    ro, rok = ro_ring.next()
                    P.op("act", lambda e, ro=ro, pso=pso, nq=nq: e.activation(out=ro[:, 0:nq], in_=pso[:, 0:nq], func=AF.Copy), reads=[psok], writes=[rok])
                    rsq, rsqk = rsq_ring.next()
                    P.op("act", lambda e, ro=ro, rsq=rsq, nq=nq: e.activation(out=rsq[:, 0:nq], in_=ro[:, 0:nq], func=AF.Square), reads=[rok], writes=[rsqk])
                    psn, psnk = ps_n.next()
                    mm_group(psn[:, 0:nq], [(ones_bf[:], rsq[:, 0:nq])], [rsqk, "ones_bf"], [psnk])
                    rrs, rrsk = rrs_ring.next()
                    rstd_from_ssq(psn[:, 0:nq], rrs[:, 0:nq], 128, [psnk], [rrsk], None)
                    P.op("dve", lambda e, ro=ro, rrs=rrs, nq=nq, h=h: e.scalar_tensor_tensor(
                        out=ro[:, 0:nq], in0=ro[:, 0:nq], scalar=retn_s[:, l, h:h + 1], in1=rrs[:, 0:nq], op0=ALU.mult, op1=ALU.mult),
                        reads=[rok, rrsk, "retn"], writes=[rok])
                    gp, gpk = rgp_ring.next()
                    P.dma("sp", gpk, lambda e, gp=gp, h=h, t0=t0, nq=nq: e.dma_start(out=gp[:, 0:nq], in_=zfm_d[FM_SLOT["gp%d" % (4 + h)], :, t0:t0 + nq]), writes=[gpk])
                    rob, robk = rob_ring.next()
                    P.op("pool", lambda e, ro=ro, gp=gp, rob=rob, nq=nq: e.tensor_tensor(out=rob[:, 0:nq], in0=ro[:, 0:nq], in1=gp[:, 0:nq], op=ALU.mult),
                         reads=[rok, gpk], writes=[robk])
                    P.dma("sp", "st_" + robk, lambda e, rob=rob, h=h, t0=t0, nq=nq: e.dma_start(out=oT_d[4 + h, :, t0:t0 + nq], in_=rob[:, 0:nq]),
                          reads=[robk], writes=["oT_r%d_%d" % (h, t0)])
                if not is_lat:
                    cb = int(sname[3:])
                    out_toks.append(P.dma("sp", "st_sf", lambda e, cb=cb, h=h: e.dma_start(out=o_st[cb, l, 0, h], in_=st_f[:]), reads=["st_f"], writes=["o_stf"]))
        P.end()

        P.begin()
        last = (l == DEPTH - 1)
        RF = finish_rings(last)
        oT = Ring(P, "coT", 1, [128, 16, 512], BF16)
        gt = Ring(P, "cgt", 2, [128, 4, 512], BF16)
        wbr = Ring(P, "cwb", 3, [128, 4, 4, 128], BF16)
        wor = Ring(P, "cwo", 4, [128, 16, 128], BF16)
        tT = Ring(P, "ctT", 1, [128, 16, 512], BF16)
        tacc = Ring(P, "ctacc", 2, [128, 512], F32)
        ttmp = Ring(P, "cttmp", 3, [128, 512], F32)
        xb_ring = Ring(P, "cxb", 1, [128, 16, 512], F32)
        yb_ring = Ring(P, "cyb", 1, [128, 16, 512], F32)
        ysq = Ring(P, "cysq", 3, [128, 512], BF16)
        yrs = Ring(P, "cyrs", 2, [128, 512], F32)
        ps_u = PsRing(psb, [0, 1, 2])
        ps_y = PsRing(psb, [3, 4])
        ps_ys = PsRing(psb, [5])
        deferred = None
        for b in range(NBLK):
            t0 = b * 512
            cond = 0 if b < NLB else 1
            o_, ok = oT.next()
            for q4 in range(4):
                P.dma("sp", ok + "_%d" % q4, lambda e, o_=o_, q4=q4, t0=t0: e.dma_start(
                    out=o_[:, q4 * 4:(q4 + 1) * 4, :], in_=oT_d[q4 * 4:(q4 + 1) * 4, :, t0:t0 + 512].rearrange("j p t -> p j t")),
                    writes=[ok + "_%d" % q4])
            t_, tk = tT.next()
            wb_q = {}
            wo_q = {}

            def issue_wb(j):
                wb, wk = wbr.next()
                P.dma("pool", wk, lambda e, wb=wb, j=j: e.dma_start(out=wb[:], in_=w_br[l, j]), writes=[wk])
                wb_q[j] = (wb, wk)

            def issue_wo(i):
                wo, wok = wor.next()
                P.dma("pool", wok, lambda e, wo=wo, i=i: e.dma_start(out=wo[:], in_=w_o[l, i]), writes=[wok])
                wo_q[i] = (wo, wok)
            issue_wb(0)
            issue_wb(1)
            for j in range(16):
                g_, gk = gt.next()
                P.dma("sp", gk, lambda e, g_=g_, j=j, t0=t0: e.dma_start(
                    out=g_[:], in_=zfm_d[FM_SLOT["g0_0"]:FM_SLOT["g0_0"] + 64, :, t0:t0 + 512].rearrange("(n j) p t -> j p n t", j=16)[j]), writes=[gk])
                wb, wk = wb_q.pop(j)
                ta, tak = tacc.next()
                for n in range(4):
                    ps, pk = ps_u.next()
                    mm_group(ps[:], [(wb[:, n, k, :], o_[:, n * 4 + k, :]) for k in range(4)], [wk, ok + "_%d" % n], [pk])
                    if n == 0:
                        P.op("dve", lambda e, ps=ps, g_=g_, ta=ta, n=n: e.tensor_tensor(out=ta[:], in0=ps[:], in1=g_[:, n, :], op=ALU.mult),
                             reads=[pk, gk], writes=[tak])
                    else:
                        tm, tmk = ttmp.next()
                        P.op("dve", lambda e, ps=ps, g_=g_, tm=tm, n=n: e.tensor_tensor(out=tm[:], in0=ps[:], in1=g_[:, n, :], op=ALU.mult),
                             reads=[pk, gk], writes=[tmk])
                        if n < 3:
                            P.op("pool", lambda e, ta=ta, tm=tm: e.tensor_tensor(out=ta[:], in0=ta[:], in1=tm[:], op=ALU.add), reads=[tak, tmk], writes=[tak])
                        else:
                            P.op("pool", lambda e, ta=ta, tm=tm, t_=t_, j=j: e.tensor_tensor(out=t_[:, j, :], in0=ta[:], in1=tm[:], op=ALU.add),
                                 reads=[tak, tmk], writes=[tk + "_%d" % j])
                if j + 2 < 16:
                    issue_wb(j + 2)
                elif j == 14:
                    issue_wo(0)
                    issue_wo(1)
                else:
                    issue_wo(2)
            if deferred is not None:
                finish_block(*deferred)
                deferred = None
            xb, xk = xb_ring.next()
            for q4 in range(4):
                P.dma("sp", xk + "_%d" % q4, lambda e, xb=xb, q4=q4, t0=t0: e.dma_start(
                    out=xb[:, q4 * 4:(q4 + 1) * 4, :], in_=xT_d[q4 * 4:(q4 + 1) * 4, :, t0:t0 + 512].rearrange("j p t -> p j t")),
                    writes=[xk + "_t%d" % j for j in range(q4 * 4, q4 * 4 + 4)])
            yb, yk = yb_ring.next()
            pss, pssk = ps_ys.next()
            tkeys = [tk + "_%d" % j for j in range(16)]
            for i in range(16):
                wo, wok = wo_q.pop(i)
                if i + 3 < 16:
                    issue_wo(i + 3)
                ps, pk = ps_y.next()
                mm_group(ps[:], [(wo[:, j, :], t_[:, j, :]) for j in range(16)], [wok] + tkeys, [pk])
                P.op("dve", lambda e, ps=ps, yb=yb, i=i: e.tensor_copy(out=yb[:, i, :], in_=ps[:]), reads=[pk], writes=[yk + "_%d" % i])
                sq, sqk = ysq.next()
                P.op("act", lambda e, yb=yb, i=i, sq=sq: e.activation(out=sq[:], in_=yb[:, i, :], func=AF.Square), reads=[yk + "_%d" % i], writes=[sqk])
                P.op("pe", lambda e, sq=sq, i=i, pss=pss: e.matmul(pss[:], ones_bf[:], sq[:], start=(i == 0), stop=(i == 15)),
                     reads=[sqk, "ones_bf"], writes=[pssk])
            rs, rsk = yrs.next()
            rstd_from_ssq(pss[:], rs[:], D, [pssk], [rsk], None)
            for i in range(16):
                tm, tmk = ttmp.next()
                P.op("dve", lambda e, yb=yb, rs=rs, tm=tm, i=i: e.tensor_tensor(out=tm[:], in0=yb[:, i, :], in1=rs[:], op=ALU.mult),
                     reads=[yk + "_%d" % i, rsk], writes=[tmk])
                P.op("dve", lambda e, xb=xb, tm=tm, i=i, cond=cond: e.scalar_tensor_tensor(
                    out=xb[:, i, :], in0=tm[:], scalar=modG[:, l, cond, i:i + 1], in1=xb[:, i, :], op0=ALU.mult, op1=ALU.add),
                    reads=[tmk, xk + "_t%d" % i, "modG"], writes=[xk + "_t%d" % i])
            deferred = (xb, [xk + "_t%d" % i for i in range(16)], b, None if last else l + 1, RF, xk)
        finish_block(*deferred)
        P.end()

    P.close()
    return nc


def rope_tables(S, T):
    pos = np.arange(S)
    row = (pos // 64).astype(np.float32)
    col = (pos % 64).astype(np.float32)
    inv = (10000.0 ** (-np.arange(16, dtype=np.float32) / 16)).astype(np.float32)
    ang = np.concatenate([row[:, None] * inv[None], col[:, None] * inv[None]], axis=-1)
    cos = np.cos(ang).astype(np.float32)
    sin = np.sin(ang).astype(np.float32)
    cT = np.ones((128, T), np.float32)
    sT = np.zeros((128, T), np.float32)
    for p in range(128):
        d = p % 64
        f = d % 32
        cT[p, :S] = cos[:, f]
        sT[p, :S] = -sin[:, f] if d < 32 else sin[:, f]
    return cT, sT


def nat_bias_tables(rpb, rows):
    DEPTH = rpb.shape[0]
    nb = rows // 8
    out = np.full((DEPTH, 3, 8, 4, 128, 512), NEG, np.float32)
    for cls, b in ((0, 0), (1, 1), (2, nb - 1)):
        if cls == 1 and nb < 3:
            continue
        lo = max(0, 8 * b - 4)
        hi = min(rows, 8 * b + 12)
        i = np.arange(2)[:, None, None, None]
        kc = np.arange(64)[None, :, None, None]
        j = np.arange(8)[None, None, :, None]
        qc = np.arange(64)[None, None, None, :]
        for ti in range((hi - lo) // 2):
            kr = lo + 2 * ti + i
            r = 8 * b + j
            rs = np.clip(r - 4, 0, rows - 8)
            cs = np.clip(qc - 8, 0, 48)
            valid = (kr >= rs) & (kr < rs + 8) & (kc >= cs) & (kc < cs + 16)
            ro = np.clip(kr - r + 7, 0, 14)
            co = np.clip(kc - qc + 15, 0, 30)
            valid = np.broadcast_to(valid, (2, 64, 8, 64))
            ro = np.broadcast_to(ro, (2, 64, 8, 64))
            co = np.broadcast_to(co, (2, 64, 8, 64))
            g = rpb[:, :, ro, co]
            g = np.where(valid[None, None], g, np.float32(NEG))
            out[:, cls, ti] = g.reshape(DEPTH, 4, 128, 512)
    return out


def const_tables():
    k = np.arange(128)[:, None].astype(np.float32)
    q = np.arange(128)[None, :].astype(np.float32)
    Df = np.where(k <= q, q - k, 0.0).astype(np.float32)
    Ub = np.where(k > q, k - q, 0.0).astype(np.float32)
    i1 = np.broadcast_to(np.arange(128, dtype=np.float32)[None, :] + 1.0, (128, 128))
    i2 = np.broadcast_to(128.0 - np.arange(128, dtype=np.float32)[None, :], (128, 128))
    ret_tab = np.stack([Df, Ub, i1, i2]).astype(np.float32)
    ret_col = np.stack([127.0 - np.arange(128), np.arange(128)], axis=1).astype(np.float32)
    kk = np.arange(128)[:, None]
    qq = np.arange(512)[None, :]
    swa = np.stack([np.where(np.abs(qq - (o * 128 + kk)) <= 128, 0.0, NEG) for o in range(-1, 5)]).astype(np.float32)
    return ret_tab, ret_col, swa


def pcol(v, n):
    return np.ascontiguousarray(v.reshape(n, 128).T)


def host_weights(S, DEPTH, inp):
    T = S + 2 * L
    w = {}
    w_in = inp["w_in"]
    fm = np.empty((DEPTH, len(FM), 128, 16, 128), np.float32)
    for ti, (_, _, cols) in enumerate(FM):
        blk = w_in[:, :, cols]
        fm[:, ti] = blk.reshape(DEPTH, 16, 128, 128).transpose(0, 2, 1, 3)
    w["w_fm"] = fm
    tmcols = sum([c for _, c, _ in TMG], [])
    w["w_tm"] = np.ascontiguousarray(w_in[:, :, tmcols].reshape(DEPTH, 16, 128, TM_COLS).transpose(0, 2, 1, 3))
    w["w_mod"] = np.ascontiguousarray(inp["w_mod"].reshape(DEPTH, 16, 128, 24, 256).transpose(0, 3, 2, 1, 4))
    w["bmod"] = np.ascontiguousarray(np.stack([pcol(inp["b_mod"][l], 48) for l in range(DEPTH)], axis=1))
    w["npre"] = np.ascontiguousarray(np.stack([pcol(inp["norm_pre"][l], 16) for l in range(DEPTH)], axis=1))
    w["npost"] = np.ascontiguousarray(np.stack([pcol(inp["norm_post"][l], 16) for l in range(DEPTH)], axis=1))
    w["qn"] = np.ascontiguousarray(np.stack([pcol(inp["mla_q_norm"][l], 4) for l in range(DEPTH)], axis=1))
    w["kvn"] = np.ascontiguousarray(np.stack([pcol(inp["mla_kv_norm"][l], 2) for l in range(DEPTH)], axis=1))
    w["retn"] = np.ascontiguousarray(np.stack([pcol(inp["ret_norm"][l], 4) for l in range(DEPTH)], axis=1))
    w["kvn_row"] = np.ascontiguousarray(np.broadcast_to(inp["mla_kv_norm"][None], (128, DEPTH, 256)))
    w["decay"] = np.ascontiguousarray(np.broadcast_to(inp["ret_decay"].reshape(DEPTH, 8)[None], (128, DEPTH, 8)))
    w["sink"] = np.ascontiguousarray(np.broadcast_to(inp["swa_sink"][None], (128, DEPTH, 8)))
    qcols = []
    for h in range(4):
        b0 = h * 192
        qcols += list(range(b0, b0 + 128)) + list(range(b0 + 128, b0 + 192)) + _swap64(b0 + 128)
    w["w_qup"] = np.ascontiguousarray(inp["mla_w_q_up"][:, :, qcols].reshape(DEPTH, 4, 128, 1024).transpose(0, 2, 1, 3))
    kcols = []
    for h in range(4):
        kcols += list(range(h * 256, h * 256 + 128))
    for h in range(4):
        kcols += list(range(h * 256 + 128, h * 256 + 256))
    w["w_kvup"] = np.ascontiguousarray(inp["mla_w_kv_up"][:, :, kcols].reshape(DEPTH, 2, 128, 1024).transpose(0, 2, 1, 3))
    w["w_br"] = np.ascontiguousarray(inp["w_branch"].reshape(DEPTH, 4, 4, 128, 16, 128).transpose(0, 4, 3, 1, 2, 5))
    w["w_o"] = np.ascontiguousarray(inp["w_out"].reshape(DEPTH, 16, 128, 16, 128).transpose(0, 3, 2, 1, 4))
    w["nat_bias"] = nat_bias_tables(inp["nat_rpb"], S // 64)
    cT, sT = rope_tables(S, T)
    w["cosT"], w["sinT"] = cT, sT
    w["ident"] = np.eye(128, dtype=np.float32)
    rt, rc, sw = const_tables()
    w["ret_tab"], w["ret_col"], w["swa_mask"] = rt, rc, sw
    return w


def core_inputs(S, DEPTH, inp, shared, core):
    seq = core // 2
    m = dict(shared)
    m["x_lat"] = np.ascontiguousarray(inp["x_sample"][seq])
    m["x_ctx"] = np.ascontiguousarray(inp["x_prompt"][2 * core:2 * core + 2].reshape(2 * L, D))
    m["c_ckv"] = np.ascontiguousarray(inp["cache_mla_ckv"][seq])
    m["c_kr"] = np.ascontiguousarray(inp["cache_mla_krope"][seq])
    m["c_st"] = np.ascontiguousarray(inp["state_ret"][seq])
    m["c_nk"] = np.ascontiguousarray(inp["cache_nat_k"][seq].reshape(DEPTH, PAST, 512))
    m["c_nv"] = np.ascontiguousarray(inp["cache_nat_v"][seq].reshape(DEPTH, PAST, 512))
    m["c_sk"] = np.ascontiguousarray(inp["cache_swa_k"][seq].reshape(DEPTH, PAST, 128))
    m["c_sv"] = np.ascontiguousarray(inp["cache_swa_v"][seq].reshape(DEPTH, PAST, 128))
    cond = np.stack([inp["c"][seq], inp["c_ctx"]], axis=-1)
    m["condT"] = np.ascontiguousarray(cond.reshape(16, 128, 2).transpose(1, 0, 2))
    return m


def run(S, DEPTH, inp, n_cores=8, trace=False):
    inp = {k: np.asarray(v, dtype=np.float32) for k, v in inp.items()}
    nc = build(S, DEPTH)
    shared = host_weights(S, DEPTH, inp)
    in_maps = [core_inputs(S, DEPTH, inp, shared, c) for c in range(n_cores)]
    res = run_bass_kernel_spmd(nc, in_maps, core_ids=list(range(n_cores)), trace=trace)
    R = res.results
    nb = 2 * n_cores
    yp = np.concatenate([R[c]["y_ctx"].reshape(2, L, D) for c in range(n_cores)], axis=0)
    ys = np.stack([R[2 * i]["y_lat"] for i in range(n_cores // 2)], axis=0)
    ckv = np.concatenate([R[c]["o_ckv"] for c in range(n_cores)], axis=0)
    kr = np.concatenate([R[c]["o_kr"] for c in range(n_cores)], axis=0)
    st = np.concatenate([R[c]["o_st"] for c in range(n_cores)], axis=0)
    nk = np.concatenate([R[c]["o_nk"] for c in range(n_cores)], axis=0).reshape(nb, DEPTH, L, 4, 128)
    nv = np.concatenate([R[c]["o_nv"] for c in range(n_cores)], axis=0).reshape(nb, DEPTH, L, 4, 128)
    sk = np.concatenate([R[c]["o_sk"] for c in range(n_cores)], axis=0).reshape(nb, DEPTH, L, 2, 64)
    sv = np.concatenate([R[c]["o_sv"] for c in range(n_cores)], axis=0).reshape(nb, DEPTH, L, 2, 64)
    outs = (yp, ys, ckv, kr, st, nk, nv, sk, sv)
    return tuple(np.ascontiguousarray(o, dtype=np.float32) for o in outs), res


def kernel(**inputs):
    outs, _ = run(4096, 4, inputs)
    return outs
```
